# Optimizing a Trainium2 kernel written in Bass

```python
import math
import jax, jax.numpy as jnp
from jax import lax
import numpy as np

D_MODEL = 1024
BATCH = 2
SEQ = 8192
DEPTH = 2

MIX_WIDTH = D_MODEL
DA_HEADS = 4
DA_QK_DIM = 64
DA_V_DIM = 2 * DA_QK_DIM
DA_WIDTH = DA_HEADS * DA_V_DIM
RET_HEADS = 4
RET_QK_DIM = 64
RET_V_DIM = 2 * RET_QK_DIM
RET_WIDTH = RET_HEADS * RET_V_DIM
Q_BLOCK = 128
RET_CHUNK = 128
EPS = 1e-6

DA_Q_COLS = DA_HEADS * 2 * DA_QK_DIM
DA_K_COLS = DA_HEADS * 2 * DA_QK_DIM
DA_V_COLS = DA_WIDTH
DA_G_COLS = DA_WIDTH
RET_Q_COLS = RET_HEADS * RET_QK_DIM
RET_K_COLS = RET_HEADS * RET_QK_DIM
RET_V_COLS = RET_WIDTH
RET_G_COLS = RET_WIDTH
IN_WIDTH = DA_Q_COLS + DA_K_COLS + DA_V_COLS + DA_G_COLS + RET_Q_COLS + RET_K_COLS + RET_V_COLS + RET_G_COLS

kernel_name = "hymba_diffattn_retention_encoder"


def _rms(x, w=None):
    xf = x.astype(jnp.float32)
    y = xf * lax.rsqrt(jnp.mean(xf * xf, axis=-1, keepdims=True) + EPS)
    if w is not None:
        y = y * w.astype(jnp.float32)
    return y.astype(x.dtype)


def _alibi_slopes(n_heads):
    return jnp.asarray([2.0 ** (-8.0 * (i + 1) / n_heads) for i in range(n_heads)], dtype=jnp.float32)


def _diff_attention(q, k, v, lam, slopes):
    b, s = q.shape[0], q.shape[1]
    q = jnp.transpose(q, (0, 2, 3, 1, 4))
    k = jnp.transpose(k, (0, 2, 3, 1, 4))
    v = jnp.transpose(v, (0, 2, 1, 3))
    scale = DA_QK_DIM ** -0.5
    key_pos = jnp.arange(s)

    def block(i):
        start = i * Q_BLOCK
        qb = lax.dynamic_slice_in_dim(q, start, Q_BLOCK, axis=3)
        sc = jnp.einsum('bhmqd,bhmkd->bhmqk', qb, k).astype(jnp.float32) * scale
        qpos = start + jnp.arange(Q_BLOCK)
        dist = jnp.abs(qpos[:, None] - key_pos[None, :]).astype(jnp.float32)
        sc = sc - (slopes[:, None, None] * dist)[None, :, None]
        p = jax.nn.softmax(sc, axis=-1)
        w = p[:, :, 0] - lam * p[:, :, 1]
        return jnp.einsum('bhqk,bhkd->bhqd', w.astype(v.dtype), v)

    out = lax.map(block, jnp.arange(s // Q_BLOCK))
    return jnp.transpose(out, (1, 0, 3, 2, 4)).reshape(b, s, DA_HEADS, DA_V_DIM)


def _retention_dir(q, k, v, log_g, inclusive):
    b, h, s, dk = q.shape
    dv = v.shape[-1]
    nc = s // RET_CHUNK

    def chunks(t):
        return jnp.moveaxis(t.reshape(b, h, nc, RET_CHUNK, t.shape[-1]), 2, 0)

    n = jnp.arange(RET_CHUNK)
    diff = n[:, None] - n[None, :]
    mask = diff >= 0 if inclusive else diff > 0
    dmat = jnp.where(mask[None], jnp.exp(log_g[:, None, None] * jnp.maximum(diff, 0).astype(jnp.float32)), 0.0)
    xi = jnp.exp(log_g[:, None] * (n + 1).astype(jnp.float32))
    zeta = jnp.exp(log_g[:, None] * (RET_CHUNK - 1 - n).astype(jnp.float32))
    chunk_decay = jnp.exp(log_g * RET_CHUNK)

    def step(state, qkv):
        qc, kc, vc = qkv
        inner = jnp.einsum('bhqk,bhkv->bhqv', jnp.einsum('bhqd,bhkd->bhqk', qc, kc) * dmat, vc)
        cross = jnp.einsum('bhqd,bhdv->bhqv', qc * xi[..., None], state)
        new_state = chunk_decay[:, None, None] * state + jnp.einsum('bhkd,bhkv->bhdv', kc * zeta[..., None], vc)
        return new_state, inner + cross

    state0 = jnp.zeros((b, h, dk, dv), jnp.float32)
    _, out = lax.scan(step, state0, (chunks(q), chunks(k), chunks(v)))
    return jnp.moveaxis(out, 0, 2).reshape(b, h, s, dv)


def _bidirectional_retention(q, k, v, decay_fwd, decay_bwd):
    qf = jnp.transpose(q, (0, 2, 1, 3)).astype(jnp.float32)
    kf = jnp.transpose(k, (0, 2, 1, 3)).astype(jnp.float32) * (RET_QK_DIM ** -0.5)
    vf = jnp.transpose(v, (0, 2, 1, 3)).astype(jnp.float32)
    lg_f = jax.nn.log_sigmoid(decay_fwd.astype(jnp.float32))
    lg_b = jax.nn.log_sigmoid(decay_bwd.astype(jnp.float32))
    fwd = _retention_dir(qf, kf, vf, lg_f, True)
    flip = lambda t: jnp.flip(t, axis=2)
    bwd = flip(_retention_dir(flip(qf), flip(kf), flip(vf), lg_b, False))
    out = _rms(fwd + bwd)
    return jnp.transpose(out, (0, 2, 1, 3)).astype(q.dtype)


def setup_inputs(seed: int = 0) -> dict:
    key = jax.random.key(seed)
    ks = jax.random.split(key, 14)
    x = jax.random.normal(ks[0], (BATCH, SEQ, D_MODEL), jnp.float32)
    norm_w = 1.0 + 0.02 * jax.random.normal(ks[1], (DEPTH, D_MODEL), jnp.float32)
    w_in = jax.random.normal(ks[2], (DEPTH, D_MODEL, IN_WIDTH), jnp.float32) * D_MODEL ** -0.5
    q_norm_w = 1.0 + 0.02 * jax.random.normal(ks[3], (DEPTH, DA_QK_DIM), jnp.float32)
    k_norm_w = 1.0 + 0.02 * jax.random.normal(ks[4], (DEPTH, DA_QK_DIM), jnp.float32)
    lambda_q1 = 0.1 * jax.random.normal(ks[5], (DEPTH, DA_QK_DIM), jnp.float32)
    lambda_k1 = 0.1 * jax.random.normal(ks[6], (DEPTH, DA_QK_DIM), jnp.float32)
    lambda_q2 = 0.1 * jax.random.normal(ks[7], (DEPTH, DA_QK_DIM), jnp.float32)
    lambda_k2 = 0.1 * jax.random.normal(ks[8], (DEPTH, DA_QK_DIM), jnp.float32)
    subln_w = 1.0 + 0.02 * jax.random.normal(ks[9], (DEPTH, DA_V_DIM), jnp.float32)
    gamma = 1.0 - 2.0 ** (-5.0 - jnp.arange(RET_HEADS, dtype=jnp.float32))
    base_logit = jnp.log(gamma) - jnp.log1p(-gamma)
    ret_decay_fwd = base_logit[None] + 0.1 * jax.random.normal(ks[10], (DEPTH, RET_HEADS), jnp.float32)
    ret_decay_bwd = base_logit[None] + 0.1 * jax.random.normal(ks[11], (DEPTH, RET_HEADS), jnp.float32)
    w_out = jax.random.normal(ks[12], (DEPTH, MIX_WIDTH, D_MODEL), jnp.float32) * (MIX_WIDTH ** -0.5) / math.sqrt(2.0 * DEPTH)
    return {"x": x, "norm_w": norm_w, "w_in": w_in, "q_norm_w": q_norm_w, "k_norm_w": k_norm_w,
            "lambda_q1": lambda_q1, "lambda_k1": lambda_k1, "lambda_q2": lambda_q2, "lambda_k2": lambda_k2,
            "subln_w": subln_w, "ret_decay_fwd": ret_decay_fwd, "ret_decay_bwd": ret_decay_bwd, "w_out": w_out}


def reference(x, norm_w, w_in, q_norm_w, k_norm_w, lambda_q1, lambda_k1, lambda_q2, lambda_k2,
              subln_w, ret_decay_fwd, ret_decay_bwd, w_out):
    b, s, _ = x.shape
    slopes = _alibi_slopes(DA_HEADS)
    sizes = [DA_Q_COLS, DA_K_COLS, DA_V_COLS, DA_G_COLS, RET_Q_COLS, RET_K_COLS, RET_V_COLS, RET_G_COLS]
    offsets = [int(o) for o in np.cumsum(sizes)[:-1]]
    for l in range(DEPTH):
        xn = _rms(x, norm_w[l])
        h = jnp.einsum('bsd,de->bse', xn, w_in[l])
        qa, ka, va, ga, qr, kr, vr, gr = jnp.split(h, offsets, axis=-1)

        lambda_init = 0.8 - 0.6 * math.exp(-0.3 * l)
        lam = (jnp.exp(jnp.sum(lambda_q1[l].astype(jnp.float32) * lambda_k1[l].astype(jnp.float32)))
               - jnp.exp(jnp.sum(lambda_q2[l].astype(jnp.float32) * lambda_k2[l].astype(jnp.float32)))
               + lambda_init)
        qa = _rms(qa.reshape(b, s, DA_HEADS, 2, DA_QK_DIM), q_norm_w[l])
        ka = _rms(ka.reshape(b, s, DA_HEADS, 2, DA_QK_DIM), k_norm_w[l])
        va = va.reshape(b, s, DA_HEADS, DA_V_DIM)
        oa = _diff_attention(qa, ka, va, lam, slopes)
        oa = (_rms(oa, subln_w[l]) * (1.0 - lambda_init)).reshape(b, s, DA_WIDTH)

        qr = qr.reshape(b, s, RET_HEADS, RET_QK_DIM)
        kr = kr.reshape(b, s, RET_HEADS, RET_QK_DIM)
        vr = vr.reshape(b, s, RET_HEADS, RET_V_DIM)
        orr = _bidirectional_retention(qr, kr, vr, ret_decay_fwd[l], ret_decay_bwd[l]).reshape(b, s, RET_WIDTH)

        mixed = jnp.concatenate([oa * jax.nn.silu(ga), orr * jax.nn.silu(gr)], axis=-1)
        x = x + jnp.einsum('bse,ed->bsd', mixed, w_out[l])
    return x
```

```python
import contextlib
import math
import numpy as np
import ml_dtypes
import concourse.bass as bass
import concourse.mybir as mybir
from concourse.bass_utils import run_bass_kernel_spmd

F32 = mybir.dt.float32
BF16 = mybir.dt.bfloat16
AF = mybir.ActivationFunctionType
ALU = mybir.AluOpType
AX = mybir.AxisListType

SAME_ENGINE_SYNC = True
ENGS = ("sync", "act", "dve", "pool", "pe")
D_MODEL = 1024
EPS = 1e-6


class _Op:
    __slots__ = ("eng", "fn", "deps", "slot", "val", "waits", "idx")

    def __init__(self, eng, fn, deps, slot, idx):
        self.eng = eng
        self.fn = fn
        self.deps = deps
        self.slot = slot
        self.val = None
        self.waits = None
        self.idx = idx


class Prog:
    def __init__(self):
        self.ops = []
        self.last_w = {}
        self.readers = {}
        self.last_on_eng = {}
        self.last_on_slot = {}
        self.pending = {e: set() for e in ENGS}

    def add(self, eng, fn, reads=(), writes=(), slot=None, banks=()):
        i = len(self.ops)
        deps = set()
        if banks:
            writes = list(writes) + [("bank", b) for b in banks]
        for k in reads:
            w = self.last_w.get(k)
            if w is not None:
                deps.add(w)
        for k in writes:
            w = self.last_w.get(k)
            if w is not None:
                deps.add(w)
            for r in self.readers.get(k, ()):
                deps.add(r)
        for k in reads:
            self.readers.setdefault(k, []).append(i)
        for k in writes:
            self.last_w[k] = i
            self.readers[k] = []
        if self.pending[eng]:
            deps |= self.pending[eng]
            self.pending[eng] = set()
        if slot is not None:
            p = self.last_on_slot.get(slot)
            if p is not None:
                deps.add(p)
            self.last_on_slot[slot] = i
        else:
            self.last_on_eng[eng] = i
        deps.discard(i)
        self.ops.append(_Op(eng, fn, deps, slot, i))
        return i

    def barrier(self):
        allp = set(self.last_on_eng.values()) | set(self.last_on_slot.values())
        for e in ENGS:
            self.pending[e] |= allp

    def _semkey(self, op):
        return ("slot", op.slot) if op.slot is not None else ("eng", op.eng)

    def emit(self, nc, final_wait_eng="sync"):
        ops = self.ops
        self.barrier()
        self.add(final_wait_eng, None)
        waited = {e: {} for e in ENGS}
        need = set()
        for op in ops:
            ws = {}
            for d in op.deps:
                y = ops[d]
                sk = self._semkey(y)
                if y.slot is None and y.eng == op.eng:
                    if op.eng in ("pe", "sync"):
                        continue
                    if not SAME_ENGINE_SYNC:
                        continue
                if waited[op.eng].get(sk, -1) >= d:
                    continue
                if sk not in ws or ws[sk] < d:
                    ws[sk] = d
            for sk, d in ws.items():
                waited[op.eng][sk] = d
                need.add(d)
            op.waits = ws
        cnt = {}
        for op in ops:
            sk = self._semkey(op)
            if op.slot is not None:
                cnt[sk] = cnt.get(sk, 0) + 16
                op.val = cnt[sk]
            elif op.idx in need:
                cnt[sk] = cnt.get(sk, 0) + 1
                op.val = cnt[sk]
        self.sem_counts = dict(cnt)
        sems = {}
        with contextlib.ExitStack() as stack:
            for n_, sk in enumerate(cnt):
                sems[sk] = stack.enter_context(nc.semaphore("sem%d" % n_))
            block = stack.enter_context(nc.Block())

            def run(eng_name):
                def body(eng):
                    for op in ops:
                        if op.eng != eng_name:
                            continue
                        for sk, d in op.waits.items():
                            eng.wait_ge(sems[sk], ops[d].val)
                        if op.fn is None:
                            continue
                        ins = op.fn(eng)
                        if op.val is not None:
                            ins.then_inc(sems[self._semkey(op)], 16 if op.slot is not None else 1)
                return body

            block.sync(run("sync"))
            block.scalar(run("act"))
            block.vector(run("dve"))
            block.gpsimd(run("pool"))
            block.tensor(run("pe"))


class Arena:
    def __init__(self, base_ap_bf16, nbytes):
        self.base = base_ap_bf16
        self.nbytes = nbytes
        self.top = 0
        self.marks = []
        self.peak = 0

    def alloc(self, shape_free, dtype, parts=128, align=32):
        esz = 4 if dtype == F32 else 2
        n = int(np.prod(shape_free))
        off = (self.top + align - 1) // align * align
        nb = n * esz
        assert off + nb <= self.nbytes, ("arena overflow", off, nb, self.nbytes)
        self.top = off + nb
        self.peak = max(self.peak, self.top)
        ap = self.base[0:parts, off // 2:(off + nb) // 2]
        if dtype == F32:
            ap = ap.bitcast(F32)
        if len(shape_free) == 2:
            ap = ap.rearrange("p (a b) -> p a b", a=shape_free[0])
        elif len(shape_free) == 3:
            ap = ap.rearrange("p (a b c) -> p a b c", a=shape_free[0], b=shape_free[1])
        return ap

    def push(self):
        self.marks.append(self.top)

    def pop(self):
        self.top = self.marks.pop()


SBUF_BYTES = 212736


class Ctx:
    def __init__(self, nc, stack):
        self.nc = nc
        self.P = Prog()
        sb = stack.enter_context(nc.sbuf_tensor("arena", [128, SBUF_BYTES // 2], BF16))
        self.A = Arena(sb[:, :], SBUF_BYTES)
        ps = stack.enter_context(nc.psum_tensor("psum_all", [128, 4096], F32))
        self.PS = ps[:, :]

    def bank(self, i, n=1):
        return self.PS[:, i * 512:(i + n) * 512]

    def bankb(self, i, n=1):
        return self.PS[:, i * 512:(i + n) * 512].bitcast(BF16)


CF_M1, CF_M2, CF_T, CF_TN, CF_C5, CF_N = 0, 128, 256, 384, 512, 520
SM_LGF, SM_LGB, SM_XIF, SM_XIB, SM_ZF, SM_ZB, SM_CD, SM_NEGLAM, SM_ZERO, SM_MH = 0, 1, 2, 3, 4, 5, 6, 7, 8, 9


def emit_B(cx, io, NT):
    P, A = cx.P, cx.A
    A.push()
    _emit_B(cx, io, NT)
    A.pop()
    P.barrier()


def _emit_B(cx, io, NT):
    P, A = cx.P, cx.A
    S = NT * 128
    NQC = NT // 4
    add = P.add

    QKT = A.alloc([4, S], BF16, parts=68)
    VV = A.alloc([NT, 2, 132], BF16)
    G = A.alloc([NT, 256], BF16)
    RQ = A.alloc([NT, 128], BF16)
    CF = A.alloc([CF_N], F32)
    CB = A.alloc([256], BF16)
    WQK = A.alloc([256], F32)
    CQK = A.alloc([128], F32)
    SUBW = A.alloc([128], F32)
    V6 = A.alloc([6, 64], F32)
    SC = A.alloc([4], F32)
    SM = A.alloc([16], F32)
    DC = A.alloc([128], F32)
    TMPA = A.alloc([128], F32)
    TMPB = A.alloc([128], F32)
    TS = A.alloc([8], F32)
    ident = CB[:, 0:128]
    Bm = CB[:, 128:256]

    add("sync", lambda e: e.dma_start(out=CF, in_=io["cpackf"]), writes=["CF"], slot="ld0")
    add("sync", lambda e: e.dma_start(out=CB, in_=io["cpackb"]), writes=["CB"], slot="ld1")
    add("sync", lambda e: e.dma_start(out=V6, in_=io["vec64"].partition_broadcast(128)), writes=["V6"], slot="ld2")
    add("sync", lambda e: e.dma_start(out=SUBW, in_=io["subln"].partition_broadcast(128)), writes=["SUBW"], slot="ld3")
    add("sync", lambda e: e.dma_start(out=SC, in_=io["scal"].partition_broadcast(128)), writes=["SC"], slot="ld4")
    add("sync", lambda e: e.dma_start(out=QKT[64:68, :, :], in_=io["aug"]), writes=["QKTaug"], slot="ld5")
    add("pool", lambda e: e.memset(VV[:, :, 0, 128:129], 1.0), writes=["VVone"])
    add("pool", lambda e: e.memset(CQK[:, 0:64], 1.0), writes=["CQK"])
    add("pool", lambda e: e.memset(CQK[:, 64:128], 0.125), writes=["CQK"])
    add("pool", lambda e: e.memset(SM[:, SM_ZERO:SM_ZERO + 1], 0.0), writes=["SMz"])
    add("pool", lambda e: e.memset(SM[:, SM_MH:SM_MH + 4], -0.5), writes=["SMmh"])
    MH = SM[:, SM_MH:SM_MH + 4]
    ZERO = SM[:, SM_ZERO:SM_ZERO + 1]
    add("dve", lambda e: e.tensor_scalar(out=WQK[:, 0:128].rearrange("p (a b) -> p a b", a=2),
                                         in0=V6[:, 0, :].unsqueeze(1).to_broadcast([128, 2, 64]),
                                         scalar1=0.125, scalar2=None, op0=ALU.mult), reads=["V6"], writes=["WQK"])
    add("dve", lambda e: e.tensor_scalar(out=WQK[:, 128:256].rearrange("p (a b) -> p a b", a=2),
                                         in0=V6[:, 1, :].unsqueeze(1).to_broadcast([128, 2, 64]),
                                         scalar1=1.0, scalar2=None, op0=ALU.mult), reads=["V6"], writes=["WQK"])
    add("dve", lambda e: e.tensor_tensor(out=TMPA[:, 0:64], in0=V6[:, 2, :], in1=V6[:, 3, :], op=ALU.mult), reads=["V6"], writes=["TMPA"])
    add("dve", lambda e: e.reduce_sum(out=TS[:, 0:1], in_=TMPA[:, 0:64], axis=AX.X), reads=["TMPA"], writes=["TS0"])
    add("dve", lambda e: e.tensor_tensor(out=TMPA[:, 64:128], in0=V6[:, 4, :], in1=V6[:, 5, :], op=ALU.mult), reads=["V6"], writes=["TMPA2"])
    add("dve", lambda e: e.reduce_sum(out=TS[:, 1:2], in_=TMPA[:, 64:128], axis=AX.X), reads=["TMPA2"], writes=["TS1"])
    add("act", lambda e: e.activation(out=TS[:, 2:4], in_=TS[:, 0:2], func=AF.Exp), reads=["TS0", "TS1"], writes=["TS23"])
    add("dve", lambda e: e.tensor_tensor(out=TS[:, 4:5], in0=TS[:, 2:3], in1=TS[:, 3:4], op=ALU.subtract), reads=["TS23"], writes=["TS4"])
    add("dve", lambda e: e.tensor_scalar(out=SM[:, SM_NEGLAM:SM_NEGLAM + 1], in0=TS[:, 4:5], scalar1=SC[:, 2:3], scalar2=-1.0,
                                         op0=ALU.add, op1=ALU.mult), reads=["TS4", "SC"], writes=["NEGLAM"])
    NEGLAM = SM[:, SM_NEGLAM:SM_NEGLAM + 1]
    add("dve", lambda e: e.tensor_scalar(out=SUBW, in0=SUBW, scalar1=SC[:, 3:4], scalar2=None, op0=ALU.mult), reads=["SUBW", "SC"], writes=["SUBW"])
    add("act", lambda e: e.activation(out=TS[:, 5:7], in_=SC[:, 0:2], func=AF.Exp, scale=-1.0), reads=["SC"], writes=["TS56"])
    add("act", lambda e: e.activation(out=TS[:, 5:7], in_=TS[:, 5:7], func=AF.Ln, bias=1.0, scale=1.0), reads=["TS56"], writes=["TS56"])
    add("dve", lambda e: e.tensor_scalar(out=SM[:, 0:2], in0=TS[:, 5:7], scalar1=-1.0, scalar2=None, op0=ALU.mult), reads=["TS56"], writes=["LG"])
    LGF, LGB = SM[:, 0:1], SM[:, 1:2]
    C5 = CF[:, CF_C5:CF_C5 + 5]
    add("act", lambda e: e.activation(out=SM[:, SM_XIF:SM_XIF + 1], in_=C5[:, 0:1], func=AF.Exp, scale=LGF), reads=["LG", "CF"], writes=["XIF"])
    add("act", lambda e: e.activation(out=SM[:, SM_XIB:SM_XIB + 1], in_=C5[:, 1:2], func=AF.Exp, scale=LGB), reads=["LG", "CF"], writes=["XIB"])
    add("act", lambda e: e.activation(out=SM[:, SM_ZF:SM_ZF + 1], in_=C5[:, 2:3], func=AF.Exp, scale=LGF), reads=["LG", "CF"], writes=["ZF"])
    add("act", lambda e: e.activation(out=SM[:, SM_ZB:SM_ZB + 1], in_=C5[:, 3:4], func=AF.Exp, scale=LGB), reads=["LG", "CF"], writes=["ZB"])
    add("act", lambda e: e.activation(out=SM[0:64, SM_CD:SM_CD + 1], in_=C5[0:64, 4:5], func=AF.Exp, scale=SM[0:64, 0:1]), reads=["LG", "CF"], writes=["CDf"])
    add("act", lambda e: e.activation(out=SM[64:128, SM_CD:SM_CD + 1], in_=C5[64:128, 4:5], func=AF.Exp, scale=SM[64:128, 1:2]), reads=["LG", "CF"], writes=["CDb"])
    XIF, XIB = SM[:, SM_XIF:SM_XIF + 1], SM[:, SM_XIB:SM_XIB + 1]
    ZF, ZB = SM[:, SM_ZF:SM_ZF + 1], SM[:, SM_ZB:SM_ZB + 1]
    add("dve", lambda e: e.tensor_scalar(out=TMPB, in0=CF[:, CF_M1:CF_M1 + 128], scalar1=LGF, scalar2=None, op0=ALU.mult), reads=["LG", "CF"], writes=["TMPB"])
    add("dve", lambda e: e.scalar_tensor_tensor(out=TMPB, in0=CF[:, CF_M2:CF_M2 + 128], scalar=LGB, in1=TMPB, op0=ALU.mult, op1=ALU.add), reads=["LG", "CF", "TMPB"], writes=["TMPB"])
    add("act", lambda e: e.activation(out=DC, in_=TMPB, func=AF.Exp), reads=["TMPB"], writes=["DC"])

    A.push()
    W = A.alloc([8, 896], BF16)
    WST = [A.alloc([896], F32) for _ in range(2)]
    NW = A.alloc([8], F32)
    XN = [A.alloc([1024], BF16) for _ in range(3)]
    XT = [A.alloc([8, 128], BF16) for _ in range(2)]
    QKF = [A.alloc([256], F32) for _ in range(4)]
    SQ = [A.alloc([256], F32) for _ in range(4)]
    QKN = [A.alloc([256], BF16) for _ in range(4)]
    S4 = [A.alloc([8], F32) for _ in range(4)]
    add("sync", lambda e: e.dma_start(out=NW, in_=io["normw"]), writes=["NW"], slot="ld6")
    wv = io["w"].rearrange("(k p) n -> k p n", p=128)
    for kc in range(8):
        b_ = kc % 2
        add("sync", lambda e, kc=kc, b_=b_: e.dma_start(out=WST[b_], in_=wv[kc]), writes=[("WST", b_)], slot=("wst", b_))
        add("dve", lambda e, kc=kc, b_=b_: e.tensor_scalar(out=W[:, kc, :], in0=WST[b_], scalar1=NW[:, kc:kc + 1], scalar2=None, op0=ALU.mult),
            reads=[("WST", b_), "NW"], writes=["W"])
    xv = io["xn"].rearrange("(t p) d -> t p d", p=128)

    def HAb(t):
        return cx.bank(2 + 2 * (t % 2))

    def HBb(t):
        return cx.bank(3 + 2 * (t % 2))

    def TQb(t):
        pb = t % 2
        return cx.bankb(6 + pb)[0:64, 0:512].rearrange("p (a b) -> p a b", a=4)

    def ip_ld(t):
        x3 = t % 3
        add("sync", lambda e: e.dma_start(out=XN[x3], in_=xv[t]), writes=[("XN", x3)], slot=("xn", x3))

    def ip_xpose(t):
        x3, pb = t % 3, t % 2
        TPb = cx.bankb(pb)
        for kc in range(8):
            add("pe", lambda e, kc=kc: e.transpose(out=TPb[:, kc * 128:(kc + 1) * 128], in_=XN[x3][:, kc * 128:(kc + 1) * 128], identity=ident),
                reads=[("XN", x3), "CB"], writes=[("TP", pb)], banks=[pb])
        add("act", lambda e: e.activation(out=XT[pb].rearrange("p a b -> p (a b)"), in_=TPb, func=AF.Copy), reads=[("TP", pb)], writes=[("XT", pb)], banks=[pb])

    def ip_mm(t):
        pb = t % 2
        HA, HB = HAb(t), HBb(t)
        for kc in range(8):
            add("pe", lambda e, kc=kc: e.matmul(out=HA[:, 0:384], lhsT=XT[pb][:, kc, :], rhs=W[:, kc, 0:384], start=(kc == 0), stop=(kc == 7)),
                reads=[("XT", pb), "W"], writes=[("HA", pb)], banks=[2 + 2 * pb])
        for kc in range(8):
            add("pe", lambda e, kc=kc: e.matmul(out=HB, lhsT=XT[pb][:, kc, :], rhs=W[:, kc, 384:896], start=(kc == 0), stop=(kc == 7)),
                reads=[("XT", pb), "W"], writes=[("HB", pb)], banks=[3 + 2 * pb])

    def ip_evac(t):
        pb, q4 = t % 2, t % 4
        HA, HB = HAb(t), HBb(t)
        add("dve", lambda e: e.tensor_copy(out=QKF[q4], in_=HA[:, 0:256]), reads=[("HA", pb)], writes=[("QKF", q4)], banks=[2 + 2 * pb])
        add("dve", lambda e: e.tensor_tensor(out=RQ[:, t, :], in0=HA[:, 256:384], in1=CQK, op=ALU.mult), reads=[("HA", pb), "CQK"], writes=[("RQ", t)], banks=[2 + 2 * pb])
        add("dve", lambda e: e.tensor_copy(out=VV[:, t, :, 0:128], in_=HB[:, 0:256].rearrange("p (a b) -> p a b", a=2)), reads=[("HB", pb)], writes=[("VV", t)], banks=[3 + 2 * pb])
        add("act", lambda e: e.activation(out=G[:, t, :], in_=HB[:, 256:512], func=AF.Silu), reads=[("HB", pb)], writes=[("G", t)], banks=[3 + 2 * pb])

    def ip_normA(t):
        q4 = t % 4
        add("act", lambda e: e.activation(out=SQ[q4], in_=QKF[q4], func=AF.Square), reads=[("QKF", q4)], writes=[("SQ", q4)])
        add("dve", lambda e: e.reduce_sum(out=S4[q4][:, 0:4], in_=SQ[q4].rearrange("p (a b) -> p a b", a=4), axis=AX.X), reads=[("SQ", q4)], writes=[("S4a", q4)])
        add("dve", lambda e: e.tensor_scalar(out=S4[q4][:, 0:4], in0=S4[q4][:, 0:4], scalar1=1.0 / 64, scalar2=EPS, op0=ALU.mult, op1=ALU.add),
            reads=[("S4a", q4)], writes=[("S4a", q4)])
        add("pool", lambda e: e.tensor_tensor(out=S4[q4][:, 4:8], in0=S4[q4][:, 0:4], in1=MH, op=ALU.pow), reads=[("S4a", q4), "SMmh"], writes=[("S4b", q4)])

    def ip_normB(t):
        q4 = t % 4
        for g in range(4):
            add("dve", lambda e, g=g: e.scalar_tensor_tensor(out=QKN[q4][:, g * 64:(g + 1) * 64], in0=QKF[q4][:, g * 64:(g + 1) * 64], scalar=S4[q4][:, 4 + g:5 + g],
                                                            in1=WQK[:, g * 64:(g + 1) * 64], op0=ALU.mult, op1=ALU.mult),
                reads=[("QKF", q4), ("S4b", q4), "WQK"], writes=[("QKN", q4)])

    def ip_qkT(t):
        q4, pb = t % 4, t % 2
        TQ = TQb(t)
        for g in range(4):
            add("pe", lambda e, g=g: e.transpose(out=TQ[:, g, :], in_=QKN[q4][:, g * 64:(g + 1) * 64], identity=ident), reads=[("QKN", q4), "CB"], writes=[("TQ", pb)], banks=[6 + pb])

    def ip_qkC(t):
        pb = t % 2
        TQ = TQb(t)
        add("dve", lambda e: e.tensor_copy(out=QKT[0:64, :, t * 128:(t + 1) * 128], in_=TQ), reads=[("TQ", pb)], writes=[("QKT", t)], banks=[6 + pb])

    ip_ld(0)
    if NT > 1:
        ip_ld(1)
    ip_xpose(0)
    for t in range(NT + 3):
        if t + 2 < NT:
            ip_ld(t + 2)
        if t + 1 < NT:
            ip_xpose(t + 1)
        if t < NT:
            ip_mm(t)
        if 0 <= t - 3 < NT:
            ip_qkC(t - 3)
        if t < NT:
            ip_evac(t)
        if 0 <= t - 1 < NT:
            ip_normB(t - 1)
        if 0 <= t - 2 < NT:
            ip_qkT(t - 2)
        if t < NT:
            ip_normA(t)
    A.pop()
    P.barrier()

    A.push()
    B = A.alloc([NT, 128], F32)
    NBUF = 3
    KZ = [A.alloc([4, 128], BF16) for _ in range(NBUF)]
    RT = [A.alloc([4, 2, 128], BF16, parts=64) for _ in range(2)]
    QX = [A.alloc([4, 128], BF16) for _ in range(NBUF)]
    QXT = [A.alloc([4, 128], BF16) for _ in range(NBUF)]
    WT = [A.alloc([4, 128], BF16) for _ in range(NBUF)]
    SSg = [A.alloc([4, 128], BF16) for _ in range(NBUF)]
    SQ2 = [A.alloc([512], F32) for _ in range(1)] * 2
    R4 = [A.alloc([8], F32) for _ in range(NBUF)]
    NG = NT // 4

    def rq(g):
        return [("RQ", 4 * g + i) for i in range(4)]

    def p1_kz(g):
        b3 = g % NBUF
        add("dve", lambda e: e.tensor_scalar(out=KZ[b3][:, :, 0:64], in0=RQ[:, 4 * g:4 * g + 4, 64:128], scalar1=ZF, scalar2=None, op0=ALU.mult),
            reads=rq(g) + ["ZF"], writes=[("KZ", b3)])
        add("dve", lambda e: e.tensor_scalar(out=KZ[b3][:, :, 64:128], in0=RQ[:, 4 * g:4 * g + 4, 64:128], scalar1=ZB, scalar2=None, op0=ALU.mult),
            reads=rq(g) + ["ZB"], writes=[("KZ", b3)])

    def p1_mm(g):
        b3, pb = g % NBUF, g % 2
        UU = cx.bank(pb).rearrange("p (a b) -> p a b", a=4)
        for i in range(4):
            c = 4 * g + i
            add("pe", lambda e, c=c, i=i: e.matmul(out=UU[:, i, :], lhsT=KZ[b3][:, i, :], rhs=VV[:, c, 1, 0:128], start=True, stop=True),
                reads=[("KZ", b3), ("VV", c)], writes=[("UU", pb)], banks=[pb])

    def p1_ev(g):
        pb = g % 2
        UU = cx.bank(pb).rearrange("p (a b) -> p a b", a=4)
        add("act", lambda e: e.activation(out=B[:, 4 * g:4 * g + 4, :].rearrange("p a b -> p (a b)"), in_=UU.rearrange("p a b -> p (a b)"), func=AF.Copy),
            reads=[("UU", pb)], writes=[("B", c) for c in range(4 * g, 4 * g + 4)], banks=[pb])

    p1_kz(0)
    for g in range(NG):
        if g + 1 < NG:
            p1_kz(g + 1)
        p1_mm(g)
        p1_ev(g)
    CDf = SM[0:64, SM_CD:SM_CD + 1]
    CDb = SM[64:128, SM_CD:SM_CD + 1]
    for c in range(1, NT):
        add("dve", lambda e, c=c: e.scalar_tensor_tensor(out=B[0:64, c, :], in0=B[0:64, c - 1, :], scalar=CDf, in1=B[0:64, c, :], op0=ALU.mult, op1=ALU.add),
            reads=[("B", c - 1), ("B", c), "CDf"], writes=[("B", c)])
        cb = NT - 1 - c
        add("pool", lambda e, cb=cb: e.scalar_tensor_tensor(out=B[64:128, cb, :], in0=B[64:128, cb + 1, :], scalar=CDb, in1=B[64:128, cb, :], op0=ALU.mult, op1=ALU.add)
            if False else e.tensor_scalar(out=TMPB[64:128, :], in0=B[64:128, cb + 1, :], scalar1=CDb, scalar2=None, op0=ALU.mult),
            reads=[("Bb", cb + 1), ("Bev", cb + 1), "CDb"], writes=["TMPBb"]) if False else None
        add("dve", lambda e, cb=cb: e.scalar_tensor_tensor(out=B[64:128, cb, :], in0=B[64:128, cb + 1, :], scalar=CDb, in1=B[64:128, cb, :], op0=ALU.mult, op1=ALU.add),
            reads=[("Bb", cb + 1), ("Bb", cb), ("B", cb), ("B", cb + 1), "CDb"], writes=[("Bb", cb)])
    allB = [("B", c) for c in range(NT)] + [("Bb", c) for c in range(NT)]

    def s1(g):
        b3, pb = g % NBUF, g % 2
        TR = cx.bankb(2 + pb)[0:64, :].rearrange("p (a b c) -> p a b c", a=4, b=2)
        TX = cx.bankb(6 + pb)[:, 0:512].rearrange("p (a b) -> p a b", a=4)
        add("dve", lambda e: e.tensor_scalar(out=QX[b3][:, :, 0:64], in0=RQ[:, 4 * g:4 * g + 4, 0:64], scalar1=XIF, scalar2=None, op0=ALU.mult),
            reads=rq(g) + ["XIF"], writes=[("QX", b3)])
        add("dve", lambda e: e.tensor_scalar(out=QX[b3][:, :, 64:128], in0=RQ[:, 4 * g:4 * g + 4, 0:64], scalar1=XIB, scalar2=None, op0=ALU.mult),
            reads=rq(g) + ["XIB"], writes=[("QX", b3)])
        for i in range(4):
            c = 4 * g + i
            for j in range(2):
                add("pe", lambda e, c=c, i=i, j=j: e.transpose(out=TR[:, i, j, :], in_=RQ[:, c, j * 64:(j + 1) * 64], identity=ident),
                    reads=[("RQ", c), "CB"], writes=[("TR", pb)], banks=[2 + pb])
        for i in range(4):
            add("pe", lambda e, i=i: e.transpose(out=TX[:, i, :], in_=QX[b3][:, i, :], identity=ident), reads=[("QX", b3), "CB"], writes=[("TX", pb)], banks=[6 + pb])

    def s2(g):
        b3, pb = g % NBUF, g % 2
        TR = cx.bankb(2 + pb)[0:64, :].rearrange("p (a b c) -> p a b c", a=4, b=2)
        TX = cx.bankb(6 + pb)[:, 0:512].rearrange("p (a b) -> p a b", a=4)
        AT = cx.bank(4 + pb).rearrange("p (a b) -> p a b", a=4)
        add("act", lambda e: e.activation(out=RT[pb].rearrange("p a b c -> p (a b c)"), in_=TR.rearrange("p a b c -> p (a b c)"), func=AF.Copy),
            reads=[("TR", pb)], writes=[("RT", pb)], banks=[2 + pb])
        add("act", lambda e: e.activation(out=QXT[b3].rearrange("p a b -> p (a b)"), in_=TX.rearrange("p a b -> p (a b)"), func=AF.Copy),
            reads=[("TX", pb)], writes=[("QXT", b3)], banks=[6 + pb])
        c0 = 4 * g
        if g == 0:
            add("pool", lambda e: e.memset(SSg[b3][0:64, 0, :], 0.0), writes=[("SSg", b3)])
            add("dve", lambda e: e.tensor_copy(out=SSg[b3][0:64, 1:4, :], in_=B[0:64, 0:3, :]), reads=allB, writes=[("SSg", b3)])
        else:
            add("dve", lambda e: e.tensor_copy(out=SSg[b3][0:64, :, :], in_=B[0:64, c0 - 1:c0 + 3, :]), reads=allB, writes=[("SSg", b3)])
        if g == NG - 1:
            add("pool", lambda e: e.memset(SSg[b3][64:128, 3, :], 0.0), writes=[("SSg", b3)])
            add("dve", lambda e: e.tensor_copy(out=SSg[b3][64:128, 0:3, :], in_=B[64:128, c0 + 1:c0 + 4, :]), reads=allB, writes=[("SSg", b3)])
        else:
            add("dve", lambda e: e.tensor_copy(out=SSg[b3][64:128, :, :], in_=B[64:128, c0 + 1:c0 + 5, :]), reads=allB, writes=[("SSg", b3)])
        for i in range(4):
            add("pe", lambda e, i=i: e.matmul(out=AT[:, i, :], lhsT=RT[pb][:, i, 1, :], rhs=RT[pb][:, i, 0, :], start=True, stop=True),
                reads=[("RT", pb)], writes=[("AT", pb)], banks=[4 + pb])
        add("dve", lambda e: e.tensor_tensor(out=WT[b3], in0=AT, in1=DC.unsqueeze(1).to_broadcast([128, 4, 128]), op=ALU.mult),
            reads=[("AT", pb), "DC"], writes=[("WT", b3)], banks=[4 + pb])

    def s3(g):
        b3, pb = g % NBUF, g % 2
        OUTb = cx.bank(pb).rearrange("p (a b) -> p a b", a=4)
        for i in range(4):
            c = 4 * g + i
            add("pe", lambda e, i=i, c=c: e.matmul(out=OUTb[:, i, :], lhsT=WT[b3][:, i, :], rhs=VV[:, c, 1, 0:128], start=True, stop=False),
                reads=[("WT", b3), ("VV", c)], writes=[("OUT", pb)], banks=[pb])
            add("pe", lambda e, i=i, c=c: e.matmul(out=OUTb[:, i, :], lhsT=QXT[b3][:, i, :], rhs=SSg[b3][:, i, :], start=False, stop=True),
                reads=[("QXT", b3), ("SSg", b3)], writes=[("OUT", pb)], banks=[pb])
        add("act", lambda e: e.activation(out=SQ2[pb], in_=OUTb.rearrange("p a b -> p (a b)"), func=AF.Square), reads=[("OUT", pb)], writes=[("SQ2", 0)], banks=[pb])
        add("dve", lambda e: e.reduce_sum(out=R4[b3][:, 0:4], in_=SQ2[pb].rearrange("p (a b) -> p a b", a=4), axis=AX.X), reads=[("SQ2", 0)], writes=[("R4a", b3)])
        add("dve", lambda e: e.tensor_scalar(out=R4[b3][:, 0:4], in0=R4[b3][:, 0:4], scalar1=1.0 / 128, scalar2=EPS, op0=ALU.mult, op1=ALU.add),
            reads=[("R4a", b3)], writes=[("R4a", b3)])
        add("pool", lambda e: e.tensor_tensor(out=R4[b3][:, 4:8], in0=R4[b3][:, 0:4], in1=MH, op=ALU.pow), reads=[("R4a", b3), "SMmh"], writes=[("R4b", b3)])

    def s4(g):
        b3, pb = g % NBUF, g % 2
        OUTb = cx.bank(pb).rearrange("p (a b) -> p a b", a=4)
        for i in range(4):
            c = 4 * g + i
            add("dve", lambda e, i=i, c=c: e.scalar_tensor_tensor(out=G[:, c, 128:256], in0=OUTb[:, i, :], scalar=R4[b3][:, 4 + i:5 + i],
                                                                 in1=G[:, c, 128:256], op0=ALU.mult, op1=ALU.mult),
                reads=[("OUT", pb), ("R4b", b3), ("G", c)], writes=[("G", c)], banks=[pb])

    for s in range(NG + 3):
        if s < NG:
            s1(s)
        if 0 <= s - 1 < NG:
            s2(s - 1)
        if 0 <= s - 3 < NG:
            s4(s - 3)
        if 0 <= s - 2 < NG:
            s3(s - 2)
    A.pop()
    P.barrier()

    A.push()
    PT = [A.alloc([2, 512], BF16) for _ in range(2)]
    OS = A.alloc([3, 512], F32)
    E1 = [A.alloc([128], F32) for _ in range(2)]
    E2 = [A.alloc([128], F32) for _ in range(2)]
    ES = [A.alloc([8], F32) for _ in range(2)]
    Tt = CF[:, CF_T:CF_T + 128]
    TN = CF[:, CF_TN:CF_TN + 128]

    def acc_ap(a, base):
        bnk = 4 + a // 3
        o = (a % 3) * 129
        return base[:, (bnk - 4) * 512 + o:(bnk - 4) * 512 + o + 129]

    OB = cx.bank(4, 3)
    it = 0
    steps = [(qc, kb) for qc in range(NQC) for kb in range(NT)]

    def emit_qk(qc, kb, sp):
        Sv = cx.bank(2 * sp, 2).rearrange("p (a b) -> p a b", a=2)
        rel = kb - 4 * qc
        qk_reads = ["QKTaug"] + [("QKT", kb)] + [("QKT", 4 * qc + i) for i in range(4)]
        if rel < 0 or rel >= 4:
            Kw = 66 if rel < 0 else 68
            for m in range(2):
                add("pe", lambda e, m=m, Kw=Kw, Sv=Sv: e.matmul(out=Sv[:, m, :], lhsT=QKT[0:Kw, 2 + m, kb * 128:(kb + 1) * 128],
                                                               rhs=QKT[0:Kw, m, qc * 512:(qc + 1) * 512], start=True, stop=True),
                    reads=qk_reads, writes=[("S", sp)], banks=[2 * sp, 2 * sp + 1])
            bias = (Tt if rel < 0 else TN)[:, rel + 64:rel + 65]
            add("act", lambda e, Sv=Sv, bias=bias, sp=sp: e.activation(out=PT[sp].rearrange("p a b -> p (a b)"), in_=Sv.rearrange("p a b -> p (a b)"),
                                                                      func=AF.Exp, bias=bias, scale=1.0),
                reads=[("S", sp), "CF"], writes=[("PT", sp)], banks=[2 * sp, 2 * sp + 1])
        else:
            for m in range(2):
                for qs in range(4):
                    d = rel - qs
                    q0 = qc * 512 + qs * 128
                    if d == 0:
                        add("pe", lambda e, m=m, qs=qs, q0=q0, Sv=Sv: e.matmul(out=Sv[:, m, qs * 128:(qs + 1) * 128], lhsT=QKT[0:64, 2 + m, kb * 128:(kb + 1) * 128],
                                                                              rhs=QKT[0:64, m, q0:q0 + 128], start=True, stop=False),
                            reads=qk_reads, writes=[("S", sp)], banks=[2 * sp, 2 * sp + 1])
                        add("pe", lambda e, m=m, qs=qs, Sv=Sv: e.matmul(out=Sv[:, m, qs * 128:(qs + 1) * 128], lhsT=ident, rhs=Bm, start=False, stop=True),
                            reads=["CB"], writes=[("S", sp)], banks=[2 * sp, 2 * sp + 1])
                    else:
                        Kw = 68 if d > 0 else 66
                        add("pe", lambda e, m=m, qs=qs, q0=q0, Kw=Kw, Sv=Sv: e.matmul(out=Sv[:, m, qs * 128:(qs + 1) * 128], lhsT=QKT[0:Kw, 2 + m, kb * 128:(kb + 1) * 128],
                                                                                     rhs=QKT[0:Kw, m, q0:q0 + 128], start=True, stop=True),
                            reads=qk_reads, writes=[("S", sp)], banks=[2 * sp, 2 * sp + 1])
            for qs in range(4):
                d = rel - qs
                bias = ZERO if d == 0 else (TN if d > 0 else Tt)[:, rel + 64:rel + 65]
                add("act", lambda e, Sv=Sv, bias=bias, sp=sp, qs=qs: e.activation(out=PT[sp][:, :, qs * 128:(qs + 1) * 128], in_=Sv[:, :, qs * 128:(qs + 1) * 128],
                                                                                 func=AF.Exp, bias=bias, scale=1.0),
                    reads=[("S", sp), "CF", "SMz"], writes=[("PT", sp)], banks=[2 * sp, 2 * sp + 1])

    def emit_pv(qc, kb, sp):
        for a in range(8):
            qs, m = a // 2, a % 2
            add("pe", lambda e, a=a, qs=qs, m=m: e.matmul(out=acc_ap(a, OB), lhsT=PT[sp][:, m, qs * 128:(qs + 1) * 128], rhs=VV[:, kb, 0, 0:129],
                                                         start=(kb == 0 and a % 3 == 0), stop=(kb == NT - 1), skip_group_check=True),
                reads=[("PT", sp), ("VV", kb), "VVone"], writes=["OB"], banks=[4, 5, 6])

    def emit_epi(qc):
        for j in range(3):
            add("dve", lambda e, j=j: e.tensor_copy(out=OS[:, j, 0:387], in_=OB[:, j * 512:j * 512 + 387]), reads=["OB"], writes=["OS"], banks=[4, 5, 6])
        OSf = OS.rearrange("p a b -> p (a b)")
        for qs in range(4):
            t = 4 * qc + qs
            pb = qs % 2
            o1 = acc_ap(2 * qs, OSf)
            o2 = acc_ap(2 * qs + 1, OSf)
            es = ES[pb]
            add("dve", lambda e, o1=o1, es=es: e.reciprocal(out=es[:, 0:1], in_=o1[:, 128:129]), reads=["OS"], writes=[("ESa", pb)])
            add("dve", lambda e, o2=o2, es=es: e.reciprocal(out=es[:, 1:2], in_=o2[:, 128:129]), reads=["OS"], writes=[("ESb", pb)])
            add("dve", lambda e, es=es: e.tensor_scalar(out=es[:, 2:3], in0=es[:, 1:2], scalar1=NEGLAM, scalar2=None, op0=ALU.mult),
                reads=[("ESb", pb), "NEGLAM"], writes=[("ESc", pb)])
            add("dve", lambda e, o1=o1, es=es, pb=pb: e.tensor_scalar(out=E1[pb], in0=o1[:, 0:128], scalar1=es[:, 0:1], scalar2=None, op0=ALU.mult),
                reads=["OS", ("ESa", pb)], writes=[("E1", pb)])
            add("dve", lambda e, o2=o2, es=es, pb=pb: e.scalar_tensor_tensor(out=E1[pb], in0=o2[:, 0:128], scalar=es[:, 2:3], in1=E1[pb], op0=ALU.mult, op1=ALU.add),
                reads=["OS", ("ESc", pb), ("E1", pb)], writes=[("E1", pb)])
            add("dve", lambda e, es=es, pb=pb: e.scalar_tensor_tensor(out=E2[pb], in0=E1[pb], scalar=1.0, in1=E1[pb], op0=ALU.mult, op1=ALU.mult, accum_out=es[:, 3:4]),
                reads=[("E1", pb)], writes=[("E2", pb), ("ESd", pb)])
            add("dve", lambda e, es=es: e.tensor_scalar(out=es[:, 3:4], in0=es[:, 3:4], scalar1=1.0 / 128, scalar2=EPS, op0=ALU.mult, op1=ALU.add),
                reads=[("ESd", pb)], writes=[("ESd", pb)])
            add("pool", lambda e, es=es: e.tensor_tensor(out=es[:, 4:5], in0=es[:, 3:4], in1=MH[:, 0:1], op=ALU.pow),
                reads=[("ESd", pb), "SMmh"], writes=[("ESe", pb)])
            add("dve", lambda e, es=es, pb=pb: e.scalar_tensor_tensor(out=E2[pb], in0=E1[pb], scalar=es[:, 4:5], in1=SUBW, op0=ALU.mult, op1=ALU.mult),
                reads=[("E1", pb), ("ESe", pb), "SUBW", ("E2", pb)], writes=[("E2", pb)])
            add("dve", lambda e, t=t, pb=pb: e.tensor_tensor(out=G[:, t, 0:128], in0=E2[pb], in1=G[:, t, 0:128], op=ALU.mult),
                reads=[("E2", pb), ("G", t)], writes=[("G", t)])
        mv = io["mix"].rearrange("(t p) c -> p t c", p=128)
        add("sync", lambda e, qc=qc: e.dma_start(out=mv[:, 4 * qc:4 * qc + 4, :], in_=G[:, 4 * qc:4 * qc + 4, :]),
            reads=[("G", 4 * qc + i) for i in range(4)], slot=("mixout", qc % 2))

    n = len(steps)
    emit_qk(steps[0][0], steps[0][1], 0)
    for i, (qc, kb) in enumerate(steps):
        sp = i % 2
        if i + 1 < n:
            emit_qk(steps[i + 1][0], steps[i + 1][1], (i + 1) % 2)
        emit_pv(qc, kb, sp)
        if kb == NT - 1:
            emit_epi(qc)
    A.pop()


def declare_B_io(nc, NT):
    S = NT * 128
    io = {}
    io["xn"] = nc.dram_tensor("xn", [S, 1024], BF16, kind="ExternalInput").ap()
    io["w"] = nc.dram_tensor("w", [1024, 896], F32, kind="ExternalInput").ap()
    io["normw"] = nc.dram_tensor("normw", [128, 8], F32, kind="ExternalInput").ap()
    io["vec64"] = nc.dram_tensor("vec64", [1, 6 * 64], F32, kind="ExternalInput").ap()
    io["subln"] = nc.dram_tensor("subln", [1, 128], F32, kind="ExternalInput").ap()
    io["scal"] = nc.dram_tensor("scal", [1, 4], F32, kind="ExternalInput").ap()
    io["cpackf"] = nc.dram_tensor("cpackf", [128, CF_N], F32, kind="ExternalInput").ap()
    io["cpackb"] = nc.dram_tensor("cpackb", [128, 256], BF16, kind="ExternalInput").ap()
    io["aug"] = nc.dram_tensor("aug", [4, 4, S], BF16, kind="ExternalInput").ap()
    io["mix"] = nc.dram_tensor("mix", [S, 256], BF16, kind="ExternalOutput").ap()
    return io


def build_B(NT):
    nc = bass.Bass("TRN2", target_bir_lowering=False)
    io = declare_B_io(nc, NT)
    with contextlib.ExitStack() as st:
        cx = Ctx(nc, st)
        emit_B(cx, io, NT)
        cx.P.emit(nc)
    return nc


def alibi_slope(h):
    return 2.0 ** (-8.0 * (h + 1) / 4)


def const_tables(hh, S):
    s = alibi_slope(hh)
    p = np.arange(128, dtype=np.float64)
    cf = np.zeros((128, CF_N), np.float32)
    tt = p[None, :] - p[:, None]
    cf[:, CF_M1:CF_M1 + 128] = np.maximum(tt, 0)
    cf[:, CF_M2:CF_M2 + 128] = np.maximum(-tt, 0)
    u = np.arange(128, dtype=np.float64)
    T = s * (p[:, None] + 128.0 * (u[None, :] - 64))
    cf[:, CF_T:CF_T + 128] = T
    cf[:, CF_TN:CF_TN + 128] = -T
    cf[:, CF_C5 + 0] = p + 1
    cf[:, CF_C5 + 1] = 128 - p
    cf[:, CF_C5 + 2] = 127 - p
    cf[:, CF_C5 + 3] = p
    cf[:, CF_C5 + 4] = 128
    cb = np.zeros((128, 256), np.float32)
    cb[:, 0:128] = np.eye(128)
    cb[:, 128:256] = -s * np.abs(tt)
    r = np.arange(S) % 512
    rlo = (r & 255).astype(np.float64)
    rhi = (r - (r & 255)).astype(np.float64)
    qa = np.stack([-s * rlo, -s * rhi, -2 * s * rlo, -2 * s * rhi])
    ka = np.stack([np.ones(S), np.ones(S), -np.ones(S), -np.ones(S)])
    aug = np.stack([qa, qa, ka, ka], axis=1)
    return cf, cb.astype(ml_dtypes.bfloat16), aug.astype(ml_dtypes.bfloat16)


def w_in_cols(hh):
    qa = [hh * 128 + m * 64 + j for m in range(2) for j in range(64)]
    ka = [512 + c for c in qa]
    va = [1024 + hh * 128 + j for j in range(128)]
    ga = [1536 + hh * 128 + j for j in range(128)]
    qr = [2048 + hh * 64 + j for j in range(64)]
    kr = [2304 + hh * 64 + j for j in range(64)]
    vr = [2560 + hh * 128 + j for j in range(128)]
    gr = [3072 + hh * 128 + j for j in range(128)]
    return np.array(qa + ka + qr + kr + va + vr + ga + gr)


def B_inputs(inputs, l, hh, xn_b, S):
    cf, cb, aug = const_tables(hh, S)
    lam_init = 0.8 - 0.6 * math.exp(-0.3 * l)
    f32 = np.float32
    vec = np.stack([inputs["q_norm_w"][l], inputs["k_norm_w"][l], inputs["lambda_q1"][l], inputs["lambda_k1"][l],
                    inputs["lambda_q2"][l], inputs["lambda_k2"][l]]).astype(f32).reshape(1, 384)
    scal = np.array([[inputs["ret_decay_fwd"][l, hh], inputs["ret_decay_bwd"][l, hh], lam_init, 1.0 - lam_init]], f32)
    return dict(
        xn=xn_b,
        w=np.ascontiguousarray(inputs["w_in"][l][:, w_in_cols(hh)]).astype(f32),
        normw=np.ascontiguousarray(inputs["norm_w"][l].reshape(8, 128).T).astype(f32),
        vec64=vec,
        subln=inputs["subln_w"][l].astype(f32).reshape(1, 128),
        scal=scal,
        cpackf=cf, cpackb=cb, aug=aug,
    )


def emit_norm_tile(cx, src_f32, dst_bf16, junk, st, key, MH):
    add = cx.P.add
    add("act", lambda e: e.activation(out=junk, in_=src_f32, func=AF.Square, accum_out=st[:, 0:1]), reads=[key], writes=[("nj", id(junk)), ("st0", id(st))])
    add("dve", lambda e: e.tensor_scalar(out=st[:, 1:2], in0=st[:, 0:1], scalar1=1.0 / D_MODEL, scalar2=EPS, op0=ALU.mult, op1=ALU.add),
        reads=[("st0", id(st))], writes=[("st1", id(st))])
    add("pool", lambda e: e.tensor_tensor(out=st[:, 2:3], in0=st[:, 1:2], in1=MH, op=ALU.pow), reads=[("st1", id(st)), "MHc"], writes=[("st2", id(st))])
    add("dve", lambda e: e.tensor_scalar(out=dst_bf16, in0=src_f32, scalar1=st[:, 2:3], scalar2=None, op0=ALU.mult),
        reads=[key, ("st2", id(st))], writes=[("xb", id(dst_bf16))])


def emit_A(cx, io, NTQ):
    P, A = cx.P, cx.A
    add = P.add
    A.push()
    XF = [A.alloc([1024], F32) for _ in range(2)]
    XB = [A.alloc([1024], BF16) for _ in range(2)]
    JK = [A.alloc([1024], BF16) for _ in range(2)]
    ST = [A.alloc([4], F32) for _ in range(2)]
    MHc = A.alloc([1], F32)
    add("pool", lambda e: e.memset(MHc, -0.5), writes=["MHc"])
    xv = io["x"].rearrange("(t p) d -> t p d", p=128)
    ov = io["xn_out"].rearrange("(t p) d -> t p d", p=128)
    for t in range(NTQ):
        pb = t % 2
        add("sync", lambda e, t=t, pb=pb: e.dma_start(out=XF[pb], in_=xv[t]), writes=[("XF", pb)], slot=("xf", pb))
        emit_norm_tile(cx, XF[pb], XB[pb], JK[pb], ST[pb], ("XF", pb), MHc)
        add("sync", lambda e, t=t, pb=pb: e.dma_start(out=ov[t], in_=XB[pb]), reads=[("xb", id(XB[pb]))], slot=("xno", pb))
    A.pop()
    P.barrier()


def build_A(NTQ):
    nc = bass.Bass("TRN2", target_bir_lowering=False)
    io = {}
    io["x"] = nc.dram_tensor("x", [NTQ * 128, 1024], F32, kind="ExternalInput").ap()
    io["xn_out"] = nc.dram_tensor("xn_out", [NTQ * 128, 1024], BF16, kind="ExternalOutput").ap()
    with contextlib.ExitStack() as st:
        cx = Ctx(nc, st)
        emit_A(cx, io, NTQ)
        cx.P.emit(nc)
    return nc


def emit_C(cx, io, NTQ, with_norm):
    P, A = cx.P, cx.A
    add = P.add
    A.push()
    WO = A.alloc([8, 1024], BF16)
    WST = [A.alloc([1024], F32) for _ in range(2)]
    IDN = A.alloc([128], BF16)
    MX = [A.alloc([1024], BF16) for _ in range(2)]
    MT = [A.alloc([8, 128], BF16) for _ in range(2)]
    XR = [A.alloc([1024], F32) for _ in range(2)]
    XO = [A.alloc([1024], F32) for _ in range(2)]
    XB = [A.alloc([1024], BF16) for _ in range(2)]
    JK = [A.alloc([1024], BF16) for _ in range(2)]
    ST = [A.alloc([4], F32) for _ in range(2)]
    MHc = A.alloc([1], F32)
    add("pool", lambda e: e.memset(MHc, -0.5), writes=["MHc"])
    add("sync", lambda e: e.dma_start(out=IDN, in_=io["ident"]), writes=["IDN"], slot="ldi")
    wv = io["wout"].rearrange("(k p) n -> k p n", p=128)
    for kc in range(8):
        b_ = kc % 2
        add("sync", lambda e, kc=kc, b_=b_: e.dma_start(out=WST[b_], in_=wv[kc]), writes=[("WSTo", b_)], slot=("wsto", b_))
        add("dve" if kc % 2 else "pool", lambda e, kc=kc, b_=b_: e.tensor_copy(out=WO[:, kc, :], in_=WST[b_]), reads=[("WSTo", b_)], writes=["WO"])
    if "mix_hm" in io:
        mvh = io["mix_hm"].rearrange("h (t p) c -> t p h c", p=128)
        mv = [mvh[t] for t in range(NTQ)]
        MXv = [m.rearrange("p (h c) -> p h c", h=4) for m in MX]
    else:
        mvf = io["mixq"].rearrange("(t p) d -> t p d", p=128)
        mv = [mvf[t] for t in range(NTQ)]
        MXv = MX
    xv = io["xres"].rearrange("(t p) d -> t p d", p=128)
    ov = io["xnew"].rearrange("(t p) d -> t p d", p=128)
    if with_norm:
        nv = io["xn_out"].rearrange("(t p) d -> t p d", p=128)
    for t in range(NTQ):
        pb = t % 2
        TPb = cx.bankb(pb)
        ACC = cx.bank(2 + 2 * pb, 2)
        add("sync", lambda e, t=t, pb=pb: e.dma_start(out=MXv[pb], in_=mv[t]), writes=[("MX", pb)], slot=("mx", pb))
        add("sync", lambda e, t=t, pb=pb: e.dma_start(out=XR[pb], in_=xv[t]), writes=[("XR", pb)], slot=("xr", pb))
        for kc in range(8):
            add("pe", lambda e, pb=pb, kc=kc, TPb=TPb: e.transpose(out=TPb[:, kc * 128:(kc + 1) * 128], in_=MX[pb][:, kc * 128:(kc + 1) * 128], identity=IDN),
                reads=[("MX", pb), "IDN"], writes=[("TPc", pb)], banks=[pb])
        add("act", lambda e, pb=pb, TPb=TPb: e.activation(out=MT[pb].rearrange("p a b -> p (a b)"), in_=TPb, func=AF.Copy),
            reads=[("TPc", pb)], writes=[("MT", pb)], banks=[pb])
        for hf in range(2):
            for kc in range(8):
                add("pe", lambda e, pb=pb, kc=kc, hf=hf, ACC=ACC: e.matmul(out=ACC[:, hf * 512:(hf + 1) * 512], lhsT=MT[pb][:, kc, :], rhs=WO[:, kc, hf * 512:(hf + 1) * 512],
                                                                          start=(kc == 0), stop=(kc == 7)),
                    reads=[("MT", pb), "WO"], writes=[("ACC", pb)], banks=[2 + 2 * pb, 3 + 2 * pb])
        add("dve", lambda e, pb=pb, ACC=ACC: e.tensor_tensor(out=XO[pb], in0=ACC, in1=XR[pb], op=ALU.add),
            reads=[("ACC", pb), ("XR", pb)], writes=[("XO", pb)], banks=[2 + 2 * pb, 3 + 2 * pb])
        add("sync", lambda e, t=t, pb=pb: e.dma_start(out=ov[t], in_=XO[pb]), reads=[("XO", pb)], slot=("xo", pb))
        if with_norm:
            emit_norm_tile(cx, XO[pb], XB[pb], JK[pb], ST[pb], ("XO", pb), MHc)
            add("sync", lambda e, t=t, pb=pb: e.dma_start(out=nv[t], in_=XB[pb]), reads=[("xb", id(XB[pb]))], slot=("xnc", pb))
    A.pop()
    P.barrier()


def build_C(NTQ, with_norm):
    nc = bass.Bass("TRN2", target_bir_lowering=False)
    io = {}
    io["mixq"] = nc.dram_tensor("mixq", [NTQ * 128, 1024], BF16, kind="ExternalInput").ap()
    io["xres"] = nc.dram_tensor("xres", [NTQ * 128, 1024], F32, kind="ExternalInput").ap()
    io["wout"] = nc.dram_tensor("wout", [1024, 1024], F32, kind="ExternalInput").ap()
    io["ident"] = nc.dram_tensor("ident", [128, 128], BF16, kind="ExternalInput").ap()
    io["xnew"] = nc.dram_tensor("xnew", [NTQ * 128, 1024], F32, kind="ExternalOutput").ap()
    if with_norm:
        io["xn_out"] = nc.dram_tensor("xn_out", [NTQ * 128, 1024], BF16, kind="ExternalOutput").ap()
    with contextlib.ExitStack() as st:
        cx = Ctx(nc, st)
        emit_C(cx, io, NTQ, with_norm)
        cx.P.emit(nc)
    return nc


def wout_rows():
    rows = []
    for h in range(4):
        rows += list(range(h * 128, (h + 1) * 128)) + list(range(512 + h * 128, 512 + (h + 1) * 128))
    return np.array(rows)


_CACHE = {}


def _get(name, fn):
    if name not in _CACHE:
        _CACHE[name] = fn()
    return _CACHE[name]


def kernel_unfused(**inputs):
    inputs = {k: np.asarray(v) for k, v in inputs.items()}
    x = inputs["x"]
    Bsz, S, Dm = x.shape
    NT = S // 128
    NTQ = NT // 4
    TQ = S // 4
    cores = list(range(8))
    ident = np.eye(128, dtype=np.float32).astype(ml_dtypes.bfloat16)
    ncA = _get("A", lambda: build_A(NTQ))
    res = run_bass_kernel_spmd(ncA, [dict(x=np.ascontiguousarray(x[c // 4, (c % 4) * TQ:(c % 4 + 1) * TQ])) for c in cores], core_ids=cores)
    xn_q = [r["xn_out"] for r in res.results]
    xcur = [np.ascontiguousarray(x[c // 4, (c % 4) * TQ:(c % 4 + 1) * TQ]) for c in cores]
    for l in range(2):
        xn_b = [np.concatenate([xn_q[b * 4 + j] for j in range(4)], axis=0) for b in range(Bsz)]
        ncB = _get("B", lambda: build_B(NT))
        res = run_bass_kernel_spmd(ncB, [B_inputs(inputs, l, c % 4, xn_b[c // 4], S) for c in cores], core_ids=cores)
        mix = [r["mix"] for r in res.results]
        wout = np.ascontiguousarray(inputs["w_out"][l][wout_rows()]).astype(np.float32)
        last = (l == 1)
        ncC = _get("C%d" % l, lambda: build_C(NTQ, not last))
        ins = []
        for c in cores:
            b, j = c // 4, c % 4
            mixq = np.concatenate([mix[b * 4 + h][j * TQ:(j + 1) * TQ] for h in range(4)], axis=1)
            ins.append(dict(mixq=np.ascontiguousarray(mixq), xres=xcur[c], wout=wout, ident=ident))
        res = run_bass_kernel_spmd(ncC, ins, core_ids=cores)
        xcur = [r["xnew"] for r in res.results]
        if not last:
            xn_q = [r["xn_out"] for r in res.results]
    out = np.stack([np.concatenate([xcur[b * 4 + j] for j in range(4)], axis=0) for b in range(Bsz)], axis=0)
    return out.astype(np.float32)


def build_fused(NT):
    S = NT * 128
    nc = bass.Bass("TRN2", target_bir_lowering=False)
    dt = nc.dram_tensor
    x = dt("x", [S, 1024], F32, kind="ExternalInput").ap()
    w_all = dt("w_all", [2, 4, 1024, 896], F32, kind="ExternalInput").ap()
    normw = dt("normw", [2, 128, 8], F32, kind="ExternalInput").ap()
    vec64 = dt("vec64", [2, 1, 384], F32, kind="ExternalInput").ap()
    subln = dt("subln", [2, 1, 128], F32, kind="ExternalInput").ap()
    scal = dt("scal", [2, 4, 1, 4], F32, kind="ExternalInput").ap()
    cpackf = dt("cpackf", [4, 128, CF_N], F32, kind="ExternalInput").ap()
    cpackb = dt("cpackb", [4, 128, 256], BF16, kind="ExternalInput").ap()
    aug = dt("aug", [4, 4, 4, S], BF16, kind="ExternalInput").ap()
    wout = dt("wout", [2, 1024, 1024], F32, kind="ExternalInput").ap()
    identd = dt("ident", [128, 128], BF16, kind="ExternalInput").ap()
    out = dt("out", [S, 1024], F32, kind="ExternalOutput").ap()
    XNs = dt("xn_scratch", [S, 1024], BF16, kind="Internal").ap()
    MIXs = dt("mix_scratch", [4, S, 256], BF16, kind="Internal").ap()
    X1 = dt("x1_scratch", [S, 1024], F32, kind="Internal").ap()
    with contextlib.ExitStack() as st:
        cx = Ctx(nc, st)
        emit_A(cx, dict(x=x, xn_out=XNs), NT)
        for l in range(2):
            for h in range(4):
                emit_B(cx, dict(xn=XNs, w=w_all[l, h], normw=normw[l], vec64=vec64[l], subln=subln[l], scal=scal[l, h],
                                cpackf=cpackf[h], cpackb=cpackb[h], aug=aug[h], mix=MIXs[h]), NT)
            ioC = dict(mix_hm=MIXs, xres=(x if l == 0 else X1), wout=wout[l], ident=identd, xnew=(X1 if l == 0 else out))
            if l == 0:
                ioC["xn_out"] = XNs
            emit_C(cx, ioC, NT, with_norm=(l == 0))
        cx.P.emit(nc)
    print("[build_fused] ops", len(cx.P.ops), "sem counts", {str(k): v for k, v in cx.P.sem_counts.items() if k[0] == "eng"}, flush=True)
    return nc


def fused_inputs(inputs, b, S):
    f32 = np.float32
    tabs = [const_tables(h, S) for h in range(4)]
    lam_init = [0.8 - 0.6 * math.exp(-0.3 * l) for l in range(2)]
    vec = np.stack([np.stack([inputs["q_norm_w"][l], inputs["k_norm_w"][l], inputs["lambda_q1"][l], inputs["lambda_k1"][l],
                              inputs["lambda_q2"][l], inputs["lambda_k2"][l]]).reshape(1, 384) for l in range(2)]).astype(f32)
    scal = np.array([[[[inputs["ret_decay_fwd"][l, h], inputs["ret_decay_bwd"][l, h], lam_init[l], 1.0 - lam_init[l]]] for h in range(4)] for l in range(2)], f32)
    return dict(
        x=np.ascontiguousarray(inputs["x"][b]).astype(f32),
        w_all=np.stack([np.stack([inputs["w_in"][l][:, w_in_cols(h)] for h in range(4)]) for l in range(2)]).astype(f32),
        normw=np.stack([inputs["norm_w"][l].reshape(8, 128).T for l in range(2)]).astype(f32),
        vec64=vec,
        subln=np.stack([inputs["subln_w"][l].reshape(1, 128) for l in range(2)]).astype(f32),
        scal=scal,
        cpackf=np.stack([t[0] for t in tabs]), cpackb=np.stack([t[1] for t in tabs]), aug=np.stack([t[2] for t in tabs]),
        wout=np.stack([inputs["w_out"][l][wout_rows()] for l in range(2)]).astype(f32),
        ident=np.eye(128, dtype=f32).astype(ml_dtypes.bfloat16),
    )


def kernel_fused(**inputs):
    inputs = {k: np.asarray(v) for k, v in inputs.items()}
    Bsz, S, _ = inputs["x"].shape
    nc = _get("F", lambda: build_fused(S // 128))
    res = run_bass_kernel_spmd(nc, [fused_inputs(inputs, b, S) for b in range(Bsz)], core_ids=list(range(Bsz)))
    return np.stack([res.results[b]["out"] for b in range(Bsz)], axis=0).astype(np.float32)


def kernel(**inputs):
    return kernel_fused(**inputs)
```

```python
import contextlib
import math
import numpy as np
import ml_dtypes
import concourse.bass as bass
import concourse.mybir as mybir
from concourse.bass_utils import run_bass_kernel_spmd

F32 = mybir.dt.float32
BF16 = mybir.dt.bfloat16
AF = mybir.ActivationFunctionType
ALU = mybir.AluOpType
AX = mybir.AxisListType

SAME_ENGINE_SYNC = True
ENGS = ("sync", "act", "dve", "pool", "pe")
D_MODEL = 1024
EPS = 1e-6


class _Op:
    __slots__ = ("eng", "fn", "deps", "slot", "val", "waits", "idx")

    def __init__(self, eng, fn, deps, slot, idx):
        self.eng = eng
        self.fn = fn
        self.deps = deps
        self.slot = slot
        self.val = None
        self.waits = None
        self.idx = idx


class Prog:
    def __init__(self):
        self.ops = []
        self.last_w = {}
        self.readers = {}
        self.last_on_eng = {}
        self.last_on_slot = {}
        self.pending = {e: set() for e in ENGS}

    def add(self, eng, fn, reads=(), writes=(), slot=None, banks=()):
        i = len(self.ops)
        deps = set()
        if banks:
            writes = list(writes) + [("bank", b) for b in banks]
        for k in reads:
            w = self.last_w.get(k)
            if w is not None:
                deps.add(w)
        for k in writes:
            w = self.last_w.get(k)
            if w is not None:
                deps.add(w)
            for r in self.readers.get(k, ()):
                deps.add(r)
        for k in reads:
            self.readers.setdefault(k, []).append(i)
        for k in writes:
            self.last_w[k] = i
            self.readers[k] = []
        if self.pending[eng]:
            deps |= self.pending[eng]
            self.pending[eng] = set()
        if slot is not None:
            p = self.last_on_slot.get(slot)
            if p is not None:
                deps.add(p)
            self.last_on_slot[slot] = i
        else:
            self.last_on_eng[eng] = i
        deps.discard(i)
        self.ops.append(_Op(eng, fn, deps, slot, i))
        return i

    def barrier(self):
        allp = set(self.last_on_eng.values()) | set(self.last_on_slot.values())
        for e in ENGS:
            self.pending[e] |= allp

    def _semkey(self, op):
        return ("slot", op.slot) if op.slot is not None else ("eng", op.eng)

    def emit(self, nc, final_wait_eng="sync"):
        ops = self.ops
        self.barrier()
        self.add(final_wait_eng, None)
        waited = {e: {} for e in ENGS}
        need = set()
        for op in ops:
            ws = {}
            for d in op.deps:
                y = ops[d]
                sk = self._semkey(y)
                if y.slot is None and y.eng == op.eng:
                    if op.eng in ("pe", "sync"):
                        continue
                    if not SAME_ENGINE_SYNC:
                        continue
                if waited[op.eng].get(sk, -1) >= d:
                    continue
                if sk not in ws or ws[sk] < d:
                    ws[sk] = d
            for sk, d in ws.items():
                waited[op.eng][sk] = d
                need.add(d)
            op.waits = ws
        cnt = {}
        for op in ops:
            sk = self._semkey(op)
            if op.slot is not None:
                cnt[sk] = cnt.get(sk, 0) + 16
                op.val = cnt[sk]
            elif op.idx in need:
                cnt[sk] = cnt.get(sk, 0) + 1
                op.val = cnt[sk]
        self.sem_counts = dict(cnt)
        sems = {}
        with contextlib.ExitStack() as stack:
            for n_, sk in enumerate(cnt):
                sems[sk] = stack.enter_context(nc.semaphore("sem%d" % n_))
            block = stack.enter_context(nc.Block())

            def run(eng_name):
                def body(eng):
                    for op in ops:
                        if op.eng != eng_name:
                            continue
                        for sk, d in op.waits.items():
                            eng.wait_ge(sems[sk], ops[d].val)
                        if op.fn is None:
                            continue
                        ins = op.fn(eng)
                        if op.val is not None:
                            ins.then_inc(sems[self._semkey(op)], 16 if op.slot is not None else 1)
                return body

            block.sync(run("sync"))
            block.scalar(run("act"))
            block.vector(run("dve"))
            block.gpsimd(run("pool"))
            block.tensor(run("pe"))


class Arena:
    def __init__(self, base_ap_bf16, nbytes):
        self.base = base_ap_bf16
        self.nbytes = nbytes
        self.top = 0
        self.marks = []
        self.peak = 0

    def alloc(self, shape_free, dtype, parts=128, align=32):
        esz = 4 if dtype == F32 else 2
        n = int(np.prod(shape_free))
        off = (self.top + align - 1) // align * align
        nb = n * esz
        assert off + nb <= self.nbytes, ("arena overflow", off, nb, self.nbytes)
        self.top = off + nb
        self.peak = max(self.peak, self.top)
        ap = self.base[0:parts, off // 2:(off + nb) // 2]
        if dtype == F32:
            ap = ap.bitcast(F32)
        if len(shape_free) == 2:
            ap = ap.rearrange("p (a b) -> p a b", a=shape_free[0])
        elif len(shape_free) == 3:
            ap = ap.rearrange("p (a b c) -> p a b c", a=shape_free[0], b=shape_free[1])
        return ap

    def push(self):
        self.marks.append(self.top)

    def pop(self):
        self.top = self.marks.pop()


SBUF_BYTES = 212736


class Ctx:
    def __init__(self, nc, stack):
        self.nc = nc
        self.P = Prog()
        sb = stack.enter_context(nc.sbuf_tensor("arena", [128, SBUF_BYTES // 2], BF16))
        self.A = Arena(sb[:, :], SBUF_BYTES)
        ps = stack.enter_context(nc.psum_tensor("psum_all", [128, 4096], F32))
        self.PS = ps[:, :]

    def bank(self, i, n=1):
        return self.PS[:, i * 512:(i + n) * 512]

    def bankb(self, i, n=1):
        return self.PS[:, i * 512:(i + n) * 512].bitcast(BF16)


CF_M1, CF_M2, CF_T, CF_TN, CF_C5, CF_N = 0, 128, 256, 384, 512, 520
SM_LGF, SM_LGB, SM_XIF, SM_XIB, SM_ZF, SM_ZB, SM_CD, SM_NEGLAM, SM_ZERO, SM_MH = 0, 1, 2, 3, 4, 5, 6, 7, 8, 9


SKIP_ARG = 87.4 + 16.0


def emit_B(cx, io, NT, slope=None):
    P, A = cx.P, cx.A
    A.push()
    _emit_B(cx, io, NT, slope)
    A.pop()
    P.barrier()


def _emit_B(cx, io, NT, slope=None):
    P, A = cx.P, cx.A
    S = NT * 128
    NQC = NT // 4
    add = P.add

    QKT = A.alloc([4, S], BF16, parts=68)
    VV = A.alloc([NT, 2, 132], BF16)
    G = A.alloc([NT, 256], BF16)
    RQ = A.alloc([NT, 128], BF16)
    CF = A.alloc([CF_N], F32)
    CB = A.alloc([256], BF16)
    WQK = A.alloc([256], F32)
    CQK = A.alloc([128], F32)
    SUBW = A.alloc([128], F32)
    V6 = A.alloc([6, 64], F32)
    SC = A.alloc([4], F32)
    SM = A.alloc([16], F32)
    DC = A.alloc([128], F32)
    TMPA = A.alloc([128], F32)
    TMPB = A.alloc([128], F32)
    TS = A.alloc([8], F32)
    ident = CB[:, 0:128]
    Bm = CB[:, 128:256]

    add("sync", lambda e: e.dma_start(out=CF, in_=io["cpackf"]), writes=["CF"], slot="ld0")
    add("sync", lambda e: e.dma_start(out=CB, in_=io["cpackb"]), writes=["CB"], slot="ld1")
    add("sync", lambda e: e.dma_start(out=V6, in_=io["vec64"].partition_broadcast(128)), writes=["V6"], slot="ld2")
    add("sync", lambda e: e.dma_start(out=SUBW, in_=io["subln"].partition_broadcast(128)), writes=["SUBW"], slot="ld3")
    add("sync", lambda e: e.dma_start(out=SC, in_=io["scal"].partition_broadcast(128)), writes=["SC"], slot="ld4")
    add("sync", lambda e: e.dma_start(out=QKT[64:68, :, :], in_=io["aug"]), writes=["QKTaug"], slot="ld5")
    add("pool", lambda e: e.memset(VV[:, :, 0, 128:129], 1.0), writes=["VVone"])
    add("pool", lambda e: e.memset(CQK[:, 0:64], 1.0), writes=["CQK"])
    add("pool", lambda e: e.memset(CQK[:, 64:128], 0.125), writes=["CQK"])
    add("pool", lambda e: e.memset(SM[:, SM_ZERO:SM_ZERO + 1], 0.0), writes=["SMz"])
    add("pool", lambda e: e.memset(SM[:, SM_MH:SM_MH + 4], -0.5), writes=["SMmh"])
    MH = SM[:, SM_MH:SM_MH + 4]
    ZERO = SM[:, SM_ZERO:SM_ZERO + 1]
    add("dve", lambda e: e.tensor_scalar(out=WQK[:, 0:128].rearrange("p (a b) -> p a b", a=2),
                                         in0=V6[:, 0, :].unsqueeze(1).to_broadcast([128, 2, 64]),
                                         scalar1=0.125, scalar2=None, op0=ALU.mult), reads=["V6"], writes=["WQK"])
    add("dve", lambda e: e.tensor_scalar(out=WQK[:, 128:256].rearrange("p (a b) -> p a b", a=2),
                                         in0=V6[:, 1, :].unsqueeze(1).to_broadcast([128, 2, 64]),
                                         scalar1=1.0, scalar2=None, op0=ALU.mult), reads=["V6"], writes=["WQK"])
    add("dve", lambda e: e.tensor_tensor(out=TMPA[:, 0:64], in0=V6[:, 2, :], in1=V6[:, 3, :], op=ALU.mult), reads=["V6"], writes=["TMPA"])
    add("dve", lambda e: e.reduce_sum(out=TS[:, 0:1], in_=TMPA[:, 0:64], axis=AX.X), reads=["TMPA"], writes=["TS0"])
    add("dve", lambda e: e.tensor_tensor(out=TMPA[:, 64:128], in0=V6[:, 4, :], in1=V6[:, 5, :], op=ALU.mult), reads=["V6"], writes=["TMPA2"])
    add("dve", lambda e: e.reduce_sum(out=TS[:, 1:2], in_=TMPA[:, 64:128], axis=AX.X), reads=["TMPA2"], writes=["TS1"])
    add("act", lambda e: e.activation(out=TS[:, 2:4], in_=TS[:, 0:2], func=AF.Exp), reads=["TS0", "TS1"], writes=["TS23"])
    add("dve", lambda e: e.tensor_tensor(out=TS[:, 4:5], in0=TS[:, 2:3], in1=TS[:, 3:4], op=ALU.subtract), reads=["TS23"], writes=["TS4"])
    add("dve", lambda e: e.tensor_scalar(out=SM[:, SM_NEGLAM:SM_NEGLAM + 1], in0=TS[:, 4:5], scalar1=SC[:, 2:3], scalar2=-1.0,
                                         op0=ALU.add, op1=ALU.mult), reads=["TS4", "SC"], writes=["NEGLAM"])
    NEGLAM = SM[:, SM_NEGLAM:SM_NEGLAM + 1]
    add("dve", lambda e: e.tensor_scalar(out=SUBW, in0=SUBW, scalar1=SC[:, 3:4], scalar2=None, op0=ALU.mult), reads=["SUBW", "SC"], writes=["SUBW"])
    add("act", lambda e: e.activation(out=TS[:, 5:7], in_=SC[:, 0:2], func=AF.Exp, scale=-1.0), reads=["SC"], writes=["TS56"])
    add("act", lambda e: e.activation(out=TS[:, 5:7], in_=TS[:, 5:7], func=AF.Ln, bias=1.0, scale=1.0), reads=["TS56"], writes=["TS56"])
    add("dve", lambda e: e.tensor_scalar(out=SM[:, 0:2], in0=TS[:, 5:7], scalar1=-1.0, scalar2=None, op0=ALU.mult), reads=["TS56"], writes=["LG"])
    LGF, LGB = SM[:, 0:1], SM[:, 1:2]
    C5 = CF[:, CF_C5:CF_C5 + 5]
    add("act", lambda e: e.activation(out=SM[:, SM_XIF:SM_XIF + 1], in_=C5[:, 0:1], func=AF.Exp, scale=LGF), reads=["LG", "CF"], writes=["XIF"])
    add("act", lambda e: e.activation(out=SM[:, SM_XIB:SM_XIB + 1], in_=C5[:, 1:2], func=AF.Exp, scale=LGB), reads=["LG", "CF"], writes=["XIB"])
    add("act", lambda e: e.activation(out=SM[:, SM_ZF:SM_ZF + 1], in_=C5[:, 2:3], func=AF.Exp, scale=LGF), reads=["LG", "CF"], writes=["ZF"])
    add("act", lambda e: e.activation(out=SM[:, SM_ZB:SM_ZB + 1], in_=C5[:, 3:4], func=AF.Exp, scale=LGB), reads=["LG", "CF"], writes=["ZB"])
    add("act", lambda e: e.activation(out=SM[0:64, SM_CD:SM_CD + 1], in_=C5[0:64, 4:5], func=AF.Exp, scale=SM[0:64, 0:1]), reads=["LG", "CF"], writes=["CDf"])
    add("act", lambda e: e.activation(out=SM[64:128, SM_CD:SM_CD + 1], in_=C5[64:128, 4:5], func=AF.Exp, scale=SM[64:128, 1:2]), reads=["LG", "CF"], writes=["CDb"])
    XIF, XIB = SM[:, SM_XIF:SM_XIF + 1], SM[:, SM_XIB:SM_XIB + 1]
    ZF, ZB = SM[:, SM_ZF:SM_ZF + 1], SM[:, SM_ZB:SM_ZB + 1]
    add("dve", lambda e: e.tensor_scalar(out=TMPB, in0=CF[:, CF_M1:CF_M1 + 128], scalar1=LGF, scalar2=None, op0=ALU.mult), reads=["LG", "CF"], writes=["TMPB"])
    add("dve", lambda e: e.scalar_tensor_tensor(out=TMPB, in0=CF[:, CF_M2:CF_M2 + 128], scalar=LGB, in1=TMPB, op0=ALU.mult, op1=ALU.add), reads=["LG", "CF", "TMPB"], writes=["TMPB"])
    add("act", lambda e: e.activation(out=DC, in_=TMPB, func=AF.Exp), reads=["TMPB"], writes=["DC"])

    A.push()
    W = A.alloc([8, 896], BF16)
    WST = [A.alloc([896], F32) for _ in range(2)]
    NW = A.alloc([8], F32)
    XN = [A.alloc([1024], BF16) for _ in range(3)]
    XT = [A.alloc([8, 128], BF16) for _ in range(2)]
    QKF = [A.alloc([256], F32) for _ in range(4)]
    SQ = [A.alloc([256], F32) for _ in range(4)]
    QKN = [A.alloc([256], BF16) for _ in range(4)]
    S4 = [A.alloc([8], F32) for _ in range(4)]
    add("sync", lambda e: e.dma_start(out=NW, in_=io["normw"]), writes=["NW"], slot="ld6")
    wv = io["w"].rearrange("(k p) n -> k p n", p=128)
    for kc in range(8):
        b_ = kc % 2
        add("sync", lambda e, kc=kc, b_=b_: e.dma_start(out=WST[b_], in_=wv[kc]), writes=[("WST", b_)], slot=("wst", b_))
        add("dve", lambda e, kc=kc, b_=b_: e.tensor_scalar(out=W[:, kc, :], in0=WST[b_], scalar1=NW[:, kc:kc + 1], scalar2=None, op0=ALU.mult),
            reads=[("WST", b_), "NW"], writes=["W"])
    xv = io["xn"].rearrange("(t p) d -> t p d", p=128)

    def HAb(t):
        return cx.bank(2 + 2 * (t % 2))

    def HBb(t):
        return cx.bank(3 + 2 * (t % 2))

    def TQb(t):
        pb = t % 2
        return cx.bankb(6 + pb)[0:64, 0:512].rearrange("p (a b) -> p a b", a=4)

    def ip_ld(t):
        x3 = t % 3
        add("sync", lambda e: e.dma_start(out=XN[x3], in_=xv[t]), writes=[("XN", x3)], slot=("xn", x3))

    def ip_xpose(t):
        x3, pb = t % 3, t % 2
        TPb = cx.bankb(pb)
        for kc in range(8):
            add("pe", lambda e, kc=kc: e.transpose(out=TPb[:, kc * 128:(kc + 1) * 128], in_=XN[x3][:, kc * 128:(kc + 1) * 128], identity=ident),
                reads=[("XN", x3), "CB"], writes=[("TP", pb)], banks=[pb])
        add("act", lambda e: e.activation(out=XT[pb].rearrange("p a b -> p (a b)"), in_=TPb, func=AF.Copy), reads=[("TP", pb)], writes=[("XT", pb)], banks=[pb])

    def ip_mm(t):
        pb = t % 2
        HA, HB = HAb(t), HBb(t)
        for kc in range(8):
            add("pe", lambda e, kc=kc: e.matmul(out=HA[:, 0:384], lhsT=XT[pb][:, kc, :], rhs=W[:, kc, 0:384], start=(kc == 0), stop=(kc == 7)),
                reads=[("XT", pb), "W"], writes=[("HA", pb)], banks=[2 + 2 * pb])
        for kc in range(8):
            add("pe", lambda e, kc=kc: e.matmul(out=HB, lhsT=XT[pb][:, kc, :], rhs=W[:, kc, 384:896], start=(kc == 0), stop=(kc == 7)),
                reads=[("XT", pb), "W"], writes=[("HB", pb)], banks=[3 + 2 * pb])

    def ip_evac(t):
        pb, q4 = t % 2, t % 4
        HA, HB = HAb(t), HBb(t)
        add("dve", lambda e: e.tensor_copy(out=QKF[q4], in_=HA[:, 0:256]), reads=[("HA", pb)], writes=[("QKF", q4)], banks=[2 + 2 * pb])
        add("dve", lambda e: e.tensor_tensor(out=RQ[:, t, :], in0=HA[:, 256:384], in1=CQK, op=ALU.mult), reads=[("HA", pb), "CQK"], writes=[("RQ", t)], banks=[2 + 2 * pb])
        add("dve", lambda e: e.tensor_copy(out=VV[:, t, :, 0:128], in_=HB[:, 0:256].rearrange("p (a b) -> p a b", a=2)), reads=[("HB", pb)], writes=[("VV", t)], banks=[3 + 2 * pb])
        add("act", lambda e: e.activation(out=G[:, t, :], in_=HB[:, 256:512], func=AF.Silu), reads=[("HB", pb)], writes=[("G", t)], banks=[3 + 2 * pb])

    def ip_normA(t):
        q4 = t % 4
        add("act", lambda e: e.activation(out=SQ[q4], in_=QKF[q4], func=AF.Square), reads=[("QKF", q4)], writes=[("SQ", q4)])
        add("dve", lambda e: e.reduce_sum(out=S4[q4][:, 0:4], in_=SQ[q4].rearrange("p (a b) -> p a b", a=4), axis=AX.X), reads=[("SQ", q4)], writes=[("S4a", q4)])
        add("dve", lambda e: e.tensor_scalar(out=S4[q4][:, 0:4], in0=S4[q4][:, 0:4], scalar1=1.0 / 64, scalar2=EPS, op0=ALU.mult, op1=ALU.add),
            reads=[("S4a", q4)], writes=[("S4a", q4)])
        add("pool", lambda e: e.tensor_tensor(out=S4[q4][:, 4:8], in0=S4[q4][:, 0:4], in1=MH, op=ALU.pow), reads=[("S4a", q4), "SMmh"], writes=[("S4b", q4)])

    def ip_normB(t):
        q4 = t % 4
        for g in range(4):
            add("dve", lambda e, g=g: e.scalar_tensor_tensor(out=QKN[q4][:, g * 64:(g + 1) * 64], in0=QKF[q4][:, g * 64:(g + 1) * 64], scalar=S4[q4][:, 4 + g:5 + g],
                                                            in1=WQK[:, g * 64:(g + 1) * 64], op0=ALU.mult, op1=ALU.mult),
                reads=[("QKF", q4), ("S4b", q4), "WQK"], writes=[("QKN", q4)])

    def ip_qkT(t):
        q4, pb = t % 4, t % 2
        TQ = TQb(t)
        for g in range(4):
            add("pe", lambda e, g=g: e.transpose(out=TQ[:, g, :], in_=QKN[q4][:, g * 64:(g + 1) * 64], identity=ident), reads=[("QKN", q4), "CB"], writes=[("TQ", pb)], banks=[6 + pb])

    def ip_qkC(t):
        pb = t % 2
        TQ = TQb(t)
        add("dve", lambda e: e.tensor_copy(out=QKT[0:64, :, t * 128:(t + 1) * 128], in_=TQ), reads=[("TQ", pb)], writes=[("QKT", t)], banks=[6 + pb])

    ip_ld(0)
    if NT > 1:
        ip_ld(1)
    ip_xpose(0)
    for t in range(NT + 3):
        if t + 2 < NT:
            ip_ld(t + 2)
        if t + 1 < NT:
            ip_xpose(t + 1)
        if t < NT:
            ip_mm(t)
        if 0 <= t - 3 < NT:
            ip_qkC(t - 3)
        if t < NT:
            ip_evac(t)
        if 0 <= t - 1 < NT:
            ip_normB(t - 1)
        if 0 <= t - 2 < NT:
            ip_qkT(t - 2)
        if t < NT:
            ip_normA(t)
    A.pop()
    P.barrier()

    A.push()
    B = A.alloc([NT, 128], F32)
    NBUF = 3
    KZ = [A.alloc([4, 128], BF16) for _ in range(NBUF)]
    RT = [A.alloc([4, 2, 128], BF16, parts=64) for _ in range(2)]
    QX = [A.alloc([4, 128], BF16) for _ in range(NBUF)]
    QXT = [A.alloc([4, 128], BF16) for _ in range(NBUF)]
    WT = [A.alloc([4, 128], BF16) for _ in range(NBUF)]
    SSg = [A.alloc([4, 128], BF16) for _ in range(NBUF)]
    SQ2 = [A.alloc([512], F32) for _ in range(1)] * 2
    R4 = [A.alloc([8], F32) for _ in range(NBUF)]
    NG = NT // 4

    def rq(g):
        return [("RQ", 4 * g + i) for i in range(4)]

    def p1_kz(g):
        b3 = g % NBUF
        add("dve", lambda e: e.tensor_scalar(out=KZ[b3][:, :, 0:64], in0=RQ[:, 4 * g:4 * g + 4, 64:128], scalar1=ZF, scalar2=None, op0=ALU.mult),
            reads=rq(g) + ["ZF"], writes=[("KZ", b3)])
        add("dve", lambda e: e.tensor_scalar(out=KZ[b3][:, :, 64:128], in0=RQ[:, 4 * g:4 * g + 4, 64:128], scalar1=ZB, scalar2=None, op0=ALU.mult),
            reads=rq(g) + ["ZB"], writes=[("KZ", b3)])

    def p1_mm(g):
        b3, pb = g % NBUF, g % 2
        UU = cx.bank(pb).rearrange("p (a b) -> p a b", a=4)
        for i in range(4):
            c = 4 * g + i
            add("pe", lambda e, c=c, i=i: e.matmul(out=UU[:, i, :], lhsT=KZ[b3][:, i, :], rhs=VV[:, c, 1, 0:128], start=True, stop=True),
                reads=[("KZ", b3), ("VV", c)], writes=[("UU", pb)], banks=[pb])

    def p1_ev(g):
        pb = g % 2
        UU = cx.bank(pb).rearrange("p (a b) -> p a b", a=4)
        add("act", lambda e: e.activation(out=B[:, 4 * g:4 * g + 4, :].rearrange("p a b -> p (a b)"), in_=UU.rearrange("p a b -> p (a b)"), func=AF.Copy),
            reads=[("UU", pb)], writes=[("B", c) for c in range(4 * g, 4 * g + 4)], banks=[pb])

    p1_kz(0)
    for g in range(NG):
        if g + 1 < NG:
            p1_kz(g + 1)
        p1_mm(g)
        p1_ev(g)
    CDf = SM[0:64, SM_CD:SM_CD + 1]
    CDb = SM[64:128, SM_CD:SM_CD + 1]
    for c in range(1, NT):
        add("dve", lambda e, c=c: e.scalar_tensor_tensor(out=B[0:64, c, :], in0=B[0:64, c - 1, :], scalar=CDf, in1=B[0:64, c, :], op0=ALU.mult, op1=ALU.add),
            reads=[("B", c - 1), ("B", c), "CDf"], writes=[("B", c)])
        cb = NT - 1 - c
        add("pool", lambda e, cb=cb: e.scalar_tensor_tensor(out=B[64:128, cb, :], in0=B[64:128, cb + 1, :], scalar=CDb, in1=B[64:128, cb, :], op0=ALU.mult, op1=ALU.add)
            if False else e.tensor_scalar(out=TMPB[64:128, :], in0=B[64:128, cb + 1, :], scalar1=CDb, scalar2=None, op0=ALU.mult),
            reads=[("Bb", cb + 1), ("Bev", cb + 1), "CDb"], writes=["TMPBb"]) if False else None
        add("dve", lambda e, cb=cb: e.scalar_tensor_tensor(out=B[64:128, cb, :], in0=B[64:128, cb + 1, :], scalar=CDb, in1=B[64:128, cb, :], op0=ALU.mult, op1=ALU.add),
            reads=[("Bb", cb + 1), ("Bb", cb), ("B", cb), ("B", cb + 1), "CDb"], writes=[("Bb", cb)])
    allB = [("B", c) for c in range(NT)] + [("Bb", c) for c in range(NT)]

    def s1(g):
        b3, pb = g % NBUF, g % 2
        TR = cx.bankb(2 + pb)[0:64, :].rearrange("p (a b c) -> p a b c", a=4, b=2)
        TX = cx.bankb(6 + pb)[:, 0:512].rearrange("p (a b) -> p a b", a=4)
        add("dve", lambda e: e.tensor_scalar(out=QX[b3][:, :, 0:64], in0=RQ[:, 4 * g:4 * g + 4, 0:64], scalar1=XIF, scalar2=None, op0=ALU.mult),
            reads=rq(g) + ["XIF"], writes=[("QX", b3)])
        add("dve", lambda e: e.tensor_scalar(out=QX[b3][:, :, 64:128], in0=RQ[:, 4 * g:4 * g + 4, 0:64], scalar1=XIB, scalar2=None, op0=ALU.mult),
            reads=rq(g) + ["XIB"], writes=[("QX", b3)])
        for i in range(4):
            c = 4 * g + i
            for j in range(2):
                add("pe", lambda e, c=c, i=i, j=j: e.transpose(out=TR[:, i, j, :], in_=RQ[:, c, j * 64:(j + 1) * 64], identity=ident),
                    reads=[("RQ", c), "CB"], writes=[("TR", pb)], banks=[2 + pb])
        for i in range(4):
            add("pe", lambda e, i=i: e.transpose(out=TX[:, i, :], in_=QX[b3][:, i, :], identity=ident), reads=[("QX", b3), "CB"], writes=[("TX", pb)], banks=[6 + pb])

    def s2(g):
        b3, pb = g % NBUF, g % 2
        TR = cx.bankb(2 + pb)[0:64, :].rearrange("p (a b c) -> p a b c", a=4, b=2)
        TX = cx.bankb(6 + pb)[:, 0:512].rearrange("p (a b) -> p a b", a=4)
        AT = cx.bank(4 + pb).rearrange("p (a b) -> p a b", a=4)
        add("act", lambda e: e.activation(out=RT[pb].rearrange("p a b c -> p (a b c)"), in_=TR.rearrange("p a b c -> p (a b c)"), func=AF.Copy),
            reads=[("TR", pb)], writes=[("RT", pb)], banks=[2 + pb])
        add("act", lambda e: e.activation(out=QXT[b3].rearrange("p a b -> p (a b)"), in_=TX.rearrange("p a b -> p (a b)"), func=AF.Copy),
            reads=[("TX", pb)], writes=[("QXT", b3)], banks=[6 + pb])
        c0 = 4 * g
        if g == 0:
            add("pool", lambda e: e.memset(SSg[b3][0:64, 0, :], 0.0), writes=[("SSg", b3)])
            add("dve", lambda e: e.tensor_copy(out=SSg[b3][0:64, 1:4, :], in_=B[0:64, 0:3, :]), reads=allB, writes=[("SSg", b3)])
        else:
            add("dve", lambda e: e.tensor_copy(out=SSg[b3][0:64, :, :], in_=B[0:64, c0 - 1:c0 + 3, :]), reads=allB, writes=[("SSg", b3)])
        if g == NG - 1:
            add("pool", lambda e: e.memset(SSg[b3][64:128, 3, :], 0.0), writes=[("SSg", b3)])
            add("dve", lambda e: e.tensor_copy(out=SSg[b3][64:128, 0:3, :], in_=B[64:128, c0 + 1:c0 + 4, :]), reads=allB, writes=[("SSg", b3)])
        else:
            add("dve", lambda e: e.tensor_copy(out=SSg[b3][64:128, :, :], in_=B[64:128, c0 + 1:c0 + 5, :]), reads=allB, writes=[("SSg", b3)])
        for i in range(4):
            add("pe", lambda e, i=i: e.matmul(out=AT[:, i, :], lhsT=RT[pb][:, i, 1, :], rhs=RT[pb][:, i, 0, :], start=True, stop=True),
                reads=[("RT", pb)], writes=[("AT", pb)], banks=[4 + pb])
        add("dve", lambda e: e.tensor_tensor(out=WT[b3], in0=AT, in1=DC.unsqueeze(1).to_broadcast([128, 4, 128]), op=ALU.mult),
            reads=[("AT", pb), "DC"], writes=[("WT", b3)], banks=[4 + pb])

    def s3(g):
        b3, pb = g % NBUF, g % 2
        OUTb = cx.bank(pb).rearrange("p (a b) -> p a b", a=4)
        for i in range(4):
            c = 4 * g + i
            add("pe", lambda e, i=i, c=c: e.matmul(out=OUTb[:, i, :], lhsT=WT[b3][:, i, :], rhs=VV[:, c, 1, 0:128], start=True, stop=False),
                reads=[("WT", b3), ("VV", c)], writes=[("OUT", pb)], banks=[pb])
            add("pe", lambda e, i=i, c=c: e.matmul(out=OUTb[:, i, :], lhsT=QXT[b3][:, i, :], rhs=SSg[b3][:, i, :], start=False, stop=True),
                reads=[("QXT", b3), ("SSg", b3)], writes=[("OUT", pb)], banks=[pb])
        add("act", lambda e: e.activation(out=SQ2[pb], in_=OUTb.rearrange("p a b -> p (a b)"), func=AF.Square), reads=[("OUT", pb)], writes=[("SQ2", 0)], banks=[pb])
        add("dve", lambda e: e.reduce_sum(out=R4[b3][:, 0:4], in_=SQ2[pb].rearrange("p (a b) -> p a b", a=4), axis=AX.X), reads=[("SQ2", 0)], writes=[("R4a", b3)])
        add("dve", lambda e: e.tensor_scalar(out=R4[b3][:, 0:4], in0=R4[b3][:, 0:4], scalar1=1.0 / 128, scalar2=EPS, op0=ALU.mult, op1=ALU.add),
            reads=[("R4a", b3)], writes=[("R4a", b3)])
        add("pool", lambda e: e.tensor_tensor(out=R4[b3][:, 4:8], in0=R4[b3][:, 0:4], in1=MH, op=ALU.pow), reads=[("R4a", b3), "SMmh"], writes=[("R4b", b3)])

    def s4(g):
        b3, pb = g % NBUF, g % 2
        OUTb = cx.bank(pb).rearrange("p (a b) -> p a b", a=4)
        for i in range(4):
            c = 4 * g + i
            add("dve", lambda e, i=i, c=c: e.scalar_tensor_tensor(out=G[:, c, 128:256], in0=OUTb[:, i, :], scalar=R4[b3][:, 4 + i:5 + i],
                                                                 in1=G[:, c, 128:256], op0=ALU.mult, op1=ALU.mult),
                reads=[("OUT", pb), ("R4b", b3), ("G", c)], writes=[("G", c)], banks=[pb])

    for s in range(NG + 3):
        if s < NG:
            s1(s)
        if 0 <= s - 1 < NG:
            s2(s - 1)
        if 0 <= s - 3 < NG:
            s4(s - 3)
        if 0 <= s - 2 < NG:
            s3(s - 2)
    A.pop()
    P.barrier()

    A.push()
    PT = [A.alloc([2, 512], BF16) for _ in range(2)]
    OS = A.alloc([3, 512], F32)
    E1 = [A.alloc([128], F32) for _ in range(2)]
    E2 = [A.alloc([128], F32) for _ in range(2)]
    ES = [A.alloc([8], F32) for _ in range(2)]
    Tt = CF[:, CF_T:CF_T + 128]
    TN = CF[:, CF_TN:CF_TN + 128]

    def acc_ap(a, base):
        bnk = 4 + a // 3
        o = (a % 3) * 129
        return base[:, (bnk - 4) * 512 + o:(bnk - 4) * 512 + o + 129]

    OB = cx.bank(4, 3)
    it = 0
    def keep(qc, kb):
        if slope is None:
            return True
        md = max(0, kb * 128 - (qc * 512 + 511), qc * 512 - (kb * 128 + 127))
        return slope * md <= SKIP_ARG

    steps = [(qc, kb) for qc in range(NQC) for kb in range(NT) if keep(qc, kb)]
    first_kb = {qc: min(kb for (q_, kb) in steps if q_ == qc) for qc in range(NQC)}
    last_kb = {qc: max(kb for (q_, kb) in steps if q_ == qc) for qc in range(NQC)}

    def emit_qk(qc, kb, sp):
        Sv = cx.bank(2 * sp, 2).rearrange("p (a b) -> p a b", a=2)
        rel = kb - 4 * qc
        qk_reads = ["QKTaug"] + [("QKT", kb)] + [("QKT", 4 * qc + i) for i in range(4)]
        if rel < 0 or rel >= 4:
            Kw = 66 if rel < 0 else 68
            for m in range(2):
                add("pe", lambda e, m=m, Kw=Kw, Sv=Sv: e.matmul(out=Sv[:, m, :], lhsT=QKT[0:Kw, 2 + m, kb * 128:(kb + 1) * 128],
                                                               rhs=QKT[0:Kw, m, qc * 512:(qc + 1) * 512], start=True, stop=True),
                    reads=qk_reads, writes=[("S", sp)], banks=[2 * sp, 2 * sp + 1])
            bias = (Tt if rel < 0 else TN)[:, rel + 64:rel + 65]
            add("act", lambda e, Sv=Sv, bias=bias, sp=sp: e.activation(out=PT[sp].rearrange("p a b -> p (a b)"), in_=Sv.rearrange("p a b -> p (a b)"),
                                                                      func=AF.Exp, bias=bias, scale=1.0),
                reads=[("S", sp), "CF"], writes=[("PT", sp)], banks=[2 * sp, 2 * sp + 1])
        else:
            for m in range(2):
                for qs in range(4):
                    d = rel - qs
                    q0 = qc * 512 + qs * 128
                    if d == 0:
                        add("pe", lambda e, m=m, qs=qs, q0=q0, Sv=Sv: e.matmul(out=Sv[:, m, qs * 128:(qs + 1) * 128], lhsT=QKT[0:64, 2 + m, kb * 128:(kb + 1) * 128],
                                                                              rhs=QKT[0:64, m, q0:q0 + 128], start=True, stop=False),
                            reads=qk_reads, writes=[("S", sp)], banks=[2 * sp, 2 * sp + 1])
                        add("pe", lambda e, m=m, qs=qs, Sv=Sv: e.matmul(out=Sv[:, m, qs * 128:(qs + 1) * 128], lhsT=ident, rhs=Bm, start=False, stop=True),
                            reads=["CB"], writes=[("S", sp)], banks=[2 * sp, 2 * sp + 1])
                    else:
                        Kw = 68 if d > 0 else 66
                        add("pe", lambda e, m=m, qs=qs, q0=q0, Kw=Kw, Sv=Sv: e.matmul(out=Sv[:, m, qs * 128:(qs + 1) * 128], lhsT=QKT[0:Kw, 2 + m, kb * 128:(kb + 1) * 128],
                                                                                     rhs=QKT[0:Kw, m, q0:q0 + 128], start=True, stop=True),
                            reads=qk_reads, writes=[("S", sp)], banks=[2 * sp, 2 * sp + 1])
            for qs in range(4):
                d = rel - qs
                bias = ZERO if d == 0 else (TN if d > 0 else Tt)[:, rel + 64:rel + 65]
                add("act", lambda e, Sv=Sv, bias=bias, sp=sp, qs=qs: e.activation(out=PT[sp][:, :, qs * 128:(qs + 1) * 128], in_=Sv[:, :, qs * 128:(qs + 1) * 128],
                                                                                 func=AF.Exp, bias=bias, scale=1.0),
                    reads=[("S", sp), "CF", "SMz"], writes=[("PT", sp)], banks=[2 * sp, 2 * sp + 1])

    def emit_pv(qc, kb, sp):
        for a in range(8):
            qs, m = a // 2, a % 2
            add("pe", lambda e, a=a, qs=qs, m=m: e.matmul(out=acc_ap(a, OB), lhsT=PT[sp][:, m, qs * 128:(qs + 1) * 128], rhs=VV[:, kb, 0, 0:129],
                                                         start=(kb == first_kb[qc] and a % 3 == 0), stop=(kb == last_kb[qc]), skip_group_check=True),
                reads=[("PT", sp), ("VV", kb), "VVone"], writes=["OB"], banks=[4, 5, 6])

    def emit_epi(qc):
        for j in range(3):
            add("dve", lambda e, j=j: e.tensor_copy(out=OS[:, j, 0:387], in_=OB[:, j * 512:j * 512 + 387]), reads=["OB"], writes=["OS"], banks=[4, 5, 6])
        OSf = OS.rearrange("p a b -> p (a b)")
        for qs in range(4):
            t = 4 * qc + qs
            pb = qs % 2
            o1 = acc_ap(2 * qs, OSf)
            o2 = acc_ap(2 * qs + 1, OSf)
            es = ES[pb]
            add("dve", lambda e, o1=o1, es=es: e.reciprocal(out=es[:, 0:1], in_=o1[:, 128:129]), reads=["OS"], writes=[("ESa", pb)])
            add("dve", lambda e, o2=o2, es=es: e.reciprocal(out=es[:, 1:2], in_=o2[:, 128:129]), reads=["OS"], writes=[("ESb", pb)])
            add("dve", lambda e, es=es: e.tensor_scalar(out=es[:, 2:3], in0=es[:, 1:2], scalar1=NEGLAM, scalar2=None, op0=ALU.mult),
                reads=[("ESb", pb), "NEGLAM"], writes=[("ESc", pb)])
            add("dve", lambda e, o1=o1, es=es, pb=pb: e.tensor_scalar(out=E1[pb], in0=o1[:, 0:128], scalar1=es[:, 0:1], scalar2=None, op0=ALU.mult),
                reads=["OS", ("ESa", pb)], writes=[("E1", pb)])
            add("dve", lambda e, o2=o2, es=es, pb=pb: e.scalar_tensor_tensor(out=E1[pb], in0=o2[:, 0:128], scalar=es[:, 2:3], in1=E1[pb], op0=ALU.mult, op1=ALU.add),
                reads=["OS", ("ESc", pb), ("E1", pb)], writes=[("E1", pb)])
            add("dve", lambda e, es=es, pb=pb: e.scalar_tensor_tensor(out=E2[pb], in0=E1[pb], scalar=1.0, in1=E1[pb], op0=ALU.mult, op1=ALU.mult, accum_out=es[:, 3:4]),
                reads=[("E1", pb)], writes=[("E2", pb), ("ESd", pb)])
            add("dve", lambda e, es=es: e.tensor_scalar(out=es[:, 3:4], in0=es[:, 3:4], scalar1=1.0 / 128, scalar2=EPS, op0=ALU.mult, op1=ALU.add),
                reads=[("ESd", pb)], writes=[("ESd", pb)])
            add("pool", lambda e, es=es: e.tensor_tensor(out=es[:, 4:5], in0=es[:, 3:4], in1=MH[:, 0:1], op=ALU.pow),
                reads=[("ESd", pb), "SMmh"], writes=[("ESe", pb)])
            add("dve", lambda e, es=es, pb=pb: e.scalar_tensor_tensor(out=E2[pb], in0=E1[pb], scalar=es[:, 4:5], in1=SUBW, op0=ALU.mult, op1=ALU.mult),
                reads=[("E1", pb), ("ESe", pb), "SUBW", ("E2", pb)], writes=[("E2", pb)])
            add("dve", lambda e, t=t, pb=pb: e.tensor_tensor(out=G[:, t, 0:128], in0=E2[pb], in1=G[:, t, 0:128], op=ALU.mult),
                reads=[("E2", pb), ("G", t)], writes=[("G", t)])
        mv = io["mix"].rearrange("(t p) c -> p t c", p=128)
        add("sync", lambda e, qc=qc: e.dma_start(out=mv[:, 4 * qc:4 * qc + 4, :], in_=G[:, 4 * qc:4 * qc + 4, :]),
            reads=[("G", 4 * qc + i) for i in range(4)], slot=("mixout", qc % 2))

    n = len(steps)
    emit_qk(steps[0][0], steps[0][1], 0)
    for i, (qc, kb) in enumerate(steps):
        sp = i % 2
        if i + 1 < n:
            emit_qk(steps[i + 1][0], steps[i + 1][1], (i + 1) % 2)
        emit_pv(qc, kb, sp)
        if kb == last_kb[qc]:
            emit_epi(qc)
    A.pop()


def declare_B_io(nc, NT):
    S = NT * 128
    io = {}
    io["xn"] = nc.dram_tensor("xn", [S, 1024], BF16, kind="ExternalInput").ap()
    io["w"] = nc.dram_tensor("w", [1024, 896], F32, kind="ExternalInput").ap()
    io["normw"] = nc.dram_tensor("normw", [128, 8], F32, kind="ExternalInput").ap()
    io["vec64"] = nc.dram_tensor("vec64", [1, 6 * 64], F32, kind="ExternalInput").ap()
    io["subln"] = nc.dram_tensor("subln", [1, 128], F32, kind="ExternalInput").ap()
    io["scal"] = nc.dram_tensor("scal", [1, 4], F32, kind="ExternalInput").ap()
    io["cpackf"] = nc.dram_tensor("cpackf", [128, CF_N], F32, kind="ExternalInput").ap()
    io["cpackb"] = nc.dram_tensor("cpackb", [128, 256], BF16, kind="ExternalInput").ap()
    io["aug"] = nc.dram_tensor("aug", [4, 4, S], BF16, kind="ExternalInput").ap()
    io["mix"] = nc.dram_tensor("mix", [S, 256], BF16, kind="ExternalOutput").ap()
    return io


def build_B(NT):
    nc = bass.Bass("TRN2", target_bir_lowering=False)
    io = declare_B_io(nc, NT)
    with contextlib.ExitStack() as st:
        cx = Ctx(nc, st)
        emit_B(cx, io, NT)
        cx.P.emit(nc)
    return nc


def alibi_slope(h):
    return 2.0 ** (-8.0 * (h + 1) / 4)


def const_tables(hh, S):
    s = alibi_slope(hh)
    p = np.arange(128, dtype=np.float64)
    cf = np.zeros((128, CF_N), np.float32)
    tt = p[None, :] - p[:, None]
    cf[:, CF_M1:CF_M1 + 128] = np.maximum(tt, 0)
    cf[:, CF_M2:CF_M2 + 128] = np.maximum(-tt, 0)
    u = np.arange(128, dtype=np.float64)
    T = s * (p[:, None] + 128.0 * (u[None, :] - 64))
    cf[:, CF_T:CF_T + 128] = T
    cf[:, CF_TN:CF_TN + 128] = -T
    cf[:, CF_C5 + 0] = p + 1
    cf[:, CF_C5 + 1] = 128 - p
    cf[:, CF_C5 + 2] = 127 - p
    cf[:, CF_C5 + 3] = p
    cf[:, CF_C5 + 4] = 128
    cb = np.zeros((128, 256), np.float32)
    cb[:, 0:128] = np.eye(128)
    cb[:, 128:256] = -s * np.abs(tt)
    r = np.arange(S) % 512
    rlo = (r & 255).astype(np.float64)
    rhi = (r - (r & 255)).astype(np.float64)
    qa = np.stack([-s * rlo, -s * rhi, -2 * s * rlo, -2 * s * rhi])
    ka = np.stack([np.ones(S), np.ones(S), -np.ones(S), -np.ones(S)])
    aug = np.stack([qa, qa, ka, ka], axis=1)
    return cf, cb.astype(ml_dtypes.bfloat16), aug.astype(ml_dtypes.bfloat16)


def w_in_cols(hh):
    qa = [hh * 128 + m * 64 + j for m in range(2) for j in range(64)]
    ka = [512 + c for c in qa]
    va = [1024 + hh * 128 + j for j in range(128)]
    ga = [1536 + hh * 128 + j for j in range(128)]
    qr = [2048 + hh * 64 + j for j in range(64)]
    kr = [2304 + hh * 64 + j for j in range(64)]
    vr = [2560 + hh * 128 + j for j in range(128)]
    gr = [3072 + hh * 128 + j for j in range(128)]
    return np.array(qa + ka + qr + kr + va + vr + ga + gr)


def B_inputs(inputs, l, hh, xn_b, S):
    cf, cb, aug = const_tables(hh, S)
    lam_init = 0.8 - 0.6 * math.exp(-0.3 * l)
    f32 = np.float32
    vec = np.stack([inputs["q_norm_w"][l], inputs["k_norm_w"][l], inputs["lambda_q1"][l], inputs["lambda_k1"][l],
                    inputs["lambda_q2"][l], inputs["lambda_k2"][l]]).astype(f32).reshape(1, 384)
    scal = np.array([[inputs["ret_decay_fwd"][l, hh], inputs["ret_decay_bwd"][l, hh], lam_init, 1.0 - lam_init]], f32)
    return dict(
        xn=xn_b,
        w=np.ascontiguousarray(inputs["w_in"][l][:, w_in_cols(hh)]).astype(f32),
        normw=np.ascontiguousarray(inputs["norm_w"][l].reshape(8, 128).T).astype(f32),
        vec64=vec,
        subln=inputs["subln_w"][l].astype(f32).reshape(1, 128),
        scal=scal,
        cpackf=cf, cpackb=cb, aug=aug,
    )


def emit_norm_tile(cx, src_f32, dst_bf16, junk, st, key, MH):
    add = cx.P.add
    add("act", lambda e: e.activation(out=junk, in_=src_f32, func=AF.Square, accum_out=st[:, 0:1]), reads=[key], writes=[("nj", id(junk)), ("st0", id(st))])
    add("dve", lambda e: e.tensor_scalar(out=st[:, 1:2], in0=st[:, 0:1], scalar1=1.0 / D_MODEL, scalar2=EPS, op0=ALU.mult, op1=ALU.add),
        reads=[("st0", id(st))], writes=[("st1", id(st))])
    add("pool", lambda e: e.tensor_tensor(out=st[:, 2:3], in0=st[:, 1:2], in1=MH, op=ALU.pow), reads=[("st1", id(st)), "MHc"], writes=[("st2", id(st))])
    add("dve", lambda e: e.tensor_scalar(out=dst_bf16, in0=src_f32, scalar1=st[:, 2:3], scalar2=None, op0=ALU.mult),
        reads=[key, ("st2", id(st))], writes=[("xb", id(dst_bf16))])


def emit_A(cx, io, NTQ):
    P, A = cx.P, cx.A
    add = P.add
    A.push()
    XF = [A.alloc([1024], F32) for _ in range(2)]
    XB = [A.alloc([1024], BF16) for _ in range(2)]
    JK = [A.alloc([1024], BF16) for _ in range(2)]
    ST = [A.alloc([4], F32) for _ in range(2)]
    MHc = A.alloc([1], F32)
    add("pool", lambda e: e.memset(MHc, -0.5), writes=["MHc"])
    xv = io["x"].rearrange("(t p) d -> t p d", p=128)
    ov = io["xn_out"].rearrange("(t p) d -> t p d", p=128)
    for t in range(NTQ):
        pb = t % 2
        add("sync", lambda e, t=t, pb=pb: e.dma_start(out=XF[pb], in_=xv[t]), writes=[("XF", pb)], slot=("xf", pb))
        emit_norm_tile(cx, XF[pb], XB[pb], JK[pb], ST[pb], ("XF", pb), MHc)
        add("sync", lambda e, t=t, pb=pb: e.dma_start(out=ov[t], in_=XB[pb]), reads=[("xb", id(XB[pb]))], slot=("xno", pb))
    A.pop()
    P.barrier()


def build_A(NTQ):
    nc = bass.Bass("TRN2", target_bir_lowering=False)
    io = {}
    io["x"] = nc.dram_tensor("x", [NTQ * 128, 1024], F32, kind="ExternalInput").ap()
    io["xn_out"] = nc.dram_tensor("xn_out", [NTQ * 128, 1024], BF16, kind="ExternalOutput").ap()
    with contextlib.ExitStack() as st:
        cx = Ctx(nc, st)
        emit_A(cx, io, NTQ)
        cx.P.emit(nc)
    return nc


def emit_C(cx, io, NTQ, with_norm):
    P, A = cx.P, cx.A
    add = P.add
    A.push()
    WO = A.alloc([8, 1024], BF16)
    WST = [A.alloc([1024], F32) for _ in range(2)]
    IDN = A.alloc([128], BF16)
    MX = [A.alloc([1024], BF16) for _ in range(2)]
    MT = [A.alloc([8, 128], BF16) for _ in range(2)]
    XR = [A.alloc([1024], F32) for _ in range(2)]
    XO = [A.alloc([1024], F32) for _ in range(2)]
    XB = [A.alloc([1024], BF16) for _ in range(2)]
    JK = [A.alloc([1024], BF16) for _ in range(2)]
    ST = [A.alloc([4], F32) for _ in range(2)]
    MHc = A.alloc([1], F32)
    add("pool", lambda e: e.memset(MHc, -0.5), writes=["MHc"])
    add("sync", lambda e: e.dma_start(out=IDN, in_=io["ident"]), writes=["IDN"], slot="ldi")
    wv = io["wout"].rearrange("(k p) n -> k p n", p=128)
    for kc in range(8):
        b_ = kc % 2
        add("sync", lambda e, kc=kc, b_=b_: e.dma_start(out=WST[b_], in_=wv[kc]), writes=[("WSTo", b_)], slot=("wsto", b_))
        add("dve" if kc % 2 else "pool", lambda e, kc=kc, b_=b_: e.tensor_copy(out=WO[:, kc, :], in_=WST[b_]), reads=[("WSTo", b_)], writes=["WO"])
    if "mix_hm" in io:
        mvh = io["mix_hm"].rearrange("h (t p) c -> t p h c", p=128)
        mv = [mvh[t] for t in range(NTQ)]
        MXv = [m.rearrange("p (h c) -> p h c", h=4) for m in MX]
    else:
        mvf = io["mixq"].rearrange("(t p) d -> t p d", p=128)
        mv = [mvf[t] for t in range(NTQ)]
        MXv = MX
    xv = io["xres"].rearrange("(t p) d -> t p d", p=128)
    ov = io["xnew"].rearrange("(t p) d -> t p d", p=128)
    if with_norm:
        nv = io["xn_out"].rearrange("(t p) d -> t p d", p=128)
    for t in range(NTQ):
        pb = t % 2
        TPb = cx.bankb(pb)
        ACC = cx.bank(2 + 2 * pb, 2)
        add("sync", lambda e, t=t, pb=pb: e.dma_start(out=MXv[pb], in_=mv[t]), writes=[("MX", pb)], slot=("mx", pb))
        add("sync", lambda e, t=t, pb=pb: e.dma_start(out=XR[pb], in_=xv[t]), writes=[("XR", pb)], slot=("xr", pb))
        for kc in range(8):
            add("pe", lambda e, pb=pb, kc=kc, TPb=TPb: e.transpose(out=TPb[:, kc * 128:(kc + 1) * 128], in_=MX[pb][:, kc * 128:(kc + 1) * 128], identity=IDN),
                reads=[("MX", pb), "IDN"], writes=[("TPc", pb)], banks=[pb])
        add("act", lambda e, pb=pb, TPb=TPb: e.activation(out=MT[pb].rearrange("p a b -> p (a b)"), in_=TPb, func=AF.Copy),
            reads=[("TPc", pb)], writes=[("MT", pb)], banks=[pb])
        for hf in range(2):
            for kc in range(8):
                add("pe", lambda e, pb=pb, kc=kc, hf=hf, ACC=ACC: e.matmul(out=ACC[:, hf * 512:(hf + 1) * 512], lhsT=MT[pb][:, kc, :], rhs=WO[:, kc, hf * 512:(hf + 1) * 512],
                                                                          start=(kc == 0), stop=(kc == 7)),
                    reads=[("MT", pb), "WO"], writes=[("ACC", pb)], banks=[2 + 2 * pb, 3 + 2 * pb])
        add("dve", lambda e, pb=pb, ACC=ACC: e.tensor_tensor(out=XO[pb], in0=ACC, in1=XR[pb], op=ALU.add),
            reads=[("ACC", pb), ("XR", pb)], writes=[("XO", pb)], banks=[2 + 2 * pb, 3 + 2 * pb])
        add("sync", lambda e, t=t, pb=pb: e.dma_start(out=ov[t], in_=XO[pb]), reads=[("XO", pb)], slot=("xo", pb))
        if with_norm:
            emit_norm_tile(cx, XO[pb], XB[pb], JK[pb], ST[pb], ("XO", pb), MHc)
            add("sync", lambda e, t=t, pb=pb: e.dma_start(out=nv[t], in_=XB[pb]), reads=[("xb", id(XB[pb]))], slot=("xnc", pb))
    A.pop()
    P.barrier()


def build_C(NTQ, with_norm):
    nc = bass.Bass("TRN2", target_bir_lowering=False)
    io = {}
    io["mixq"] = nc.dram_tensor("mixq", [NTQ * 128, 1024], BF16, kind="ExternalInput").ap()
    io["xres"] = nc.dram_tensor("xres", [NTQ * 128, 1024], F32, kind="ExternalInput").ap()
    io["wout"] = nc.dram_tensor("wout", [1024, 1024], F32, kind="ExternalInput").ap()
    io["ident"] = nc.dram_tensor("ident", [128, 128], BF16, kind="ExternalInput").ap()
    io["xnew"] = nc.dram_tensor("xnew", [NTQ * 128, 1024], F32, kind="ExternalOutput").ap()
    if with_norm:
        io["xn_out"] = nc.dram_tensor("xn_out", [NTQ * 128, 1024], BF16, kind="ExternalOutput").ap()
    with contextlib.ExitStack() as st:
        cx = Ctx(nc, st)
        emit_C(cx, io, NTQ, with_norm)
        cx.P.emit(nc)
    return nc


def wout_rows():
    rows = []
    for h in range(4):
        rows += list(range(h * 128, (h + 1) * 128)) + list(range(512 + h * 128, 512 + (h + 1) * 128))
    return np.array(rows)


_CACHE = {}


def _get(name, fn):
    if name not in _CACHE:
        _CACHE[name] = fn()
    return _CACHE[name]


def kernel_unfused(**inputs):
    inputs = {k: np.asarray(v) for k, v in inputs.items()}
    x = inputs["x"]
    Bsz, S, Dm = x.shape
    NT = S // 128
    NTQ = NT // 4
    TQ = S // 4
    cores = list(range(8))
    ident = np.eye(128, dtype=np.float32).astype(ml_dtypes.bfloat16)
    ncA = _get("A", lambda: build_A(NTQ))
    res = run_bass_kernel_spmd(ncA, [dict(x=np.ascontiguousarray(x[c // 4, (c % 4) * TQ:(c % 4 + 1) * TQ])) for c in cores], core_ids=cores)
    xn_q = [r["xn_out"] for r in res.results]
    xcur = [np.ascontiguousarray(x[c // 4, (c % 4) * TQ:(c % 4 + 1) * TQ]) for c in cores]
    for l in range(2):
        xn_b = [np.concatenate([xn_q[b * 4 + j] for j in range(4)], axis=0) for b in range(Bsz)]
        ncB = _get("B", lambda: build_B(NT))
        res = run_bass_kernel_spmd(ncB, [B_inputs(inputs, l, c % 4, xn_b[c // 4], S) for c in cores], core_ids=cores)
        mix = [r["mix"] for r in res.results]
        wout = np.ascontiguousarray(inputs["w_out"][l][wout_rows()]).astype(np.float32)
        last = (l == 1)
        ncC = _get("C%d" % l, lambda: build_C(NTQ, not last))
        ins = []
        for c in cores:
            b, j = c // 4, c % 4
            mixq = np.concatenate([mix[b * 4 + h][j * TQ:(j + 1) * TQ] for h in range(4)], axis=1)
            ins.append(dict(mixq=np.ascontiguousarray(mixq), xres=xcur[c], wout=wout, ident=ident))
        res = run_bass_kernel_spmd(ncC, ins, core_ids=cores)
        xcur = [r["xnew"] for r in res.results]
        if not last:
            xn_q = [r["xn_out"] for r in res.results]
    out = np.stack([np.concatenate([xcur[b * 4 + j] for j in range(4)], axis=0) for b in range(Bsz)], axis=0)
    return out.astype(np.float32)


def build_fused(NT):
    S = NT * 128
    nc = bass.Bass("TRN2", target_bir_lowering=False)
    dt = nc.dram_tensor
    x = dt("x", [S, 1024], F32, kind="ExternalInput").ap()
    w_all = dt("w_all", [2, 4, 1024, 896], F32, kind="ExternalInput").ap()
    normw = dt("normw", [2, 128, 8], F32, kind="ExternalInput").ap()
    vec64 = dt("vec64", [2, 1, 384], F32, kind="ExternalInput").ap()
    subln = dt("subln", [2, 1, 128], F32, kind="ExternalInput").ap()
    scal = dt("scal", [2, 4, 1, 4], F32, kind="ExternalInput").ap()
    cpackf = dt("cpackf", [4, 128, CF_N], F32, kind="ExternalInput").ap()
    cpackb = dt("cpackb", [4, 128, 256], BF16, kind="ExternalInput").ap()
    aug = dt("aug", [4, 4, 4, S], BF16, kind="ExternalInput").ap()
    wout = dt("wout", [2, 1024, 1024], F32, kind="ExternalInput").ap()
    identd = dt("ident", [128, 128], BF16, kind="ExternalInput").ap()
    out = dt("out", [S, 1024], F32, kind="ExternalOutput").ap()
    XNs = dt("xn_scratch", [S, 1024], BF16, kind="Internal").ap()
    MIXs = dt("mix_scratch", [4, S, 256], BF16, kind="Internal").ap()
    X1 = dt("x1_scratch", [S, 1024], F32, kind="Internal").ap()
    with contextlib.ExitStack() as st:
        cx = Ctx(nc, st)
        emit_A(cx, dict(x=x, xn_out=XNs), NT)
        for l in range(2):
            for h in range(4):
                emit_B(cx, dict(xn=XNs, w=w_all[l, h], normw=normw[l], vec64=vec64[l], subln=subln[l], scal=scal[l, h],
                                cpackf=cpackf[h], cpackb=cpackb[h], aug=aug[h], mix=MIXs[h]), NT, slope=alibi_slope(h))
            ioC = dict(mix_hm=MIXs, xres=(x if l == 0 else X1), wout=wout[l], ident=identd, xnew=(X1 if l == 0 else out))
            if l == 0:
                ioC["xn_out"] = XNs
            emit_C(cx, ioC, NT, with_norm=(l == 0))
        cx.P.emit(nc)
    print("[build_fused] ops", len(cx.P.ops), "sem counts", {str(k): v for k, v in cx.P.sem_counts.items() if k[0] == "eng"}, flush=True)
    return nc


def fused_inputs(inputs, b, S):
    f32 = np.float32
    tabs = [const_tables(h, S) for h in range(4)]
    lam_init = [0.8 - 0.6 * math.exp(-0.3 * l) for l in range(2)]
    vec = np.stack([np.stack([inputs["q_norm_w"][l], inputs["k_norm_w"][l], inputs["lambda_q1"][l], inputs["lambda_k1"][l],
                              inputs["lambda_q2"][l], inputs["lambda_k2"][l]]).reshape(1, 384) for l in range(2)]).astype(f32)
    scal = np.array([[[[inputs["ret_decay_fwd"][l, h], inputs["ret_decay_bwd"][l, h], lam_init[l], 1.0 - lam_init[l]]] for h in range(4)] for l in range(2)], f32)
    return dict(
        x=np.ascontiguousarray(inputs["x"][b]).astype(f32),
        w_all=np.stack([np.stack([inputs["w_in"][l][:, w_in_cols(h)] for h in range(4)]) for l in range(2)]).astype(f32),
        normw=np.stack([inputs["norm_w"][l].reshape(8, 128).T for l in range(2)]).astype(f32),
        vec64=vec,
        subln=np.stack([inputs["subln_w"][l].reshape(1, 128) for l in range(2)]).astype(f32),
        scal=scal,
        cpackf=np.stack([t[0] for t in tabs]), cpackb=np.stack([t[1] for t in tabs]), aug=np.stack([t[2] for t in tabs]),
        wout=np.stack([inputs["w_out"][l][wout_rows()] for l in range(2)]).astype(f32),
        ident=np.eye(128, dtype=f32).astype(ml_dtypes.bfloat16),
    )


def kernel_fused(**inputs):
    inputs = {k: np.asarray(v) for k, v in inputs.items()}
    Bsz, S, _ = inputs["x"].shape
    nc = _get("F", lambda: build_fused(S // 128))
    res = run_bass_kernel_spmd(nc, [fused_inputs(inputs, b, S) for b in range(Bsz)], core_ids=list(range(Bsz)))
    return np.stack([res.results[b]["out"] for b in range(Bsz)], axis=0).astype(np.float32)


def kernel(**inputs):
    return kernel_fused(**inputs)
```

```python
import contextlib
import math
import numpy as np
import ml_dtypes
import concourse.bass as bass
import concourse.mybir as mybir
from concourse.bass_utils import run_bass_kernel_spmd

F32 = mybir.dt.float32
BF16 = mybir.dt.bfloat16
AF = mybir.ActivationFunctionType
ALU = mybir.AluOpType
AX = mybir.AxisListType

SAME_ENGINE_SYNC = True
ENGS = ("sync", "act", "dve", "pool", "pe")
D_MODEL = 1024
EPS = 1e-6


class _Op:
    __slots__ = ("eng", "fn", "deps", "slot", "val", "waits", "idx")

    def __init__(self, eng, fn, deps, slot, idx):
        self.eng = eng
        self.fn = fn
        self.deps = deps
        self.slot = slot
        self.val = None
        self.waits = None
        self.idx = idx


class Prog:
    def __init__(self):
        self.ops = []
        self.last_w = {}
        self.readers = {}
        self.last_on_eng = {}
        self.last_on_slot = {}
        self.pending = {e: set() for e in ENGS}

    def add(self, eng, fn, reads=(), writes=(), slot=None, banks=()):
        i = len(self.ops)
        deps = set()
        if banks:
            writes = list(writes) + [("bank", b) for b in banks]
        for k in reads:
            w = self.last_w.get(k)
            if w is not None:
                deps.add(w)
        for k in writes:
            w = self.last_w.get(k)
            if w is not None:
                deps.add(w)
            for r in self.readers.get(k, ()):
                deps.add(r)
        for k in reads:
            self.readers.setdefault(k, []).append(i)
        for k in writes:
            self.last_w[k] = i
            self.readers[k] = []
        if self.pending[eng]:
            deps |= self.pending[eng]
            self.pending[eng] = set()
        if slot is not None:
            p = self.last_on_slot.get(slot)
            if p is not None:
                deps.add(p)
            self.last_on_slot[slot] = i
        else:
            self.last_on_eng[eng] = i
        deps.discard(i)
        self.ops.append(_Op(eng, fn, deps, slot, i))
        return i

    def barrier(self):
        allp = set(self.last_on_eng.values()) | set(self.last_on_slot.values())
        for e in ENGS:
            self.pending[e] |= allp

    def _semkey(self, op):
        return ("slot", op.slot) if op.slot is not None else ("eng", op.eng)

    def emit(self, nc, final_wait_eng="sync"):
        ops = self.ops
        self.barrier()
        self.add(final_wait_eng, None)
        waited = {e: {} for e in ENGS}
        need = set()
        for op in ops:
            ws = {}
            for d in op.deps:
                y = ops[d]
                sk = self._semkey(y)
                if y.slot is None and y.eng == op.eng:
                    if op.eng in ("pe", "sync"):
                        continue
                    if not SAME_ENGINE_SYNC:
                        continue
                if waited[op.eng].get(sk, -1) >= d:
                    continue
                if sk not in ws or ws[sk] < d:
                    ws[sk] = d
            for sk, d in ws.items():
                waited[op.eng][sk] = d
                need.add(d)
            op.waits = ws
        cnt = {}
        for op in ops:
            sk = self._semkey(op)
            if op.slot is not None:
                cnt[sk] = cnt.get(sk, 0) + 16
                op.val = cnt[sk]
            elif op.idx in need:
                cnt[sk] = cnt.get(sk, 0) + 1
                op.val = cnt[sk]
        self.sem_counts = dict(cnt)
        sems = {}
        with contextlib.ExitStack() as stack:
            for n_, sk in enumerate(cnt):
                sems[sk] = stack.enter_context(nc.semaphore("sem%d" % n_))
            block = stack.enter_context(nc.Block())

            def run(eng_name):
                def body(eng):
                    for op in ops:
                        if op.eng != eng_name:
                            continue
                        for sk, d in op.waits.items():
                            eng.wait_ge(sems[sk], ops[d].val)
                        if op.fn is None:
                            continue
                        ins = op.fn(eng)
                        if op.val is not None:
                            ins.then_inc(sems[self._semkey(op)], 16 if op.slot is not None else 1)
                return body

            block.sync(run("sync"))
            block.scalar(run("act"))
            block.vector(run("dve"))
            block.gpsimd(run("pool"))
            block.tensor(run("pe"))


class Arena:
    def __init__(self, base_ap_bf16, nbytes):
        self.base = base_ap_bf16
        self.nbytes = nbytes
        self.top = 0
        self.marks = []
        self.peak = 0

    def alloc(self, shape_free, dtype, parts=128, align=32):
        esz = 4 if dtype == F32 else 2
        n = int(np.prod(shape_free))
        off = (self.top + align - 1) // align * align
        nb = n * esz
        assert off + nb <= self.nbytes, ("arena overflow", off, nb, self.nbytes)
        self.top = off + nb
        self.peak = max(self.peak, self.top)
        ap = self.base[0:parts, off // 2:(off + nb) // 2]
        if dtype == F32:
            ap = ap.bitcast(F32)
        if len(shape_free) == 2:
            ap = ap.rearrange("p (a b) -> p a b", a=shape_free[0])
        elif len(shape_free) == 3:
            ap = ap.rearrange("p (a b c) -> p a b c", a=shape_free[0], b=shape_free[1])
        return ap

    def push(self):
        self.marks.append(self.top)

    def pop(self):
        self.top = self.marks.pop()


SBUF_BYTES = 212736


class Ctx:
    def __init__(self, nc, stack):
        self.nc = nc
        self.P = Prog()
        sb = stack.enter_context(nc.sbuf_tensor("arena", [128, SBUF_BYTES // 2], BF16))
        self.A = Arena(sb[:, :], SBUF_BYTES)
        ps = stack.enter_context(nc.psum_tensor("psum_all", [128, 4096], F32))
        self.PS = ps[:, :]

    def bank(self, i, n=1):
        return self.PS[:, i * 512:(i + n) * 512]

    def bankb(self, i, n=1):
        return self.PS[:, i * 512:(i + n) * 512].bitcast(BF16)


CF_M1, CF_M2, CF_T, CF_TN, CF_C5, CF_N = 0, 128, 256, 384, 512, 520
SM_LGF, SM_LGB, SM_XIF, SM_XIB, SM_ZF, SM_ZB, SM_CD, SM_NEGLAM, SM_ZERO, SM_MH = 0, 1, 2, 3, 4, 5, 6, 7, 8, 9


SKIP_ARG = 87.4 + 16.0


def emit_B(cx, io, NT, slope=None):
    P, A = cx.P, cx.A
    A.push()
    _emit_B(cx, io, NT, slope)
    A.pop()
    P.barrier()


def _emit_B(cx, io, NT, slope=None):
    P, A = cx.P, cx.A
    S = NT * 128
    NQC = NT // 4
    add = P.add

    QKT = A.alloc([4, S], BF16, parts=68)
    VV = A.alloc([NT, 2, 132], BF16)
    G = A.alloc([NT, 256], BF16)
    RQ = A.alloc([NT, 128], BF16)
    CF = A.alloc([CF_N], F32)
    CB = A.alloc([256], BF16)
    WQK = A.alloc([256], F32)
    CQK = A.alloc([128], F32)
    SUBW = A.alloc([128], F32)
    V6 = A.alloc([6, 64], F32)
    SC = A.alloc([4], F32)
    SM = A.alloc([16], F32)
    DC = A.alloc([128], F32)
    TMPA = A.alloc([128], F32)
    TMPB = A.alloc([128], F32)
    TS = A.alloc([8], F32)
    ident = CB[:, 0:128]
    Bm = CB[:, 128:256]

    add("sync", lambda e: e.dma_start(out=CF, in_=io["cpackf"]), writes=["CF"], slot="ld0")
    add("sync", lambda e: e.dma_start(out=CB, in_=io["cpackb"]), writes=["CB"], slot="ld1")
    add("sync", lambda e: e.dma_start(out=V6, in_=io["vec64"].partition_broadcast(128)), writes=["V6"], slot="ld2")
    add("sync", lambda e: e.dma_start(out=SUBW, in_=io["subln"].partition_broadcast(128)), writes=["SUBW"], slot="ld3")
    add("sync", lambda e: e.dma_start(out=SC, in_=io["scal"].partition_broadcast(128)), writes=["SC"], slot="ld4")
    add("sync", lambda e: e.dma_start(out=QKT[64:68, :, :], in_=io["aug"]), writes=["QKTaug"], slot="ld5")
    add("pool", lambda e: e.memset(VV[:, :, 0, 128:129], 1.0), writes=["VVone"])
    add("pool", lambda e: e.memset(CQK[:, 0:64], 1.0), writes=["CQK"])
    add("pool", lambda e: e.memset(CQK[:, 64:128], 0.125), writes=["CQK"])
    add("pool", lambda e: e.memset(SM[:, SM_ZERO:SM_ZERO + 1], 0.0), writes=["SMz"])
    add("pool", lambda e: e.memset(SM[:, SM_MH:SM_MH + 4], -0.5), writes=["SMmh"])
    MH = SM[:, SM_MH:SM_MH + 4]
    ZERO = SM[:, SM_ZERO:SM_ZERO + 1]
    add("dve", lambda e: e.tensor_scalar(out=WQK[:, 0:128].rearrange("p (a b) -> p a b", a=2),
                                         in0=V6[:, 0, :].unsqueeze(1).to_broadcast([128, 2, 64]),
                                         scalar1=0.125, scalar2=None, op0=ALU.mult), reads=["V6"], writes=["WQK"])
    add("dve", lambda e: e.tensor_scalar(out=WQK[:, 128:256].rearrange("p (a b) -> p a b", a=2),
                                         in0=V6[:, 1, :].unsqueeze(1).to_broadcast([128, 2, 64]),
                                         scalar1=1.0, scalar2=None, op0=ALU.mult), reads=["V6"], writes=["WQK"])
    add("dve", lambda e: e.tensor_tensor(out=TMPA[:, 0:64], in0=V6[:, 2, :], in1=V6[:, 3, :], op=ALU.mult), reads=["V6"], writes=["TMPA"])
    add("dve", lambda e: e.reduce_sum(out=TS[:, 0:1], in_=TMPA[:, 0:64], axis=AX.X), reads=["TMPA"], writes=["TS0"])
    add("dve", lambda e: e.tensor_tensor(out=TMPA[:, 64:128], in0=V6[:, 4, :], in1=V6[:, 5, :], op=ALU.mult), reads=["V6"], writes=["TMPA2"])
    add("dve", lambda e: e.reduce_sum(out=TS[:, 1:2], in_=TMPA[:, 64:128], axis=AX.X), reads=["TMPA2"], writes=["TS1"])
    add("act", lambda e: e.activation(out=TS[:, 2:4], in_=TS[:, 0:2], func=AF.Exp), reads=["TS0", "TS1"], writes=["TS23"])
    add("dve", lambda e: e.tensor_tensor(out=TS[:, 4:5], in0=TS[:, 2:3], in1=TS[:, 3:4], op=ALU.subtract), reads=["TS23"], writes=["TS4"])
    add("dve", lambda e: e.tensor_scalar(out=SM[:, SM_NEGLAM:SM_NEGLAM + 1], in0=TS[:, 4:5], scalar1=SC[:, 2:3], scalar2=-1.0,
                                         op0=ALU.add, op1=ALU.mult), reads=["TS4", "SC"], writes=["NEGLAM"])
    NEGLAM = SM[:, SM_NEGLAM:SM_NEGLAM + 1]
    add("dve", lambda e: e.tensor_scalar(out=SUBW, in0=SUBW, scalar1=SC[:, 3:4], scalar2=None, op0=ALU.mult), reads=["SUBW", "SC"], writes=["SUBW"])
    add("act", lambda e: e.activation(out=TS[:, 5:7], in_=SC[:, 0:2], func=AF.Exp, scale=-1.0), reads=["SC"], writes=["TS56"])
    add("act", lambda e: e.activation(out=TS[:, 5:7], in_=TS[:, 5:7], func=AF.Ln, bias=1.0, scale=1.0), reads=["TS56"], writes=["TS56"])
    add("dve", lambda e: e.tensor_scalar(out=SM[:, 0:2], in0=TS[:, 5:7], scalar1=-1.0, scalar2=None, op0=ALU.mult), reads=["TS56"], writes=["LG"])
    LGF, LGB = SM[:, 0:1], SM[:, 1:2]
    C5 = CF[:, CF_C5:CF_C5 + 5]
    add("act", lambda e: e.activation(out=SM[:, SM_XIF:SM_XIF + 1], in_=C5[:, 0:1], func=AF.Exp, scale=LGF), reads=["LG", "CF"], writes=["XIF"])
    add("act", lambda e: e.activation(out=SM[:, SM_XIB:SM_XIB + 1], in_=C5[:, 1:2], func=AF.Exp, scale=LGB), reads=["LG", "CF"], writes=["XIB"])
    add("act", lambda e: e.activation(out=SM[:, SM_ZF:SM_ZF + 1], in_=C5[:, 2:3], func=AF.Exp, scale=LGF), reads=["LG", "CF"], writes=["ZF"])
    add("act", lambda e: e.activation(out=SM[:, SM_ZB:SM_ZB + 1], in_=C5[:, 3:4], func=AF.Exp, scale=LGB), reads=["LG", "CF"], writes=["ZB"])
    add("act", lambda e: e.activation(out=SM[0:64, SM_CD:SM_CD + 1], in_=C5[0:64, 4:5], func=AF.Exp, scale=SM[0:64, 0:1]), reads=["LG", "CF"], writes=["CDf"])
    add("act", lambda e: e.activation(out=SM[64:128, SM_CD:SM_CD + 1], in_=C5[64:128, 4:5], func=AF.Exp, scale=SM[64:128, 1:2]), reads=["LG", "CF"], writes=["CDb"])
    XIF, XIB = SM[:, SM_XIF:SM_XIF + 1], SM[:, SM_XIB:SM_XIB + 1]
    ZF, ZB = SM[:, SM_ZF:SM_ZF + 1], SM[:, SM_ZB:SM_ZB + 1]
    add("dve", lambda e: e.tensor_scalar(out=TMPB, in0=CF[:, CF_M1:CF_M1 + 128], scalar1=LGF, scalar2=None, op0=ALU.mult), reads=["LG", "CF"], writes=["TMPB"])
    add("dve", lambda e: e.scalar_tensor_tensor(out=TMPB, in0=CF[:, CF_M2:CF_M2 + 128], scalar=LGB, in1=TMPB, op0=ALU.mult, op1=ALU.add), reads=["LG", "CF", "TMPB"], writes=["TMPB"])
    add("act", lambda e: e.activation(out=DC, in_=TMPB, func=AF.Exp), reads=["TMPB"], writes=["DC"])

    A.push()
    W = A.alloc([8, 896], BF16)
    WST = [A.alloc([896], F32) for _ in range(2)]
    NW = A.alloc([8], F32)
    XN = [A.alloc([1024], BF16) for _ in range(3)]
    XT = [A.alloc([8, 128], BF16) for _ in range(2)]
    QKF = [A.alloc([256], F32) for _ in range(4)]
    SQ = [A.alloc([256], F32) for _ in range(4)]
    QKN = [A.alloc([256], BF16) for _ in range(4)]
    S4 = [A.alloc([8], F32) for _ in range(4)]
    add("sync", lambda e: e.dma_start(out=NW, in_=io["normw"]), writes=["NW"], slot="ld6")
    wv = io["w"].rearrange("(k p) n -> k p n", p=128)
    for kc in range(8):
        b_ = kc % 2
        add("sync", lambda e, kc=kc, b_=b_: e.dma_start(out=WST[b_], in_=wv[kc]), writes=[("WST", b_)], slot=("wst", b_))
        add("dve", lambda e, kc=kc, b_=b_: e.tensor_scalar(out=W[:, kc, :], in0=WST[b_], scalar1=NW[:, kc:kc + 1], scalar2=None, op0=ALU.mult),
            reads=[("WST", b_), "NW"], writes=["W"])
    xv = io["xn"].rearrange("(t p) d -> t p d", p=128)

    def HAb(t):
        return cx.bank(2 + 2 * (t % 2))

    def HBb(t):
        return cx.bank(3 + 2 * (t % 2))

    def TQb(t):
        pb = t % 2
        return cx.bankb(6 + pb)[0:64, 0:512].rearrange("p (a b) -> p a b", a=4)

    def ip_ld(t):
        x3 = t % 3
        add("sync", lambda e: e.dma_start(out=XN[x3], in_=xv[t]), writes=[("XN", x3)], slot=("xn", x3))

    def ip_xpose(t):
        x3, pb = t % 3, t % 2
        TPb = cx.bankb(pb)
        for kc in range(8):
            add("pe", lambda e, kc=kc: e.transpose(out=TPb[:, kc * 128:(kc + 1) * 128], in_=XN[x3][:, kc * 128:(kc + 1) * 128], identity=ident),
                reads=[("XN", x3), "CB"], writes=[("TP", pb)], banks=[pb])
        add("act", lambda e: e.activation(out=XT[pb].rearrange("p a b -> p (a b)"), in_=TPb, func=AF.Copy), reads=[("TP", pb)], writes=[("XT", pb)], banks=[pb])

    def ip_mm(t):
        pb = t % 2
        HA, HB = HAb(t), HBb(t)
        for kc in range(8):
            add("pe", lambda e, kc=kc: e.matmul(out=HA[:, 0:384], lhsT=XT[pb][:, kc, :], rhs=W[:, kc, 0:384], start=(kc == 0), stop=(kc == 7)),
                reads=[("XT", pb), "W"], writes=[("HA", pb)], banks=[2 + 2 * pb])
        for kc in range(8):
            add("pe", lambda e, kc=kc: e.matmul(out=HB, lhsT=XT[pb][:, kc, :], rhs=W[:, kc, 384:896], start=(kc == 0), stop=(kc == 7)),
                reads=[("XT", pb), "W"], writes=[("HB", pb)], banks=[3 + 2 * pb])

    def ip_evac(t):
        pb, q4 = t % 2, t % 4
        HA, HB = HAb(t), HBb(t)
        add("dve", lambda e: e.tensor_copy(out=QKF[q4], in_=HA[:, 0:256]), reads=[("HA", pb)], writes=[("QKF", q4)], banks=[2 + 2 * pb])
        add("dve", lambda e: e.tensor_tensor(out=RQ[:, t, :], in0=HA[:, 256:384], in1=CQK, op=ALU.mult), reads=[("HA", pb), "CQK"], writes=[("RQ", t)], banks=[2 + 2 * pb])
        add("dve", lambda e: e.tensor_copy(out=VV[:, t, :, 0:128], in_=HB[:, 0:256].rearrange("p (a b) -> p a b", a=2)), reads=[("HB", pb)], writes=[("VV", t)], banks=[3 + 2 * pb])
        add("act", lambda e: e.activation(out=G[:, t, :], in_=HB[:, 256:512], func=AF.Silu), reads=[("HB", pb)], writes=[("G", t)], banks=[3 + 2 * pb])

    def ip_normA(t):
        q4 = t % 4
        add("act", lambda e: e.activation(out=SQ[q4], in_=QKF[q4], func=AF.Square), reads=[("QKF", q4)], writes=[("SQ", q4)])
        add("dve", lambda e: e.reduce_sum(out=S4[q4][:, 0:4], in_=SQ[q4].rearrange("p (a b) -> p a b", a=4), axis=AX.X), reads=[("SQ", q4)], writes=[("S4a", q4)])
        add("dve", lambda e: e.tensor_scalar(out=S4[q4][:, 0:4], in0=S4[q4][:, 0:4], scalar1=1.0 / 64, scalar2=EPS, op0=ALU.mult, op1=ALU.add),
            reads=[("S4a", q4)], writes=[("S4a", q4)])
        add("pool", lambda e: e.tensor_tensor(out=S4[q4][:, 4:8], in0=S4[q4][:, 0:4], in1=MH, op=ALU.pow), reads=[("S4a", q4), "SMmh"], writes=[("S4b", q4)])

    def ip_normB(t):
        q4 = t % 4
        for g in range(4):
            add("dve", lambda e, g=g: e.scalar_tensor_tensor(out=QKN[q4][:, g * 64:(g + 1) * 64], in0=QKF[q4][:, g * 64:(g + 1) * 64], scalar=S4[q4][:, 4 + g:5 + g],
                                                            in1=WQK[:, g * 64:(g + 1) * 64], op0=ALU.mult, op1=ALU.mult),
                reads=[("QKF", q4), ("S4b", q4), "WQK"], writes=[("QKN", q4)])

    def ip_qkT(t):
        q4, pb = t % 4, t % 2
        TQ = TQb(t)
        for g in range(4):
            add("pe", lambda e, g=g: e.transpose(out=TQ[:, g, :], in_=QKN[q4][:, g * 64:(g + 1) * 64], identity=ident), reads=[("QKN", q4), "CB"], writes=[("TQ", pb)], banks=[6 + pb])

    def ip_qkC(t):
        pb = t % 2
        TQ = TQb(t)
        add("dve", lambda e: e.tensor_copy(out=QKT[0:64, :, t * 128:(t + 1) * 128], in_=TQ), reads=[("TQ", pb)], writes=[("QKT", t)], banks=[6 + pb])

    ip_ld(0)
    if NT > 1:
        ip_ld(1)
    ip_xpose(0)
    for t in range(NT + 3):
        if t + 2 < NT:
            ip_ld(t + 2)
        if t + 1 < NT:
            ip_xpose(t + 1)
        if t < NT:
            ip_mm(t)
        if 0 <= t - 3 < NT:
            ip_qkC(t - 3)
        if t < NT:
            ip_evac(t)
        if 0 <= t - 1 < NT:
            ip_normB(t - 1)
        if 0 <= t - 2 < NT:
            ip_qkT(t - 2)
        if t < NT:
            ip_normA(t)
    A.pop()
    P.barrier()

    A.push()
    B = A.alloc([NT, 128], F32)
    NBUF = 3
    KZ = [A.alloc([4, 128], BF16) for _ in range(NBUF)]
    RT = [A.alloc([4, 2, 128], BF16, parts=64) for _ in range(2)]
    QX = [A.alloc([4, 128], BF16) for _ in range(NBUF)]
    QXT = [A.alloc([4, 128], BF16) for _ in range(NBUF)]
    WT = [A.alloc([4, 128], BF16) for _ in range(NBUF)]
    SSg = [A.alloc([4, 128], BF16) for _ in range(NBUF)]
    SQ2 = [A.alloc([512], F32) for _ in range(1)] * 2
    R4 = [A.alloc([8], F32) for _ in range(NBUF)]
    NG = NT // 4

    def rq(g):
        return [("RQ", 4 * g + i) for i in range(4)]

    def p1_kz(g):
        b3 = g % NBUF
        add("dve", lambda e: e.tensor_scalar(out=KZ[b3][:, :, 0:64], in0=RQ[:, 4 * g:4 * g + 4, 64:128], scalar1=ZF, scalar2=None, op0=ALU.mult),
            reads=rq(g) + ["ZF"], writes=[("KZ", b3)])
        add("dve", lambda e: e.tensor_scalar(out=KZ[b3][:, :, 64:128], in0=RQ[:, 4 * g:4 * g + 4, 64:128], scalar1=ZB, scalar2=None, op0=ALU.mult),
            reads=rq(g) + ["ZB"], writes=[("KZ", b3)])

    def p1_mm(g):
        b3, pb = g % NBUF, g % 2
        UU = cx.bank(pb).rearrange("p (a b) -> p a b", a=4)
        for i in range(4):
            c = 4 * g + i
            add("pe", lambda e, c=c, i=i: e.matmul(out=UU[:, i, :], lhsT=KZ[b3][:, i, :], rhs=VV[:, c, 1, 0:128], start=True, stop=True),
                reads=[("KZ", b3), ("VV", c)], writes=[("UU", pb)], banks=[pb])

    def p1_ev(g):
        pb = g % 2
        UU = cx.bank(pb).rearrange("p (a b) -> p a b", a=4)
        add("act", lambda e: e.activation(out=B[:, 4 * g:4 * g + 4, :].rearrange("p a b -> p (a b)"), in_=UU.rearrange("p a b -> p (a b)"), func=AF.Copy),
            reads=[("UU", pb)], writes=[("B", c) for c in range(4 * g, 4 * g + 4)], banks=[pb])

    p1_kz(0)
    for g in range(NG):
        if g + 1 < NG:
            p1_kz(g + 1)
        p1_mm(g)
        p1_ev(g)
    CDf = SM[0:64, SM_CD:SM_CD + 1]
    CDb = SM[64:128, SM_CD:SM_CD + 1]
    for c in range(1, NT):
        add("dve", lambda e, c=c: e.scalar_tensor_tensor(out=B[0:64, c, :], in0=B[0:64, c - 1, :], scalar=CDf, in1=B[0:64, c, :], op0=ALU.mult, op1=ALU.add),
            reads=[("B", c - 1), ("B", c), "CDf"], writes=[("B", c)])
        cb = NT - 1 - c
        add("pool", lambda e, cb=cb: e.scalar_tensor_tensor(out=B[64:128, cb, :], in0=B[64:128, cb + 1, :], scalar=CDb, in1=B[64:128, cb, :], op0=ALU.mult, op1=ALU.add)
            if False else e.tensor_scalar(out=TMPB[64:128, :], in0=B[64:128, cb + 1, :], scalar1=CDb, scalar2=None, op0=ALU.mult),
            reads=[("Bb", cb + 1), ("Bev", cb + 1), "CDb"], writes=["TMPBb"]) if False else None
        add("dve", lambda e, cb=cb: e.scalar_tensor_tensor(out=B[64:128, cb, :], in0=B[64:128, cb + 1, :], scalar=CDb, in1=B[64:128, cb, :], op0=ALU.mult, op1=ALU.add),
            reads=[("Bb", cb + 1), ("Bb", cb), ("B", cb), ("B", cb + 1), "CDb"], writes=[("Bb", cb)])
    allB = [("B", c) for c in range(NT)] + [("Bb", c) for c in range(NT)]

    def s1(g):
        b3, pb = g % NBUF, g % 2
        TR = cx.bankb(2 + pb)[0:64, :].rearrange("p (a b c) -> p a b c", a=4, b=2)
        TX = cx.bankb(6 + pb)[:, 0:512].rearrange("p (a b) -> p a b", a=4)
        add("dve", lambda e: e.tensor_scalar(out=QX[b3][:, :, 0:64], in0=RQ[:, 4 * g:4 * g + 4, 0:64], scalar1=XIF, scalar2=None, op0=ALU.mult),
            reads=rq(g) + ["XIF"], writes=[("QX", b3)])
        add("dve", lambda e: e.tensor_scalar(out=QX[b3][:, :, 64:128], in0=RQ[:, 4 * g:4 * g + 4, 0:64], scalar1=XIB, scalar2=None, op0=ALU.mult),
            reads=rq(g) + ["XIB"], writes=[("QX", b3)])
        for i in range(4):
            c = 4 * g + i
            for j in range(2):
                add("pe", lambda e, c=c, i=i, j=j: e.transpose(out=TR[:, i, j, :], in_=RQ[:, c, j * 64:(j + 1) * 64], identity=ident),
                    reads=[("RQ", c), "CB"], writes=[("TR", pb)], banks=[2 + pb])
        for i in range(4):
            add("pe", lambda e, i=i: e.transpose(out=TX[:, i, :], in_=QX[b3][:, i, :], identity=ident), reads=[("QX", b3), "CB"], writes=[("TX", pb)], banks=[6 + pb])

    def s2(g):
        b3, pb = g % NBUF, g % 2
        TR = cx.bankb(2 + pb)[0:64, :].rearrange("p (a b c) -> p a b c", a=4, b=2)
        TX = cx.bankb(6 + pb)[:, 0:512].rearrange("p (a b) -> p a b", a=4)
        AT = cx.bank(4 + pb).rearrange("p (a b) -> p a b", a=4)
        add("act", lambda e: e.activation(out=RT[pb].rearrange("p a b c -> p (a b c)"), in_=TR.rearrange("p a b c -> p (a b c)"), func=AF.Copy),
            reads=[("TR", pb)], writes=[("RT", pb)], banks=[2 + pb])
        add("act", lambda e: e.activation(out=QXT[b3].rearrange("p a b -> p (a b)"), in_=TX.rearrange("p a b -> p (a b)"), func=AF.Copy),
            reads=[("TX", pb)], writes=[("QXT", b3)], banks=[6 + pb])
        c0 = 4 * g
        if g == 0:
            add("pool", lambda e: e.memset(SSg[b3][0:64, 0, :], 0.0), writes=[("SSg", b3)])
            add("dve", lambda e: e.tensor_copy(out=SSg[b3][0:64, 1:4, :], in_=B[0:64, 0:3, :]), reads=allB, writes=[("SSg", b3)])
        else:
            add("dve", lambda e: e.tensor_copy(out=SSg[b3][0:64, :, :], in_=B[0:64, c0 - 1:c0 + 3, :]), reads=allB, writes=[("SSg", b3)])
        if g == NG - 1:
            add("pool", lambda e: e.memset(SSg[b3][64:128, 3, :], 0.0), writes=[("SSg", b3)])
            add("dve", lambda e: e.tensor_copy(out=SSg[b3][64:128, 0:3, :], in_=B[64:128, c0 + 1:c0 + 4, :]), reads=allB, writes=[("SSg", b3)])
        else:
            add("dve", lambda e: e.tensor_copy(out=SSg[b3][64:128, :, :], in_=B[64:128, c0 + 1:c0 + 5, :]), reads=allB, writes=[("SSg", b3)])
        for i in range(4):
            add("pe", lambda e, i=i: e.matmul(out=AT[:, i, :], lhsT=RT[pb][:, i, 1, :], rhs=RT[pb][:, i, 0, :], start=True, stop=True),
                reads=[("RT", pb)], writes=[("AT", pb)], banks=[4 + pb])
        add("dve", lambda e: e.tensor_tensor(out=WT[b3], in0=AT, in1=DC.unsqueeze(1).to_broadcast([128, 4, 128]), op=ALU.mult),
            reads=[("AT", pb), "DC"], writes=[("WT", b3)], banks=[4 + pb])

    def s3(g):
        b3, pb = g % NBUF, g % 2
        OUTb = cx.bank(pb).rearrange("p (a b) -> p a b", a=4)
        for i in range(4):
            c = 4 * g + i
            add("pe", lambda e, i=i, c=c: e.matmul(out=OUTb[:, i, :], lhsT=WT[b3][:, i, :], rhs=VV[:, c, 1, 0:128], start=True, stop=False),
                reads=[("WT", b3), ("VV", c)], writes=[("OUT", pb)], banks=[pb])
            add("pe", lambda e, i=i, c=c: e.matmul(out=OUTb[:, i, :], lhsT=QXT[b3][:, i, :], rhs=SSg[b3][:, i, :], start=False, stop=True),
                reads=[("QXT", b3), ("SSg", b3)], writes=[("OUT", pb)], banks=[pb])
        add("act", lambda e: e.activation(out=SQ2[pb], in_=OUTb.rearrange("p a b -> p (a b)"), func=AF.Square), reads=[("OUT", pb)], writes=[("SQ2", 0)], banks=[pb])
        add("dve", lambda e: e.reduce_sum(out=R4[b3][:, 0:4], in_=SQ2[pb].rearrange("p (a b) -> p a b", a=4), axis=AX.X), reads=[("SQ2", 0)], writes=[("R4a", b3)])
        add("dve", lambda e: e.tensor_scalar(out=R4[b3][:, 0:4], in0=R4[b3][:, 0:4], scalar1=1.0 / 128, scalar2=EPS, op0=ALU.mult, op1=ALU.add),
            reads=[("R4a", b3)], writes=[("R4a", b3)])
        add("pool", lambda e: e.tensor_tensor(out=R4[b3][:, 4:8], in0=R4[b3][:, 0:4], in1=MH, op=ALU.pow), reads=[("R4a", b3), "SMmh"], writes=[("R4b", b3)])

    def s4(g):
        b3, pb = g % NBUF, g % 2
        OUTb = cx.bank(pb).rearrange("p (a b) -> p a b", a=4)
        for i in range(4):
            c = 4 * g + i
            add("dve", lambda e, i=i, c=c: e.scalar_tensor_tensor(out=G[:, c, 128:256], in0=OUTb[:, i, :], scalar=R4[b3][:, 4 + i:5 + i],
                                                                 in1=G[:, c, 128:256], op0=ALU.mult, op1=ALU.mult),
                reads=[("OUT", pb), ("R4b", b3), ("G", c)], writes=[("G", c)], banks=[pb])

    for s in range(NG + 3):
        if s < NG:
            s1(s)
        if 0 <= s - 1 < NG:
            s2(s - 1)
        if 0 <= s - 3 < NG:
            s4(s - 3)
        if 0 <= s - 2 < NG:
            s3(s - 2)
    A.pop()
    P.barrier()

    A.push()
    PT = [A.alloc([2, 512], BF16) for _ in range(3)]
    OS = A.alloc([3, 512], F32)
    E1 = [A.alloc([128], F32) for _ in range(2)]
    E2 = [A.alloc([128], F32) for _ in range(2)]
    ES = [A.alloc([8], F32) for _ in range(2)]
    Tt = CF[:, CF_T:CF_T + 128]
    TN = CF[:, CF_TN:CF_TN + 128]

    def acc_ap(a, base):
        bnk = 4 + a // 3
        o = (a % 3) * 129
        return base[:, (bnk - 4) * 512 + o:(bnk - 4) * 512 + o + 129]

    OB = cx.bank(4, 3)
    it = 0
    def keep(qc, kb):
        if slope is None:
            return True
        md = max(0, kb * 128 - (qc * 512 + 511), qc * 512 - (kb * 128 + 127))
        return slope * md <= SKIP_ARG

    steps = [(qc, kb) for qc in range(NQC) for kb in range(NT) if keep(qc, kb)]
    first_kb = {qc: min(kb for (q_, kb) in steps if q_ == qc) for qc in range(NQC)}
    last_kb = {qc: max(kb for (q_, kb) in steps if q_ == qc) for qc in range(NQC)}

    def emit_qk(qc, kb, sp, pt):
        Sv = cx.bank(2 * sp, 2).rearrange("p (a b) -> p a b", a=2)
        rel = kb - 4 * qc
        qk_reads = ["QKTaug"] + [("QKT", kb)] + [("QKT", 4 * qc + i) for i in range(4)]
        if rel < 0 or rel >= 4:
            Kw = 66 if rel < 0 else 68
            for m in range(2):
                add("pe", lambda e, m=m, Kw=Kw, Sv=Sv: e.matmul(out=Sv[:, m, :], lhsT=QKT[0:Kw, 2 + m, kb * 128:(kb + 1) * 128],
                                                               rhs=QKT[0:Kw, m, qc * 512:(qc + 1) * 512], start=True, stop=True),
                    reads=qk_reads, writes=[("S", sp)], banks=[2 * sp, 2 * sp + 1])
            bias = (Tt if rel < 0 else TN)[:, rel + 64:rel + 65]
            add("act", lambda e, Sv=Sv, bias=bias, pt=pt: e.activation(out=PT[pt].rearrange("p a b -> p (a b)"), in_=Sv.rearrange("p a b -> p (a b)"),
                                                                      func=AF.Exp, bias=bias, scale=1.0),
                reads=[("S", sp), "CF"], writes=[("PT", pt)], banks=[2 * sp, 2 * sp + 1])
        else:
            for m in range(2):
                for qs in range(4):
                    d = rel - qs
                    q0 = qc * 512 + qs * 128
                    if d == 0:
                        add("pe", lambda e, m=m, qs=qs, q0=q0, Sv=Sv: e.matmul(out=Sv[:, m, qs * 128:(qs + 1) * 128], lhsT=QKT[0:64, 2 + m, kb * 128:(kb + 1) * 128],
                                                                              rhs=QKT[0:64, m, q0:q0 + 128], start=True, stop=False),
                            reads=qk_reads, writes=[("S", sp)], banks=[2 * sp, 2 * sp + 1])
                        add("pe", lambda e, m=m, qs=qs, Sv=Sv: e.matmul(out=Sv[:, m, qs * 128:(qs + 1) * 128], lhsT=ident, rhs=Bm, start=False, stop=True),
                            reads=["CB"], writes=[("S", sp)], banks=[2 * sp, 2 * sp + 1])
                    else:
                        Kw = 68 if d > 0 else 66
                        add("pe", lambda e, m=m, qs=qs, q0=q0, Kw=Kw, Sv=Sv: e.matmul(out=Sv[:, m, qs * 128:(qs + 1) * 128], lhsT=QKT[0:Kw, 2 + m, kb * 128:(kb + 1) * 128],
                                                                                     rhs=QKT[0:Kw, m, q0:q0 + 128], start=True, stop=True),
                            reads=qk_reads, writes=[("S", sp)], banks=[2 * sp, 2 * sp + 1])
            for qs in range(4):
                d = rel - qs
                bias = ZERO if d == 0 else (TN if d > 0 else Tt)[:, rel + 64:rel + 65]
                add("act", lambda e, Sv=Sv, bias=bias, pt=pt, qs=qs: e.activation(out=PT[pt][:, :, qs * 128:(qs + 1) * 128], in_=Sv[:, :, qs * 128:(qs + 1) * 128],
                                                                                 func=AF.Exp, bias=bias, scale=1.0),
                    reads=[("S", sp), "CF", "SMz"], writes=[("PT", pt)], banks=[2 * sp, 2 * sp + 1])

    def emit_pv(qc, kb, pt):
        for a in range(8):
            qs, m = a // 2, a % 2
            add("pe", lambda e, a=a, qs=qs, m=m: e.matmul(out=acc_ap(a, OB), lhsT=PT[pt][:, m, qs * 128:(qs + 1) * 128], rhs=VV[:, kb, 0, 0:129],
                                                         start=(kb == first_kb[qc] and a % 3 == 0), stop=(kb == last_kb[qc]), skip_group_check=True),
                reads=[("PT", pt), ("VV", kb), "VVone"], writes=["OB"], banks=[4, 5, 6])

    def emit_epi(qc):
        for j in range(3):
            ncol = 387 if j < 2 else 258
            add("dve", lambda e, j=j, ncol=ncol: e.tensor_copy(out=OS[:, j, 0:ncol], in_=OB[:, j * 512:j * 512 + ncol]), reads=["OB"], writes=["OS"], banks=[4, 5, 6])
        OSf = OS.rearrange("p a b -> p (a b)")
        for qs in range(4):
            t = 4 * qc + qs
            pb = qs % 2
            o1 = acc_ap(2 * qs, OSf)
            o2 = acc_ap(2 * qs + 1, OSf)
            es = ES[pb]
            add("dve", lambda e, o1=o1, es=es: e.reciprocal(out=es[:, 0:1], in_=o1[:, 128:129]), reads=["OS"], writes=[("ESa", pb)])
            add("dve", lambda e, o2=o2, es=es: e.reciprocal(out=es[:, 1:2], in_=o2[:, 128:129]), reads=["OS"], writes=[("ESb", pb)])
            add("dve", lambda e, es=es: e.tensor_scalar(out=es[:, 2:3], in0=es[:, 1:2], scalar1=NEGLAM, scalar2=None, op0=ALU.mult),
                reads=[("ESb", pb), "NEGLAM"], writes=[("ESc", pb)])
            add("dve", lambda e, o1=o1, es=es, pb=pb: e.tensor_scalar(out=E1[pb], in0=o1[:, 0:128], scalar1=es[:, 0:1], scalar2=None, op0=ALU.mult),
                reads=["OS", ("ESa", pb)], writes=[("E1", pb)])
            add("dve", lambda e, o2=o2, es=es, pb=pb: e.scalar_tensor_tensor(out=E1[pb], in0=o2[:, 0:128], scalar=es[:, 2:3], in1=E1[pb], op0=ALU.mult, op1=ALU.add),
                reads=["OS", ("ESc", pb), ("E1", pb)], writes=[("E1", pb)])
            add("dve", lambda e, es=es, pb=pb: e.scalar_tensor_tensor(out=E2[pb], in0=E1[pb], scalar=1.0, in1=E1[pb], op0=ALU.mult, op1=ALU.mult, accum_out=es[:, 3:4]),
                reads=[("E1", pb)], writes=[("E2", pb), ("ESd", pb)])
            add("dve", lambda e, es=es: e.tensor_scalar(out=es[:, 3:4], in0=es[:, 3:4], scalar1=1.0 / 128, scalar2=EPS, op0=ALU.mult, op1=ALU.add),
                reads=[("ESd", pb)], writes=[("ESd", pb)])
            add("pool", lambda e, es=es: e.tensor_tensor(out=es[:, 4:5], in0=es[:, 3:4], in1=MH[:, 0:1], op=ALU.pow),
                reads=[("ESd", pb), "SMmh"], writes=[("ESe", pb)])
            add("dve", lambda e, es=es, pb=pb: e.scalar_tensor_tensor(out=E2[pb], in0=E1[pb], scalar=es[:, 4:5], in1=SUBW, op0=ALU.mult, op1=ALU.mult),
                reads=[("E1", pb), ("ESe", pb), "SUBW", ("E2", pb)], writes=[("E2", pb)])
            add("dve", lambda e, t=t, pb=pb: e.tensor_tensor(out=G[:, t, 0:128], in0=E2[pb], in1=G[:, t, 0:128], op=ALU.mult),
                reads=[("E2", pb), ("G", t)], writes=[("G", t)])
        mv = io["mix"].rearrange("(t p) c -> p t c", p=128)
        add("sync", lambda e, qc=qc: e.dma_start(out=mv[:, 4 * qc:4 * qc + 4, :], in_=G[:, 4 * qc:4 * qc + 4, :]),
            reads=[("G", 4 * qc + i) for i in range(4)], slot=("mixout", qc % 2))

    n = len(steps)
    for j in range(min(2, n)):
        emit_qk(steps[j][0], steps[j][1], j % 2, j % 3)
    for i, (qc, kb) in enumerate(steps):
        if i + 2 < n:
            emit_qk(steps[i + 2][0], steps[i + 2][1], (i + 2) % 2, (i + 2) % 3)
        emit_pv(qc, kb, i % 3)
        if kb == last_kb[qc]:
            emit_epi(qc)
    A.pop()


def declare_B_io(nc, NT):
    S = NT * 128
    io = {}
    io["xn"] = nc.dram_tensor("xn", [S, 1024], BF16, kind="ExternalInput").ap()
    io["w"] = nc.dram_tensor("w", [1024, 896], F32, kind="ExternalInput").ap()
    io["normw"] = nc.dram_tensor("normw", [128, 8], F32, kind="ExternalInput").ap()
    io["vec64"] = nc.dram_tensor("vec64", [1, 6 * 64], F32, kind="ExternalInput").ap()
    io["subln"] = nc.dram_tensor("subln", [1, 128], F32, kind="ExternalInput").ap()
    io["scal"] = nc.dram_tensor("scal", [1, 4], F32, kind="ExternalInput").ap()
    io["cpackf"] = nc.dram_tensor("cpackf", [128, CF_N], F32, kind="ExternalInput").ap()
    io["cpackb"] = nc.dram_tensor("cpackb", [128, 256], BF16, kind="ExternalInput").ap()
    io["aug"] = nc.dram_tensor("aug", [4, 4, S], BF16, kind="ExternalInput").ap()
    io["mix"] = nc.dram_tensor("mix", [S, 256], BF16, kind="ExternalOutput").ap()
    return io


def build_B(NT):
    nc = bass.Bass("TRN2", target_bir_lowering=False)
    io = declare_B_io(nc, NT)
    with contextlib.ExitStack() as st:
        cx = Ctx(nc, st)
        emit_B(cx, io, NT)
        cx.P.emit(nc)
    return nc


def alibi_slope(h):
    return 2.0 ** (-8.0 * (h + 1) / 4)


def const_tables(hh, S):
    s = alibi_slope(hh)
    p = np.arange(128, dtype=np.float64)
    cf = np.zeros((128, CF_N), np.float32)
    tt = p[None, :] - p[:, None]
    cf[:, CF_M1:CF_M1 + 128] = np.maximum(tt, 0)
    cf[:, CF_M2:CF_M2 + 128] = np.maximum(-tt, 0)
    u = np.arange(128, dtype=np.float64)
    T = s * (p[:, None] + 128.0 * (u[None, :] - 64))
    cf[:, CF_T:CF_T + 128] = T
    cf[:, CF_TN:CF_TN + 128] = -T
    cf[:, CF_C5 + 0] = p + 1
    cf[:, CF_C5 + 1] = 128 - p
    cf[:, CF_C5 + 2] = 127 - p
    cf[:, CF_C5 + 3] = p
    cf[:, CF_C5 + 4] = 128
    cb = np.zeros((128, 256), np.float32)
    cb[:, 0:128] = np.eye(128)
    cb[:, 128:256] = -s * np.abs(tt)
    r = np.arange(S) % 512
    rlo = (r & 255).astype(np.float64)
    rhi = (r - (r & 255)).astype(np.float64)
    qa = np.stack([-s * rlo, -s * rhi, -2 * s * rlo, -2 * s * rhi])
    ka = np.stack([np.ones(S), np.ones(S), -np.ones(S), -np.ones(S)])
    aug = np.stack([qa, qa, ka, ka], axis=1)
    return cf, cb.astype(ml_dtypes.bfloat16), aug.astype(ml_dtypes.bfloat16)


def w_in_cols(hh):
    qa = [hh * 128 + m * 64 + j for m in range(2) for j in range(64)]
    ka = [512 + c for c in qa]
    va = [1024 + hh * 128 + j for j in range(128)]
    ga = [1536 + hh * 128 + j for j in range(128)]
    qr = [2048 + hh * 64 + j for j in range(64)]
    kr = [2304 + hh * 64 + j for j in range(64)]
    vr = [2560 + hh * 128 + j for j in range(128)]
    gr = [3072 + hh * 128 + j for j in range(128)]
    return np.array(qa + ka + qr + kr + va + vr + ga + gr)


def B_inputs(inputs, l, hh, xn_b, S):
    cf, cb, aug = const_tables(hh, S)
    lam_init = 0.8 - 0.6 * math.exp(-0.3 * l)
    f32 = np.float32
    vec = np.stack([inputs["q_norm_w"][l], inputs["k_norm_w"][l], inputs["lambda_q1"][l], inputs["lambda_k1"][l],
                    inputs["lambda_q2"][l], inputs["lambda_k2"][l]]).astype(f32).reshape(1, 384)
    scal = np.array([[inputs["ret_decay_fwd"][l, hh], inputs["ret_decay_bwd"][l, hh], lam_init, 1.0 - lam_init]], f32)
    return dict(
        xn=xn_b,
        w=np.ascontiguousarray(inputs["w_in"][l][:, w_in_cols(hh)]).astype(f32),
        normw=np.ascontiguousarray(inputs["norm_w"][l].reshape(8, 128).T).astype(f32),
        vec64=vec,
        subln=inputs["subln_w"][l].astype(f32).reshape(1, 128),
        scal=scal,
        cpackf=cf, cpackb=cb, aug=aug,
    )


def emit_norm_tile(cx, src_f32, dst_bf16, junk, st, key, MH):
    add = cx.P.add
    add("act", lambda e: e.activation(out=junk, in_=src_f32, func=AF.Square, accum_out=st[:, 0:1]), reads=[key], writes=[("nj", id(junk)), ("st0", id(st))])
    add("dve", lambda e: e.tensor_scalar(out=st[:, 1:2], in0=st[:, 0:1], scalar1=1.0 / D_MODEL, scalar2=EPS, op0=ALU.mult, op1=ALU.add),
        reads=[("st0", id(st))], writes=[("st1", id(st))])
    add("pool", lambda e: e.tensor_tensor(out=st[:, 2:3], in0=st[:, 1:2], in1=MH, op=ALU.pow), reads=[("st1", id(st)), "MHc"], writes=[("st2", id(st))])
    add("dve", lambda e: e.tensor_scalar(out=dst_bf16, in0=src_f32, scalar1=st[:, 2:3], scalar2=None, op0=ALU.mult),
        reads=[key, ("st2", id(st))], writes=[("xb", id(dst_bf16))])


def emit_A(cx, io, NTQ):
    P, A = cx.P, cx.A
    add = P.add
    A.push()
    XF = [A.alloc([1024], F32) for _ in range(2)]
    XB = [A.alloc([1024], BF16) for _ in range(2)]
    JK = [A.alloc([1024], BF16) for _ in range(2)]
    ST = [A.alloc([4], F32) for _ in range(2)]
    MHc = A.alloc([1], F32)
    add("pool", lambda e: e.memset(MHc, -0.5), writes=["MHc"])
    xv = io["x"].rearrange("(t p) d -> t p d", p=128)
    ov = io["xn_out"].rearrange("(t p) d -> t p d", p=128)
    for t in range(NTQ):
        pb = t % 2
        add("sync", lambda e, t=t, pb=pb: e.dma_start(out=XF[pb], in_=xv[t]), writes=[("XF", pb)], slot=("xf", pb))
        emit_norm_tile(cx, XF[pb], XB[pb], JK[pb], ST[pb], ("XF", pb), MHc)
        add("sync", lambda e, t=t, pb=pb: e.dma_start(out=ov[t], in_=XB[pb]), reads=[("xb", id(XB[pb]))], slot=("xno", pb))
    A.pop()
    P.barrier()


def build_A(NTQ):
    nc = bass.Bass("TRN2", target_bir_lowering=False)
    io = {}
    io["x"] = nc.dram_tensor("x", [NTQ * 128, 1024], F32, kind="ExternalInput").ap()
    io["xn_out"] = nc.dram_tensor("xn_out", [NTQ * 128, 1024], BF16, kind="ExternalOutput").ap()
    with contextlib.ExitStack() as st:
        cx = Ctx(nc, st)
        emit_A(cx, io, NTQ)
        cx.P.emit(nc)
    return nc


def emit_C(cx, io, NTQ, with_norm):
    P, A = cx.P, cx.A
    add = P.add
    A.push()
    WO = A.alloc([8, 1024], BF16)
    WST = [A.alloc([1024], F32) for _ in range(2)]
    IDN = A.alloc([128], BF16)
    MX = [A.alloc([1024], BF16) for _ in range(2)]
    MT = [A.alloc([8, 128], BF16) for _ in range(2)]
    XR = [A.alloc([1024], F32) for _ in range(2)]
    XO = [A.alloc([1024], F32) for _ in range(2)]
    XB = [A.alloc([1024], BF16) for _ in range(2)]
    JK = [A.alloc([1024], BF16) for _ in range(2)]
    ST = [A.alloc([4], F32) for _ in range(2)]
    MHc = A.alloc([1], F32)
    add("pool", lambda e: e.memset(MHc, -0.5), writes=["MHc"])
    add("sync", lambda e: e.dma_start(out=IDN, in_=io["ident"]), writes=["IDN"], slot="ldi")
    wv = io["wout"].rearrange("(k p) n -> k p n", p=128)
    for kc in range(8):
        b_ = kc % 2
        add("sync", lambda e, kc=kc, b_=b_: e.dma_start(out=WST[b_], in_=wv[kc]), writes=[("WSTo", b_)], slot=("wsto", b_))
        add("dve" if kc % 2 else "pool", lambda e, kc=kc, b_=b_: e.tensor_copy(out=WO[:, kc, :], in_=WST[b_]), reads=[("WSTo", b_)], writes=["WO"])
    if "mix_hm" in io:
        mvh = io["mix_hm"].rearrange("h (t p) c -> t p h c", p=128)
        mv = [mvh[t] for t in range(NTQ)]
        MXv = [m.rearrange("p (h c) -> p h c", h=4) for m in MX]
    else:
        mvf = io["mixq"].rearrange("(t p) d -> t p d", p=128)
        mv = [mvf[t] for t in range(NTQ)]
        MXv = MX
    xv = io["xres"].rearrange("(t p) d -> t p d", p=128)
    ov = io["xnew"].rearrange("(t p) d -> t p d", p=128)
    if with_norm:
        nv = io["xn_out"].rearrange("(t p) d -> t p d", p=128)
    for t in range(NTQ):
        pb = t % 2
        TPb = cx.bankb(pb)
        ACC = cx.bank(2 + 2 * pb, 2)
        add("sync", lambda e, t=t, pb=pb: e.dma_start(out=MXv[pb], in_=mv[t]), writes=[("MX", pb)], slot=("mx", pb))
        add("sync", lambda e, t=t, pb=pb: e.dma_start(out=XR[pb], in_=xv[t]), writes=[("XR", pb)], slot=("xr", pb))
        for kc in range(8):
            add("pe", lambda e, pb=pb, kc=kc, TPb=TPb: e.transpose(out=TPb[:, kc * 128:(kc + 1) * 128], in_=MX[pb][:, kc * 128:(kc + 1) * 128], identity=IDN),
                reads=[("MX", pb), "IDN"], writes=[("TPc", pb)], banks=[pb])
        add("act", lambda e, pb=pb, TPb=TPb: e.activation(out=MT[pb].rearrange("p a b -> p (a b)"), in_=TPb, func=AF.Copy),
            reads=[("TPc", pb)], writes=[("MT", pb)], banks=[pb])
        for hf in range(2):
            for kc in range(8):
                add("pe", lambda e, pb=pb, kc=kc, hf=hf, ACC=ACC: e.matmul(out=ACC[:, hf * 512:(hf + 1) * 512], lhsT=MT[pb][:, kc, :], rhs=WO[:, kc, hf * 512:(hf + 1) * 512],
                                                                          start=(kc == 0), stop=(kc == 7)),
                    reads=[("MT", pb), "WO"], writes=[("ACC", pb)], banks=[2 + 2 * pb, 3 + 2 * pb])
        add("dve", lambda e, pb=pb, ACC=ACC: e.tensor_tensor(out=XO[pb], in0=ACC, in1=XR[pb], op=ALU.add),
            reads=[("ACC", pb), ("XR", pb)], writes=[("XO", pb)], banks=[2 + 2 * pb, 3 + 2 * pb])
        add("sync", lambda e, t=t, pb=pb: e.dma_start(out=ov[t], in_=XO[pb]), reads=[("XO", pb)], slot=("xo", pb))
        if with_norm:
            emit_norm_tile(cx, XO[pb], XB[pb], JK[pb], ST[pb], ("XO", pb), MHc)
            add("sync", lambda e, t=t, pb=pb: e.dma_start(out=nv[t], in_=XB[pb]), reads=[("xb", id(XB[pb]))], slot=("xnc", pb))
    A.pop()
    P.barrier()


def build_C(NTQ, with_norm):
    nc = bass.Bass("TRN2", target_bir_lowering=False)
    io = {}
    io["mixq"] = nc.dram_tensor("mixq", [NTQ * 128, 1024], BF16, kind="ExternalInput").ap()
    io["xres"] = nc.dram_tensor("xres", [NTQ * 128, 1024], F32, kind="ExternalInput").ap()
    io["wout"] = nc.dram_tensor("wout", [1024, 1024], F32, kind="ExternalInput").ap()
    io["ident"] = nc.dram_tensor("ident", [128, 128], BF16, kind="ExternalInput").ap()
    io["xnew"] = nc.dram_tensor("xnew", [NTQ * 128, 1024], F32, kind="ExternalOutput").ap()
    if with_norm:
        io["xn_out"] = nc.dram_tensor("xn_out", [NTQ * 128, 1024], BF16, kind="ExternalOutput").ap()
    with contextlib.ExitStack() as st:
        cx = Ctx(nc, st)
        emit_C(cx, io, NTQ, with_norm)
        cx.P.emit(nc)
    return nc


def wout_rows():
    rows = []
    for h in range(4):
        rows += list(range(h * 128, (h + 1) * 128)) + list(range(512 + h * 128, 512 + (h + 1) * 128))
    return np.array(rows)


_CACHE = {}


def _get(name, fn):
    if name not in _CACHE:
        _CACHE[name] = fn()
    return _CACHE[name]


def kernel_unfused(**inputs):
    inputs = {k: np.asarray(v) for k, v in inputs.items()}
    x = inputs["x"]
    Bsz, S, Dm = x.shape
    NT = S // 128
    NTQ = NT // 4
    TQ = S // 4
    cores = list(range(8))
    ident = np.eye(128, dtype=np.float32).astype(ml_dtypes.bfloat16)
    ncA = _get("A", lambda: build_A(NTQ))
    res = run_bass_kernel_spmd(ncA, [dict(x=np.ascontiguousarray(x[c // 4, (c % 4) * TQ:(c % 4 + 1) * TQ])) for c in cores], core_ids=cores)
    xn_q = [r["xn_out"] for r in res.results]
    xcur = [np.ascontiguousarray(x[c // 4, (c % 4) * TQ:(c % 4 + 1) * TQ]) for c in cores]
    for l in range(2):
        xn_b = [np.concatenate([xn_q[b * 4 + j] for j in range(4)], axis=0) for b in range(Bsz)]
        ncB = _get("B", lambda: build_B(NT))
        res = run_bass_kernel_spmd(ncB, [B_inputs(inputs, l, c % 4, xn_b[c // 4], S) for c in cores], core_ids=cores)
        mix = [r["mix"] for r in res.results]
        wout = np.ascontiguousarray(inputs["w_out"][l][wout_rows()]).astype(np.float32)
        last = (l == 1)
        ncC = _get("C%d" % l, lambda: build_C(NTQ, not last))
        ins = []
        for c in cores:
            b, j = c // 4, c % 4
            mixq = np.concatenate([mix[b * 4 + h][j * TQ:(j + 1) * TQ] for h in range(4)], axis=1)
            ins.append(dict(mixq=np.ascontiguousarray(mixq), xres=xcur[c], wout=wout, ident=ident))
        res = run_bass_kernel_spmd(ncC, ins, core_ids=cores)
        xcur = [r["xnew"] for r in res.results]
        if not last:
            xn_q = [r["xn_out"] for r in res.results]
    out = np.stack([np.concatenate([xcur[b * 4 + j] for j in range(4)], axis=0) for b in range(Bsz)], axis=0)
    return out.astype(np.float32)


def build_fused(NT):
    S = NT * 128
    nc = bass.Bass("TRN2", target_bir_lowering=False)
    dt = nc.dram_tensor
    x = dt("x", [S, 1024], F32, kind="ExternalInput").ap()
    w_all = dt("w_all", [2, 4, 1024, 896], F32, kind="ExternalInput").ap()
    normw = dt("normw", [2, 128, 8], F32, kind="ExternalInput").ap()
    vec64 = dt("vec64", [2, 1, 384], F32, kind="ExternalInput").ap()
    subln = dt("subln", [2, 1, 128], F32, kind="ExternalInput").ap()
    scal = dt("scal", [2, 4, 1, 4], F32, kind="ExternalInput").ap()
    cpackf = dt("cpackf", [4, 128, CF_N], F32, kind="ExternalInput").ap()
    cpackb = dt("cpackb", [4, 128, 256], BF16, kind="ExternalInput").ap()
    aug = dt("aug", [4, 4, 4, S], BF16, kind="ExternalInput").ap()
    wout = dt("wout", [2, 1024, 1024], F32, kind="ExternalInput").ap()
    identd = dt("ident", [128, 128], BF16, kind="ExternalInput").ap()
    out = dt("out", [S, 1024], F32, kind="ExternalOutput").ap()
    XNs = dt("xn_scratch", [S, 1024], BF16, kind="Internal").ap()
    MIXs = dt("mix_scratch", [4, S, 256], BF16, kind="Internal").ap()
    X1 = dt("x1_scratch", [S, 1024], F32, kind="Internal").ap()
    with contextlib.ExitStack() as st:
        cx = Ctx(nc, st)
        emit_A(cx, dict(x=x, xn_out=XNs), NT)
        for l in range(2):
            for h in range(4):
                emit_B(cx, dict(xn=XNs, w=w_all[l, h], normw=normw[l], vec64=vec64[l], subln=subln[l], scal=scal[l, h],
                                cpackf=cpackf[h], cpackb=cpackb[h], aug=aug[h], mix=MIXs[h]), NT, slope=alibi_slope(h))
            ioC = dict(mix_hm=MIXs, xres=(x if l == 0 else X1), wout=wout[l], ident=identd, xnew=(X1 if l == 0 else out))
            if l == 0:
                ioC["xn_out"] = XNs
            emit_C(cx, ioC, NT, with_norm=(l == 0))
        cx.P.emit(nc)
    print("[build_fused] ops", len(cx.P.ops), "sem counts", {str(k): v for k, v in cx.P.sem_counts.items() if k[0] == "eng"}, flush=True)
    return nc


def fused_inputs(inputs, b, S):
    f32 = np.float32
    tabs = [const_tables(h, S) for h in range(4)]
    lam_init = [0.8 - 0.6 * math.exp(-0.3 * l) for l in range(2)]
    vec = np.stack([np.stack([inputs["q_norm_w"][l], inputs["k_norm_w"][l], inputs["lambda_q1"][l], inputs["lambda_k1"][l],
                              inputs["lambda_q2"][l], inputs["lambda_k2"][l]]).reshape(1, 384) for l in range(2)]).astype(f32)
    scal = np.array([[[[inputs["ret_decay_fwd"][l, h], inputs["ret_decay_bwd"][l, h], lam_init[l], 1.0 - lam_init[l]]] for h in range(4)] for l in range(2)], f32)
    return dict(
        x=np.ascontiguousarray(inputs["x"][b]).astype(f32),
        w_all=np.stack([np.stack([inputs["w_in"][l][:, w_in_cols(h)] for h in range(4)]) for l in range(2)]).astype(f32),
        normw=np.stack([inputs["norm_w"][l].reshape(8, 128).T for l in range(2)]).astype(f32),
        vec64=vec,
        subln=np.stack([inputs["subln_w"][l].reshape(1, 128) for l in range(2)]).astype(f32),
        scal=scal,
        cpackf=np.stack([t[0] for t in tabs]), cpackb=np.stack([t[1] for t in tabs]), aug=np.stack([t[2] for t in tabs]),
        wout=np.stack([inputs["w_out"][l][wout_rows()] for l in range(2)]).astype(f32),
        ident=np.eye(128, dtype=f32).astype(ml_dtypes.bfloat16),
    )


def kernel_fused(**inputs):
    inputs = {k: np.asarray(v) for k, v in inputs.items()}
    Bsz, S, _ = inputs["x"].shape
    nc = _get("F", lambda: build_fused(S // 128))
    res = run_bass_kernel_spmd(nc, [fused_inputs(inputs, b, S) for b in range(Bsz)], core_ids=list(range(Bsz)))
    return np.stack([res.results[b]["out"] for b in range(Bsz)], axis=0).astype(np.float32)


def kernel(**inputs):
    return kernel_fused(**inputs)
```

```python
import contextlib
import math
import numpy as np
import ml_dtypes
import concourse.bass as bass
import concourse.mybir as mybir
from concourse.bass_utils import run_bass_kernel_spmd

F32 = mybir.dt.float32
BF16 = mybir.dt.bfloat16
AF = mybir.ActivationFunctionType
ALU = mybir.AluOpType
AX = mybir.AxisListType

SAME_ENGINE_SYNC = True
ENGS = ("sync", "act", "dve", "pool", "pe")
D_MODEL = 1024
EPS = 1e-6


class _Op:
    __slots__ = ("eng", "fn", "deps", "slot", "val", "waits", "idx")

    def __init__(self, eng, fn, deps, slot, idx):
        self.eng = eng
        self.fn = fn
        self.deps = deps
        self.slot = slot
        self.val = None
        self.waits = None
        self.idx = idx


class Prog:
    def __init__(self):
        self.ops = []
        self.last_w = {}
        self.readers = {}
        self.last_on_eng = {}
        self.last_on_slot = {}
        self.pending = {e: set() for e in ENGS}

    def add(self, eng, fn, reads=(), writes=(), slot=None, banks=()):
        i = len(self.ops)
        deps = set()
        if banks:
            writes = list(writes) + [("bank", b) for b in banks]
        for k in reads:
            w = self.last_w.get(k)
            if w is not None:
                deps.add(w)
        for k in writes:
            w = self.last_w.get(k)
            if w is not None:
                deps.add(w)
            for r in self.readers.get(k, ()):
                deps.add(r)
        for k in reads:
            self.readers.setdefault(k, []).append(i)
        for k in writes:
            self.last_w[k] = i
            self.readers[k] = []
        if self.pending[eng]:
            deps |= self.pending[eng]
            self.pending[eng] = set()
        if slot is not None:
            p = self.last_on_slot.get(slot)
            if p is not None:
                deps.add(p)
            self.last_on_slot[slot] = i
        else:
            self.last_on_eng[eng] = i
        deps.discard(i)
        self.ops.append(_Op(eng, fn, deps, slot, i))
        return i

    def barrier(self):
        allp = set(self.last_on_eng.values()) | set(self.last_on_slot.values())
        for e in ENGS:
            self.pending[e] |= allp

    def _semkey(self, op):
        return ("slot", op.slot) if op.slot is not None else ("eng", op.eng)

    def emit(self, nc, final_wait_eng="sync"):
        ops = self.ops
        self.barrier()
        self.add(final_wait_eng, None)
        waited = {e: {} for e in ENGS}
        need = set()
        for op in ops:
            ws = {}
            for d in op.deps:
                y = ops[d]
                sk = self._semkey(y)
                if y.slot is None and y.eng == op.eng:
                    if op.eng in ("pe", "sync"):
                        continue
                    if not SAME_ENGINE_SYNC:
                        continue
                if waited[op.eng].get(sk, -1) >= d:
                    continue
                if sk not in ws or ws[sk] < d:
                    ws[sk] = d
            for sk, d in ws.items():
                waited[op.eng][sk] = d
                need.add(d)
            op.waits = ws
        cnt = {}
        for op in ops:
            sk = self._semkey(op)
            if op.slot is not None:
                cnt[sk] = cnt.get(sk, 0) + 16
                op.val = cnt[sk]
            elif op.idx in need:
                cnt[sk] = cnt.get(sk, 0) + 1
                op.val = cnt[sk]
        self.sem_counts = dict(cnt)
        sems = {}
        with contextlib.ExitStack() as stack:
            for n_, sk in enumerate(cnt):
                sems[sk] = stack.enter_context(nc.semaphore("sem%d" % n_))
            block = stack.enter_context(nc.Block())

            def run(eng_name):
                def body(eng):
                    for op in ops:
                        if op.eng != eng_name:
                            continue
                        for sk, d in op.waits.items():
                            eng.wait_ge(sems[sk], ops[d].val)
                        if op.fn is None:
                            continue
                        ins = op.fn(eng)
                        if op.val is not None:
                            ins.then_inc(sems[self._semkey(op)], 16 if op.slot is not None else 1)
                return body

            block.sync(run("sync"))
            block.scalar(run("act"))
            block.vector(run("dve"))
            block.gpsimd(run("pool"))
            block.tensor(run("pe"))


class Arena:
    def __init__(self, base_ap_bf16, nbytes):
        self.base = base_ap_bf16
        self.nbytes = nbytes
        self.top = 0
        self.marks = []
        self.peak = 0

    def alloc(self, shape_free, dtype, parts=128, align=32):
        esz = 4 if dtype == F32 else 2
        n = int(np.prod(shape_free))
        off = (self.top + align - 1) // align * align
        nb = n * esz
        assert off + nb <= self.nbytes, ("arena overflow", off, nb, self.nbytes)
        self.top = off + nb
        self.peak = max(self.peak, self.top)
        ap = self.base[0:parts, off // 2:(off + nb) // 2]
        if dtype == F32:
            ap = ap.bitcast(F32)
        if len(shape_free) == 2:
            ap = ap.rearrange("p (a b) -> p a b", a=shape_free[0])
        elif len(shape_free) == 3:
            ap = ap.rearrange("p (a b c) -> p a b c", a=shape_free[0], b=shape_free[1])
        return ap

    def push(self):
        self.marks.append(self.top)

    def pop(self):
        self.top = self.marks.pop()


SBUF_BYTES = 212736


class Ctx:
    def __init__(self, nc, stack):
        self.nc = nc
        self.P = Prog()
        sb = stack.enter_context(nc.sbuf_tensor("arena", [128, SBUF_BYTES // 2], BF16))
        self.A = Arena(sb[:, :], SBUF_BYTES)
        ps = stack.enter_context(nc.psum_tensor("psum_all", [128, 4096], F32))
        self.PS = ps[:, :]

    def bank(self, i, n=1):
        return self.PS[:, i * 512:(i + n) * 512]

    def bankb(self, i, n=1):
        return self.PS[:, i * 512:(i + n) * 512].bitcast(BF16)


CF_M1, CF_M2, CF_T, CF_TN, CF_C5, CF_N = 0, 128, 256, 384, 512, 520
SM_LGF, SM_LGB, SM_XIF, SM_XIB, SM_ZF, SM_ZB, SM_CD, SM_NEGLAM, SM_ZERO, SM_MH = 0, 1, 2, 3, 4, 5, 6, 7, 8, 9


SKIP_ARG = 87.4 + 16.0


def emit_B(cx, io, NT, slope=None):
    P, A = cx.P, cx.A
    A.push()
    _emit_B(cx, io, NT, slope)
    A.pop()
    P.barrier()


def _emit_B(cx, io, NT, slope=None):
    P, A = cx.P, cx.A
    S = NT * 128
    NQC = NT // 4
    add = P.add

    QKT = A.alloc([4, S], BF16, parts=68)
    VV = A.alloc([NT, 2, 132], BF16)
    G = A.alloc([NT, 256], BF16)
    RQ = A.alloc([NT, 128], BF16)
    CF = A.alloc([CF_N], F32)
    CB = A.alloc([256], BF16)
    WQK = A.alloc([256], F32)
    CQK = A.alloc([128], F32)
    SUBW = A.alloc([128], F32)
    V6 = A.alloc([6, 64], F32)
    SC = A.alloc([4], F32)
    SM = A.alloc([16], F32)
    DC = A.alloc([128], F32)
    TMPA = A.alloc([128], F32)
    TMPB = A.alloc([128], F32)
    TS = A.alloc([8], F32)
    ident = CB[:, 0:128]
    Bm = CB[:, 128:256]

    add("sync", lambda e: e.dma_start(out=CF, in_=io["cpackf"]), writes=["CF"], slot="ld0")
    add("sync", lambda e: e.dma_start(out=CB, in_=io["cpackb"]), writes=["CB"], slot="ld1")
    add("sync", lambda e: e.dma_start(out=V6, in_=io["vec64"].partition_broadcast(128)), writes=["V6"], slot="ld2")
    add("sync", lambda e: e.dma_start(out=SUBW, in_=io["subln"].partition_broadcast(128)), writes=["SUBW"], slot="ld3")
    add("sync", lambda e: e.dma_start(out=SC, in_=io["scal"].partition_broadcast(128)), writes=["SC"], slot="ld4")
    add("sync", lambda e: e.dma_start(out=QKT[64:68, :, :], in_=io["aug"]), writes=["QKTaug"], slot="ld5")
    add("pool", lambda e: e.memset(VV[:, :, 0, 128:129], 1.0), writes=["VVone"])
    add("pool", lambda e: e.memset(CQK[:, 0:64], 1.0), writes=["CQK"])
    add("pool", lambda e: e.memset(CQK[:, 64:128], 0.125), writes=["CQK"])
    add("pool", lambda e: e.memset(SM[:, SM_ZERO:SM_ZERO + 1], 0.0), writes=["SMz"])
    add("pool", lambda e: e.memset(SM[:, SM_MH:SM_MH + 4], -0.5), writes=["SMmh"])
    MH = SM[:, SM_MH:SM_MH + 4]
    ZERO = SM[:, SM_ZERO:SM_ZERO + 1]
    add("dve", lambda e: e.tensor_scalar(out=WQK[:, 0:128].rearrange("p (a b) -> p a b", a=2),
                                         in0=V6[:, 0, :].unsqueeze(1).to_broadcast([128, 2, 64]),
                                         scalar1=0.125, scalar2=None, op0=ALU.mult), reads=["V6"], writes=["WQK"])
    add("dve", lambda e: e.tensor_scalar(out=WQK[:, 128:256].rearrange("p (a b) -> p a b", a=2),
                                         in0=V6[:, 1, :].unsqueeze(1).to_broadcast([128, 2, 64]),
                                         scalar1=1.0, scalar2=None, op0=ALU.mult), reads=["V6"], writes=["WQK"])
    add("dve", lambda e: e.tensor_tensor(out=TMPA[:, 0:64], in0=V6[:, 2, :], in1=V6[:, 3, :], op=ALU.mult), reads=["V6"], writes=["TMPA"])
    add("dve", lambda e: e.reduce_sum(out=TS[:, 0:1], in_=TMPA[:, 0:64], axis=AX.X), reads=["TMPA"], writes=["TS0"])
    add("dve", lambda e: e.tensor_tensor(out=TMPA[:, 64:128], in0=V6[:, 4, :], in1=V6[:, 5, :], op=ALU.mult), reads=["V6"], writes=["TMPA2"])
    add("dve", lambda e: e.reduce_sum(out=TS[:, 1:2], in_=TMPA[:, 64:128], axis=AX.X), reads=["TMPA2"], writes=["TS1"])
    add("act", lambda e: e.activation(out=TS[:, 2:4], in_=TS[:, 0:2], func=AF.Exp), reads=["TS0", "TS1"], writes=["TS23"])
    add("dve", lambda e: e.tensor_tensor(out=TS[:, 4:5], in0=TS[:, 2:3], in1=TS[:, 3:4], op=ALU.subtract), reads=["TS23"], writes=["TS4"])
    add("dve", lambda e: e.tensor_scalar(out=SM[:, SM_NEGLAM:SM_NEGLAM + 1], in0=TS[:, 4:5], scalar1=SC[:, 2:3], scalar2=-1.0,
                                         op0=ALU.add, op1=ALU.mult), reads=["TS4", "SC"], writes=["NEGLAM"])
    NEGLAM = SM[:, SM_NEGLAM:SM_NEGLAM + 1]
    add("dve", lambda e: e.tensor_scalar(out=SUBW, in0=SUBW, scalar1=SC[:, 3:4], scalar2=None, op0=ALU.mult), reads=["SUBW", "SC"], writes=["SUBW"])
    add("act", lambda e: e.activation(out=TS[:, 5:7], in_=SC[:, 0:2], func=AF.Exp, scale=-1.0), reads=["SC"], writes=["TS56"])
    add("act", lambda e: e.activation(out=TS[:, 5:7], in_=TS[:, 5:7], func=AF.Ln, bias=1.0, scale=1.0), reads=["TS56"], writes=["TS56"])
    add("dve", lambda e: e.tensor_scalar(out=SM[:, 0:2], in0=TS[:, 5:7], scalar1=-1.0, scalar2=None, op0=ALU.mult), reads=["TS56"], writes=["LG"])
    LGF, LGB = SM[:, 0:1], SM[:, 1:2]
    C5 = CF[:, CF_C5:CF_C5 + 5]
    add("act", lambda e: e.activation(out=SM[:, SM_XIF:SM_XIF + 1], in_=C5[:, 0:1], func=AF.Exp, scale=LGF), reads=["LG", "CF"], writes=["XIF"])
    add("act", lambda e: e.activation(out=SM[:, SM_XIB:SM_XIB + 1], in_=C5[:, 1:2], func=AF.Exp, scale=LGB), reads=["LG", "CF"], writes=["XIB"])
    add("act", lambda e: e.activation(out=SM[:, SM_ZF:SM_ZF + 1], in_=C5[:, 2:3], func=AF.Exp, scale=LGF), reads=["LG", "CF"], writes=["ZF"])
    add("act", lambda e: e.activation(out=SM[:, SM_ZB:SM_ZB + 1], in_=C5[:, 3:4], func=AF.Exp, scale=LGB), reads=["LG", "CF"], writes=["ZB"])
    add("act", lambda e: e.activation(out=SM[0:64, SM_CD:SM_CD + 1], in_=C5[0:64, 4:5], func=AF.Exp, scale=SM[0:64, 0:1]), reads=["LG", "CF"], writes=["CDf"])
    add("act", lambda e: e.activation(out=SM[64:128, SM_CD:SM_CD + 1], in_=C5[64:128, 4:5], func=AF.Exp, scale=SM[64:128, 1:2]), reads=["LG", "CF"], writes=["CDb"])
    XIF, XIB = SM[:, SM_XIF:SM_XIF + 1], SM[:, SM_XIB:SM_XIB + 1]
    ZF, ZB = SM[:, SM_ZF:SM_ZF + 1], SM[:, SM_ZB:SM_ZB + 1]
    add("dve", lambda e: e.tensor_scalar(out=TMPB, in0=CF[:, CF_M1:CF_M1 + 128], scalar1=LGF, scalar2=None, op0=ALU.mult), reads=["LG", "CF"], writes=["TMPB"])
    add("dve", lambda e: e.scalar_tensor_tensor(out=TMPB, in0=CF[:, CF_M2:CF_M2 + 128], scalar=LGB, in1=TMPB, op0=ALU.mult, op1=ALU.add), reads=["LG", "CF", "TMPB"], writes=["TMPB"])
    add("act", lambda e: e.activation(out=DC, in_=TMPB, func=AF.Exp), reads=["TMPB"], writes=["DC"])

    A.push()
    W = A.alloc([8, 896], BF16)
    WST = [A.alloc([896], F32) for _ in range(2)]
    NW = A.alloc([8], F32)
    XN = [A.alloc([1024], BF16) for _ in range(3)]
    XT = [A.alloc([8, 128], BF16) for _ in range(2)]
    QKF = [A.alloc([256], F32) for _ in range(4)]
    SQ = [A.alloc([256], F32) for _ in range(4)]
    QKN = [A.alloc([256], BF16) for _ in range(4)]
    S4 = [A.alloc([8], F32) for _ in range(4)]
    add("sync", lambda e: e.dma_start(out=NW, in_=io["normw"]), writes=["NW"], slot="ld6")
    wv = io["w"].rearrange("(k p) n -> k p n", p=128)
    for kc in range(8):
        b_ = kc % 2
        add("sync", lambda e, kc=kc, b_=b_: e.dma_start(out=WST[b_], in_=wv[kc]), writes=[("WST", b_)], slot=("wst", b_))
        add("dve", lambda e, kc=kc, b_=b_: e.tensor_scalar(out=W[:, kc, :], in0=WST[b_], scalar1=NW[:, kc:kc + 1], scalar2=None, op0=ALU.mult),
            reads=[("WST", b_), "NW"], writes=["W"])
    xv = io["xn"].rearrange("(t p) d -> t p d", p=128)

    def HAb(t):
        return cx.bank(2 + 2 * (t % 2))

    def HBb(t):
        return cx.bank(3 + 2 * (t % 2))

    def TQb(t):
        pb = t % 2
        return cx.bankb(6 + pb)[0:64, 0:512].rearrange("p (a b) -> p a b", a=4)

    def ip_ld(t):
        x3 = t % 3
        add("sync", lambda e: e.dma_start(out=XN[x3], in_=xv[t]), writes=[("XN", x3)], slot=("xn", x3))

    def ip_xpose(t):
        x3, pb = t % 3, t % 2
        TPb = cx.bankb(pb)
        for kc in range(8):
            add("pe", lambda e, kc=kc: e.transpose(out=TPb[:, kc * 128:(kc + 1) * 128], in_=XN[x3][:, kc * 128:(kc + 1) * 128], identity=ident),
                reads=[("XN", x3), "CB"], writes=[("TP", pb)], banks=[pb])
        add("act", lambda e: e.activation(out=XT[pb].rearrange("p a b -> p (a b)"), in_=TPb, func=AF.Copy), reads=[("TP", pb)], writes=[("XT", pb)], banks=[pb])

    def ip_mm(t):
        pb = t % 2
        HA, HB = HAb(t), HBb(t)
        for kc in range(8):
            add("pe", lambda e, kc=kc: e.matmul(out=HA[:, 0:384], lhsT=XT[pb][:, kc, :], rhs=W[:, kc, 0:384], start=(kc == 0), stop=(kc == 7)),
                reads=[("XT", pb), "W"], writes=[("HA", pb)], banks=[2 + 2 * pb])
        for kc in range(8):
            add("pe", lambda e, kc=kc: e.matmul(out=HB, lhsT=XT[pb][:, kc, :], rhs=W[:, kc, 384:896], start=(kc == 0), stop=(kc == 7)),
                reads=[("XT", pb), "W"], writes=[("HB", pb)], banks=[3 + 2 * pb])

    def ip_evac(t):
        pb, q4 = t % 2, t % 4
        HA, HB = HAb(t), HBb(t)
        add("dve", lambda e: e.tensor_copy(out=QKF[q4], in_=HA[:, 0:256]), reads=[("HA", pb)], writes=[("QKF", q4)], banks=[2 + 2 * pb])
        add("dve", lambda e: e.tensor_tensor(out=RQ[:, t, :], in0=HA[:, 256:384], in1=CQK, op=ALU.mult), reads=[("HA", pb), "CQK"], writes=[("RQ", t)], banks=[2 + 2 * pb])
        add("dve", lambda e: e.tensor_copy(out=VV[:, t, :, 0:128], in_=HB[:, 0:256].rearrange("p (a b) -> p a b", a=2)), reads=[("HB", pb)], writes=[("VV", t)], banks=[3 + 2 * pb])
        add("act", lambda e: e.activation(out=G[:, t, :], in_=HB[:, 256:512], func=AF.Silu), reads=[("HB", pb)], writes=[("G", t)], banks=[3 + 2 * pb])

    def ip_normA(t):
        q4 = t % 4
        add("act", lambda e: e.activation(out=SQ[q4], in_=QKF[q4], func=AF.Square), reads=[("QKF", q4)], writes=[("SQ", q4)])
        add("dve", lambda e: e.reduce_sum(out=S4[q4][:, 0:4], in_=SQ[q4].rearrange("p (a b) -> p a b", a=4), axis=AX.X), reads=[("SQ", q4)], writes=[("S4a", q4)])
        add("dve", lambda e: e.tensor_scalar(out=S4[q4][:, 0:4], in0=S4[q4][:, 0:4], scalar1=1.0 / 64, scalar2=EPS, op0=ALU.mult, op1=ALU.add),
            reads=[("S4a", q4)], writes=[("S4a", q4)])
        add("pool", lambda e: e.tensor_tensor(out=S4[q4][:, 4:8], in0=S4[q4][:, 0:4], in1=MH, op=ALU.pow), reads=[("S4a", q4), "SMmh"], writes=[("S4b", q4)])

    def ip_normB(t):
        q4 = t % 4
        for g in range(4):
            add("dve", lambda e, g=g: e.scalar_tensor_tensor(out=QKN[q4][:, g * 64:(g + 1) * 64], in0=QKF[q4][:, g * 64:(g + 1) * 64], scalar=S4[q4][:, 4 + g:5 + g],
                                                            in1=WQK[:, g * 64:(g + 1) * 64], op0=ALU.mult, op1=ALU.mult),
                reads=[("QKF", q4), ("S4b", q4), "WQK"], writes=[("QKN", q4)])

    def ip_qkT(t):
        q4, pb = t % 4, t % 2
        TQ = TQb(t)
        for g in range(4):
            add("pe", lambda e, g=g: e.transpose(out=TQ[:, g, :], in_=QKN[q4][:, g * 64:(g + 1) * 64], identity=ident), reads=[("QKN", q4), "CB"], writes=[("TQ", pb)], banks=[6 + pb])

    def ip_qkC(t):
        pb = t % 2
        TQ = TQb(t)
        add("dve", lambda e: e.tensor_copy(out=QKT[0:64, :, t * 128:(t + 1) * 128], in_=TQ), reads=[("TQ", pb)], writes=[("QKT", t)], banks=[6 + pb])

    ip_ld(0)
    if NT > 1:
        ip_ld(1)
    ip_xpose(0)
    for t in range(NT + 3):
        if t + 2 < NT:
            ip_ld(t + 2)
        if t + 1 < NT:
            ip_xpose(t + 1)
        if t < NT:
            ip_mm(t)
        if 0 <= t - 3 < NT:
            ip_qkC(t - 3)
        if t < NT:
            ip_evac(t)
        if 0 <= t - 1 < NT:
            ip_normB(t - 1)
        if 0 <= t - 2 < NT:
            ip_qkT(t - 2)
        if t < NT:
            ip_normA(t)
    A.pop()
    P.barrier()

    A.push()
    B = A.alloc([NT, 128], F32)
    NBUF = 3
    KZ = [A.alloc([4, 128], BF16) for _ in range(NBUF)]
    RT = [A.alloc([4, 2, 128], BF16, parts=64) for _ in range(2)]
    QX = [A.alloc([4, 128], BF16) for _ in range(NBUF)]
    QXT = [A.alloc([4, 128], BF16) for _ in range(NBUF)]
    WT = [A.alloc([4, 128], BF16) for _ in range(NBUF)]
    SSg = [A.alloc([4, 128], BF16) for _ in range(NBUF)]
    SQ2 = [A.alloc([512], F32) for _ in range(1)] * 2
    R4 = [A.alloc([8], F32) for _ in range(NBUF)]
    NG = NT // 4

    def rq(g):
        return [("RQ", 4 * g + i) for i in range(4)]

    def p1_kz(g):
        b3 = g % NBUF
        add("dve", lambda e: e.tensor_scalar(out=KZ[b3][:, :, 0:64], in0=RQ[:, 4 * g:4 * g + 4, 64:128], scalar1=ZF, scalar2=None, op0=ALU.mult),
            reads=rq(g) + ["ZF"], writes=[("KZ", b3)])
        add("dve", lambda e: e.tensor_scalar(out=KZ[b3][:, :, 64:128], in0=RQ[:, 4 * g:4 * g + 4, 64:128], scalar1=ZB, scalar2=None, op0=ALU.mult),
            reads=rq(g) + ["ZB"], writes=[("KZ", b3)])

    def p1_mm(g):
        b3, pb = g % NBUF, g % 2
        UU = cx.bank(pb).rearrange("p (a b) -> p a b", a=4)
        for i in range(4):
            c = 4 * g + i
            add("pe", lambda e, c=c, i=i: e.matmul(out=UU[:, i, :], lhsT=KZ[b3][:, i, :], rhs=VV[:, c, 1, 0:128], start=True, stop=True),
                reads=[("KZ", b3), ("VV", c)], writes=[("UU", pb)], banks=[pb])

    def p1_ev(g):
        pb = g % 2
        UU = cx.bank(pb).rearrange("p (a b) -> p a b", a=4)
        add("act", lambda e: e.activation(out=B[:, 4 * g:4 * g + 4, :].rearrange("p a b -> p (a b)"), in_=UU.rearrange("p a b -> p (a b)"), func=AF.Copy),
            reads=[("UU", pb)], writes=[("B", c) for c in range(4 * g, 4 * g + 4)], banks=[pb])

    p1_kz(0)
    for g in range(NG):
        if g + 1 < NG:
            p1_kz(g + 1)
        p1_mm(g)
        p1_ev(g)
    CDf = SM[0:64, SM_CD:SM_CD + 1]
    CDb = SM[64:128, SM_CD:SM_CD + 1]
    for c in range(1, NT):
        add("dve", lambda e, c=c: e.scalar_tensor_tensor(out=B[0:64, c, :], in0=B[0:64, c - 1, :], scalar=CDf, in1=B[0:64, c, :], op0=ALU.mult, op1=ALU.add),
            reads=[("B", c - 1), ("B", c), "CDf"], writes=[("B", c)])
        cb = NT - 1 - c
        add("pool", lambda e, cb=cb: e.scalar_tensor_tensor(out=B[64:128, cb, :], in0=B[64:128, cb + 1, :], scalar=CDb, in1=B[64:128, cb, :], op0=ALU.mult, op1=ALU.add)
            if False else e.tensor_scalar(out=TMPB[64:128, :], in0=B[64:128, cb + 1, :], scalar1=CDb, scalar2=None, op0=ALU.mult),
            reads=[("Bb", cb + 1), ("Bev", cb + 1), "CDb"], writes=["TMPBb"]) if False else None
        add("dve", lambda e, cb=cb: e.scalar_tensor_tensor(out=B[64:128, cb, :], in0=B[64:128, cb + 1, :], scalar=CDb, in1=B[64:128, cb, :], op0=ALU.mult, op1=ALU.add),
            reads=[("Bb", cb + 1), ("Bb", cb), ("B", cb), ("B", cb + 1), "CDb"], writes=[("Bb", cb)])
    allB = [("B", c) for c in range(NT)] + [("Bb", c) for c in range(NT)]

    def s1(g):
        b3, pb = g % NBUF, g % 2
        TR = cx.bankb(2 + pb)[0:64, :].rearrange("p (a b c) -> p a b c", a=4, b=2)
        TX = cx.bankb(6 + pb)[:, 0:512].rearrange("p (a b) -> p a b", a=4)
        add("dve", lambda e: e.tensor_scalar(out=QX[b3][:, :, 0:64], in0=RQ[:, 4 * g:4 * g + 4, 0:64], scalar1=XIF, scalar2=None, op0=ALU.mult),
            reads=rq(g) + ["XIF"], writes=[("QX", b3)])
        add("dve", lambda e: e.tensor_scalar(out=QX[b3][:, :, 64:128], in0=RQ[:, 4 * g:4 * g + 4, 0:64], scalar1=XIB, scalar2=None, op0=ALU.mult),
            reads=rq(g) + ["XIB"], writes=[("QX", b3)])
        for i in range(4):
            c = 4 * g + i
            for j in range(2):
                add("pe", lambda e, c=c, i=i, j=j: e.transpose(out=TR[:, i, j, :], in_=RQ[:, c, j * 64:(j + 1) * 64], identity=ident),
                    reads=[("RQ", c), "CB"], writes=[("TR", pb)], banks=[2 + pb])
        for i in range(4):
            add("pe", lambda e, i=i: e.transpose(out=TX[:, i, :], in_=QX[b3][:, i, :], identity=ident), reads=[("QX", b3), "CB"], writes=[("TX", pb)], banks=[6 + pb])

    def s2(g):
        b3, pb = g % NBUF, g % 2
        TR = cx.bankb(2 + pb)[0:64, :].rearrange("p (a b c) -> p a b c", a=4, b=2)
        TX = cx.bankb(6 + pb)[:, 0:512].rearrange("p (a b) -> p a b", a=4)
        AT = cx.bank(4 + pb).rearrange("p (a b) -> p a b", a=4)
        add("act", lambda e: e.activation(out=RT[pb].rearrange("p a b c -> p (a b c)"), in_=TR.rearrange("p a b c -> p (a b c)"), func=AF.Copy),
            reads=[("TR", pb)], writes=[("RT", pb)], banks=[2 + pb])
        add("act", lambda e: e.activation(out=QXT[b3].rearrange("p a b -> p (a b)"), in_=TX.rearrange("p a b -> p (a b)"), func=AF.Copy),
            reads=[("TX", pb)], writes=[("QXT", b3)], banks=[6 + pb])
        c0 = 4 * g
        if g == 0:
            add("pool", lambda e: e.memset(SSg[b3][0:64, 0, :], 0.0), writes=[("SSg", b3)])
            add("dve", lambda e: e.tensor_copy(out=SSg[b3][0:64, 1:4, :], in_=B[0:64, 0:3, :]), reads=allB, writes=[("SSg", b3)])
        else:
            add("dve", lambda e: e.tensor_copy(out=SSg[b3][0:64, :, :], in_=B[0:64, c0 - 1:c0 + 3, :]), reads=allB, writes=[("SSg", b3)])
        if g == NG - 1:
            add("pool", lambda e: e.memset(SSg[b3][64:128, 3, :], 0.0), writes=[("SSg", b3)])
            add("dve", lambda e: e.tensor_copy(out=SSg[b3][64:128, 0:3, :], in_=B[64:128, c0 + 1:c0 + 4, :]), reads=allB, writes=[("SSg", b3)])
        else:
            add("dve", lambda e: e.tensor_copy(out=SSg[b3][64:128, :, :], in_=B[64:128, c0 + 1:c0 + 5, :]), reads=allB, writes=[("SSg", b3)])
        for i in range(4):
            add("pe", lambda e, i=i: e.matmul(out=AT[:, i, :], lhsT=RT[pb][:, i, 1, :], rhs=RT[pb][:, i, 0, :], start=True, stop=True),
                reads=[("RT", pb)], writes=[("AT", pb)], banks=[4 + pb])
        add("dve", lambda e: e.tensor_tensor(out=WT[b3], in0=AT, in1=DC.unsqueeze(1).to_broadcast([128, 4, 128]), op=ALU.mult),
            reads=[("AT", pb), "DC"], writes=[("WT", b3)], banks=[4 + pb])

    def s3(g):
        b3, pb = g % NBUF, g % 2
        OUTb = cx.bank(pb).rearrange("p (a b) -> p a b", a=4)
        for i in range(4):
            c = 4 * g + i
            add("pe", lambda e, i=i, c=c: e.matmul(out=OUTb[:, i, :], lhsT=WT[b3][:, i, :], rhs=VV[:, c, 1, 0:128], start=True, stop=False),
                reads=[("WT", b3), ("VV", c)], writes=[("OUT", pb)], banks=[pb])
            add("pe", lambda e, i=i, c=c: e.matmul(out=OUTb[:, i, :], lhsT=QXT[b3][:, i, :], rhs=SSg[b3][:, i, :], start=False, stop=True),
                reads=[("QXT", b3), ("SSg", b3)], writes=[("OUT", pb)], banks=[pb])
        add("act", lambda e: e.activation(out=SQ2[pb], in_=OUTb.rearrange("p a b -> p (a b)"), func=AF.Square), reads=[("OUT", pb)], writes=[("SQ2", 0)], banks=[pb])
        add("dve", lambda e: e.reduce_sum(out=R4[b3][:, 0:4], in_=SQ2[pb].rearrange("p (a b) -> p a b", a=4), axis=AX.X), reads=[("SQ2", 0)], writes=[("R4a", b3)])
        add("dve", lambda e: e.tensor_scalar(out=R4[b3][:, 0:4], in0=R4[b3][:, 0:4], scalar1=1.0 / 128, scalar2=EPS, op0=ALU.mult, op1=ALU.add),
            reads=[("R4a", b3)], writes=[("R4a", b3)])
        add("pool", lambda e: e.tensor_tensor(out=R4[b3][:, 4:8], in0=R4[b3][:, 0:4], in1=MH, op=ALU.pow), reads=[("R4a", b3), "SMmh"], writes=[("R4b", b3)])

    def s4(g):
        b3, pb = g % NBUF, g % 2
        OUTb = cx.bank(pb).rearrange("p (a b) -> p a b", a=4)
        for i in range(4):
            c = 4 * g + i
            add("dve", lambda e, i=i, c=c: e.scalar_tensor_tensor(out=G[:, c, 128:256], in0=OUTb[:, i, :], scalar=R4[b3][:, 4 + i:5 + i],
                                                                 in1=G[:, c, 128:256], op0=ALU.mult, op1=ALU.mult),
                reads=[("OUT", pb), ("R4b", b3), ("G", c)], writes=[("G", c)], banks=[pb])

    for s in range(NG + 3):
        if s < NG:
            s1(s)
        if 0 <= s - 1 < NG:
            s2(s - 1)
        if 0 <= s - 3 < NG:
            s4(s - 3)
        if 0 <= s - 2 < NG:
            s3(s - 2)
    A.pop()
    P.barrier()

    A.push()
    PT = [A.alloc([2, 512], BF16) for _ in range(3)]
    OS = A.alloc([3, 512], F32)
    E1 = [A.alloc([128], F32) for _ in range(2)]
    E2 = [A.alloc([128], F32) for _ in range(2)]
    ES = [A.alloc([8], F32) for _ in range(2)]
    Tt = CF[:, CF_T:CF_T + 128]
    TN = CF[:, CF_TN:CF_TN + 128]

    def acc_ap(a, base):
        bnk = 4 + a // 3
        o = (a % 3) * 129
        return base[:, (bnk - 4) * 512 + o:(bnk - 4) * 512 + o + 129]

    OB = cx.bank(4, 3)
    it = 0
    def keep(qc, kb):
        if slope is None:
            return True
        md = max(0, kb * 128 - (qc * 512 + 511), qc * 512 - (kb * 128 + 127))
        return slope * md <= SKIP_ARG

    steps = [(qc, kb) for qc in range(NQC) for kb in range(NT) if keep(qc, kb)]
    first_kb = {qc: min(kb for (q_, kb) in steps if q_ == qc) for qc in range(NQC)}
    last_kb = {qc: max(kb for (q_, kb) in steps if q_ == qc) for qc in range(NQC)}

    def emit_qk(qc, kb, sp, pt):
        Sv = cx.bank(2 * sp, 2).rearrange("p (a b) -> p a b", a=2)
        rel = kb - 4 * qc
        qk_reads = ["QKTaug"] + [("QKT", kb)] + [("QKT", 4 * qc + i) for i in range(4)]
        if rel < 0 or rel >= 4:
            Kw = 66 if rel < 0 else 68
            for m in range(2):
                add("pe", lambda e, m=m, Kw=Kw, Sv=Sv: e.matmul(out=Sv[:, m, :], lhsT=QKT[0:Kw, 2 + m, kb * 128:(kb + 1) * 128],
                                                               rhs=QKT[0:Kw, m, qc * 512:(qc + 1) * 512], start=True, stop=True),
                    reads=qk_reads, writes=[("S", sp)], banks=[2 * sp, 2 * sp + 1])
            bias = (Tt if rel < 0 else TN)[:, rel + 64:rel + 65]
            add("act", lambda e, Sv=Sv, bias=bias, pt=pt: e.activation(out=PT[pt].rearrange("p a b -> p (a b)"), in_=Sv.rearrange("p a b -> p (a b)"),
                                                                      func=AF.Exp, bias=bias, scale=1.0),
                reads=[("S", sp), "CF"], writes=[("PT", pt)], banks=[2 * sp, 2 * sp + 1])
        else:
            for m in range(2):
                for qs in range(4):
                    d = rel - qs
                    q0 = qc * 512 + qs * 128
                    if d == 0:
                        add("pe", lambda e, m=m, qs=qs, q0=q0, Sv=Sv: e.matmul(out=Sv[:, m, qs * 128:(qs + 1) * 128], lhsT=QKT[0:64, 2 + m, kb * 128:(kb + 1) * 128],
                                                                              rhs=QKT[0:64, m, q0:q0 + 128], start=True, stop=False),
                            reads=qk_reads, writes=[("S", sp)], banks=[2 * sp, 2 * sp + 1])
                        add("pe", lambda e, m=m, qs=qs, Sv=Sv: e.matmul(out=Sv[:, m, qs * 128:(qs + 1) * 128], lhsT=ident, rhs=Bm, start=False, stop=True),
                            reads=["CB"], writes=[("S", sp)], banks=[2 * sp, 2 * sp + 1])
                    else:
                        Kw = 68 if d > 0 else 66
                        add("pe", lambda e, m=m, qs=qs, q0=q0, Kw=Kw, Sv=Sv: e.matmul(out=Sv[:, m, qs * 128:(qs + 1) * 128], lhsT=QKT[0:Kw, 2 + m, kb * 128:(kb + 1) * 128],
                                                                                     rhs=QKT[0:Kw, m, q0:q0 + 128], start=True, stop=True),
                            reads=qk_reads, writes=[("S", sp)], banks=[2 * sp, 2 * sp + 1])
            for qs in range(4):
                d = rel - qs
                bias = ZERO if d == 0 else (TN if d > 0 else Tt)[:, rel + 64:rel + 65]
                add("act", lambda e, Sv=Sv, bias=bias, pt=pt, qs=qs: e.activation(out=PT[pt][:, :, qs * 128:(qs + 1) * 128], in_=Sv[:, :, qs * 128:(qs + 1) * 128],
                                                                                 func=AF.Exp, bias=bias, scale=1.0),
                    reads=[("S", sp), "CF", "SMz"], writes=[("PT", pt)], banks=[2 * sp, 2 * sp + 1])

    def emit_pv(qc, kb, pt):
        for a in range(8):
            qs, m = a // 2, a % 2
            add("pe", lambda e, a=a, qs=qs, m=m: e.matmul(out=acc_ap(a, OB), lhsT=PT[pt][:, m, qs * 128:(qs + 1) * 128], rhs=VV[:, kb, 0, 0:129],
                                                         start=(kb == first_kb[qc] and a % 3 == 0), stop=(kb == last_kb[qc]), skip_group_check=True),
                reads=[("PT", pt), ("VV", kb), "VVone"], writes=["OB"], banks=[4, 5, 6])

    def emit_epi(qc):
        for j in range(3):
            ncol = 387 if j < 2 else 258
            add("dve", lambda e, j=j, ncol=ncol: e.tensor_copy(out=OS[:, j, 0:ncol], in_=OB[:, j * 512:j * 512 + ncol]), reads=["OB"], writes=["OS"], banks=[4, 5, 6])
        OSf = OS.rearrange("p a b -> p (a b)")
        for qs in range(4):
            t = 4 * qc + qs
            pb = qs % 2
            o1 = acc_ap(2 * qs, OSf)
            o2 = acc_ap(2 * qs + 1, OSf)
            es = ES[pb]
            add("dve", lambda e, o1=o1, es=es: e.reciprocal(out=es[:, 0:1], in_=o1[:, 128:129]), reads=["OS"], writes=[("ESa", pb)])
            add("dve", lambda e, o2=o2, es=es: e.reciprocal(out=es[:, 1:2], in_=o2[:, 128:129]), reads=["OS"], writes=[("ESb", pb)])
            add("dve", lambda e, es=es: e.tensor_scalar(out=es[:, 2:3], in0=es[:, 1:2], scalar1=NEGLAM, scalar2=None, op0=ALU.mult),
                reads=[("ESb", pb), "NEGLAM"], writes=[("ESc", pb)])
            add("dve", lambda e, o1=o1, es=es, pb=pb: e.tensor_scalar(out=E1[pb], in0=o1[:, 0:128], scalar1=es[:, 0:1], scalar2=None, op0=ALU.mult),
                reads=["OS", ("ESa", pb)], writes=[("E1", pb)])
            add("dve", lambda e, o2=o2, es=es, pb=pb: e.scalar_tensor_tensor(out=E1[pb], in0=o2[:, 0:128], scalar=es[:, 2:3], in1=E1[pb], op0=ALU.mult, op1=ALU.add),
                reads=["OS", ("ESc", pb), ("E1", pb)], writes=[("E1", pb)])
            add("dve", lambda e, es=es, pb=pb: e.scalar_tensor_tensor(out=E2[pb], in0=E1[pb], scalar=1.0, in1=E1[pb], op0=ALU.mult, op1=ALU.mult, accum_out=es[:, 3:4]),
                reads=[("E1", pb)], writes=[("E2", pb), ("ESd", pb)])
            add("dve", lambda e, es=es: e.tensor_scalar(out=es[:, 3:4], in0=es[:, 3:4], scalar1=1.0 / 128, scalar2=EPS, op0=ALU.mult, op1=ALU.add),
                reads=[("ESd", pb)], writes=[("ESd", pb)])
            add("pool", lambda e, es=es: e.tensor_tensor(out=es[:, 4:5], in0=es[:, 3:4], in1=MH[:, 0:1], op=ALU.pow),
                reads=[("ESd", pb), "SMmh"], writes=[("ESe", pb)])
            add("dve", lambda e, es=es, pb=pb: e.scalar_tensor_tensor(out=E2[pb], in0=E1[pb], scalar=es[:, 4:5], in1=SUBW, op0=ALU.mult, op1=ALU.mult),
                reads=[("E1", pb), ("ESe", pb), "SUBW", ("E2", pb)], writes=[("E2", pb)])
            add("dve", lambda e, t=t, pb=pb: e.tensor_tensor(out=G[:, t, 0:128], in0=E2[pb], in1=G[:, t, 0:128], op=ALU.mult),
                reads=[("E2", pb), ("G", t)], writes=[("G", t)])
        mv = io["mix"].rearrange("(t p) c -> p t c", p=128)
        add("sync", lambda e, qc=qc: e.dma_start(out=mv[:, 4 * qc:4 * qc + 4, :], in_=G[:, 4 * qc:4 * qc + 4, :]),
            reads=[("G", 4 * qc + i) for i in range(4)], slot=("mixout", qc % 2))

    n = len(steps)
    for j in range(min(2, n)):
        emit_qk(steps[j][0], steps[j][1], j % 2, j % 3)
    for i, (qc, kb) in enumerate(steps):
        if i + 2 < n:
            emit_qk(steps[i + 2][0], steps[i + 2][1], (i + 2) % 2, (i + 2) % 3)
        emit_pv(qc, kb, i % 3)
        if kb == last_kb[qc]:
            emit_epi(qc)
    A.pop()


def declare_B_io(nc, NT):
    S = NT * 128
    io = {}
    io["xn"] = nc.dram_tensor("xn", [S, 1024], BF16, kind="ExternalInput").ap()
    io["w"] = nc.dram_tensor("w", [1024, 896], F32, kind="ExternalInput").ap()
    io["normw"] = nc.dram_tensor("normw", [128, 8], F32, kind="ExternalInput").ap()
    io["vec64"] = nc.dram_tensor("vec64", [1, 6 * 64], F32, kind="ExternalInput").ap()
    io["subln"] = nc.dram_tensor("subln", [1, 128], F32, kind="ExternalInput").ap()
    io["scal"] = nc.dram_tensor("scal", [1, 4], F32, kind="ExternalInput").ap()
    io["cpackf"] = nc.dram_tensor("cpackf", [128, CF_N], F32, kind="ExternalInput").ap()
    io["cpackb"] = nc.dram_tensor("cpackb", [128, 256], BF16, kind="ExternalInput").ap()
    io["aug"] = nc.dram_tensor("aug", [4, 4, S], BF16, kind="ExternalInput").ap()
    io["mix"] = nc.dram_tensor("mix", [S, 256], BF16, kind="ExternalOutput").ap()
    return io


def build_B(NT):
    nc = bass.Bass("TRN2", target_bir_lowering=False)
    io = declare_B_io(nc, NT)
    with contextlib.ExitStack() as st:
        cx = Ctx(nc, st)
        emit_B(cx, io, NT)
        cx.P.emit(nc)
    return nc


def alibi_slope(h):
    return 2.0 ** (-8.0 * (h + 1) / 4)


def const_tables(hh, S):
    s = alibi_slope(hh)
    p = np.arange(128, dtype=np.float64)
    cf = np.zeros((128, CF_N), np.float32)
    tt = p[None, :] - p[:, None]
    cf[:, CF_M1:CF_M1 + 128] = np.maximum(tt, 0)
    cf[:, CF_M2:CF_M2 + 128] = np.maximum(-tt, 0)
    u = np.arange(128, dtype=np.float64)
    T = s * (p[:, None] + 128.0 * (u[None, :] - 64))
    cf[:, CF_T:CF_T + 128] = T
    cf[:, CF_TN:CF_TN + 128] = -T
    cf[:, CF_C5 + 0] = p + 1
    cf[:, CF_C5 + 1] = 128 - p
    cf[:, CF_C5 + 2] = 127 - p
    cf[:, CF_C5 + 3] = p
    cf[:, CF_C5 + 4] = 128
    cb = np.zeros((128, 256), np.float32)
    cb[:, 0:128] = np.eye(128)
    cb[:, 128:256] = -s * np.abs(tt)
    r = np.arange(S) % 512
    rlo = (r & 255).astype(np.float64)
    rhi = (r - (r & 255)).astype(np.float64)
    qa = np.stack([-s * rlo, -s * rhi, -2 * s * rlo, -2 * s * rhi])
    ka = np.stack([np.ones(S), np.ones(S), -np.ones(S), -np.ones(S)])
    aug = np.stack([qa, qa, ka, ka], axis=1)
    return cf, cb.astype(ml_dtypes.bfloat16), aug.astype(ml_dtypes.bfloat16)


def w_in_cols(hh):
    qa = [hh * 128 + m * 64 + j for m in range(2) for j in range(64)]
    ka = [512 + c for c in qa]
    va = [1024 + hh * 128 + j for j in range(128)]
    ga = [1536 + hh * 128 + j for j in range(128)]
    qr = [2048 + hh * 64 + j for j in range(64)]
    kr = [2304 + hh * 64 + j for j in range(64)]
    vr = [2560 + hh * 128 + j for j in range(128)]
    gr = [3072 + hh * 128 + j for j in range(128)]
    return np.array(qa + ka + qr + kr + va + vr + ga + gr)


def B_inputs(inputs, l, hh, xn_b, S):
    cf, cb, aug = const_tables(hh, S)
    lam_init = 0.8 - 0.6 * math.exp(-0.3 * l)
    f32 = np.float32
    vec = np.stack([inputs["q_norm_w"][l], inputs["k_norm_w"][l], inputs["lambda_q1"][l], inputs["lambda_k1"][l],
                    inputs["lambda_q2"][l], inputs["lambda_k2"][l]]).astype(f32).reshape(1, 384)
    scal = np.array([[inputs["ret_decay_fwd"][l, hh], inputs["ret_decay_bwd"][l, hh], lam_init, 1.0 - lam_init]], f32)
    return dict(
        xn=xn_b,
        w=np.ascontiguousarray(inputs["w_in"][l][:, w_in_cols(hh)]).astype(f32),
        normw=np.ascontiguousarray(inputs["norm_w"][l].reshape(8, 128).T).astype(f32),
        vec64=vec,
        subln=inputs["subln_w"][l].astype(f32).reshape(1, 128),
        scal=scal,
        cpackf=cf, cpackb=cb, aug=aug,
    )


def emit_norm_tile(cx, src_f32, dst_bf16, junk, st, key, MH):
    add = cx.P.add
    add("act", lambda e: e.activation(out=junk, in_=src_f32, func=AF.Square, accum_out=st[:, 0:1]), reads=[key], writes=[("nj", id(junk)), ("st0", id(st))])
    add("dve", lambda e: e.tensor_scalar(out=st[:, 1:2], in0=st[:, 0:1], scalar1=1.0 / D_MODEL, scalar2=EPS, op0=ALU.mult, op1=ALU.add),
        reads=[("st0", id(st))], writes=[("st1", id(st))])
    add("pool", lambda e: e.tensor_tensor(out=st[:, 2:3], in0=st[:, 1:2], in1=MH, op=ALU.pow), reads=[("st1", id(st)), "MHc"], writes=[("st2", id(st))])
    add("dve", lambda e: e.tensor_scalar(out=dst_bf16, in0=src_f32, scalar1=st[:, 2:3], scalar2=None, op0=ALU.mult),
        reads=[key, ("st2", id(st))], writes=[("xb", id(dst_bf16))])


def emit_A(cx, io, NTQ):
    P, A = cx.P, cx.A
    add = P.add
    A.push()
    XF = [A.alloc([1024], F32) for _ in range(2)]
    XB = [A.alloc([1024], BF16) for _ in range(2)]
    JK = [A.alloc([1024], BF16) for _ in range(2)]
    ST = [A.alloc([4], F32) for _ in range(2)]
    MHc = A.alloc([1], F32)
    add("pool", lambda e: e.memset(MHc, -0.5), writes=["MHc"])
    xv = io["x"].rearrange("(t p) d -> t p d", p=128)
    ov = io["xn_out"].rearrange("(t p) d -> t p d", p=128)
    for t in range(NTQ):
        pb = t % 2
        add("sync", lambda e, t=t, pb=pb: e.dma_start(out=XF[pb], in_=xv[t]), writes=[("XF", pb)], slot=("xf", pb))
        emit_norm_tile(cx, XF[pb], XB[pb], JK[pb], ST[pb], ("XF", pb), MHc)
        add("sync", lambda e, t=t, pb=pb: e.dma_start(out=ov[t], in_=XB[pb]), reads=[("xb", id(XB[pb]))], slot=("xno", pb))
    A.pop()
    P.barrier()


def build_A(NTQ):
    nc = bass.Bass("TRN2", target_bir_lowering=False)
    io = {}
    io["x"] = nc.dram_tensor("x", [NTQ * 128, 1024], F32, kind="ExternalInput").ap()
    io["xn_out"] = nc.dram_tensor("xn_out", [NTQ * 128, 1024], BF16, kind="ExternalOutput").ap()
    with contextlib.ExitStack() as st:
        cx = Ctx(nc, st)
        emit_A(cx, io, NTQ)
        cx.P.emit(nc)
    return nc


def emit_C(cx, io, NTQ, with_norm):
    P, A = cx.P, cx.A
    add = P.add
    A.push()
    WO = A.alloc([8, 1024], BF16)
    WST = [A.alloc([1024], F32) for _ in range(2)]
    IDN = A.alloc([128], BF16)
    MX = [A.alloc([1024], BF16) for _ in range(2)]
    MT = [A.alloc([8, 128], BF16) for _ in range(2)]
    XR = [A.alloc([1024], F32) for _ in range(2)]
    XO = [A.alloc([1024], F32) for _ in range(2)]
    XB = [A.alloc([1024], BF16) for _ in range(2)]
    JK = [A.alloc([1024], BF16) for _ in range(2)]
    ST = [A.alloc([4], F32) for _ in range(2)]
    MHc = A.alloc([1], F32)
    add("pool", lambda e: e.memset(MHc, -0.5), writes=["MHc"])
    add("sync", lambda e: e.dma_start(out=IDN, in_=io["ident"]), writes=["IDN"], slot="ldi")
    wv = io["wout"].rearrange("(k p) n -> k p n", p=128)
    for kc in range(8):
        b_ = kc % 2
        add("sync", lambda e, kc=kc, b_=b_: e.dma_start(out=WST[b_], in_=wv[kc]), writes=[("WSTo", b_)], slot=("wsto", b_))
        add("dve" if kc % 2 else "pool", lambda e, kc=kc, b_=b_: e.tensor_copy(out=WO[:, kc, :], in_=WST[b_]), reads=[("WSTo", b_)], writes=["WO"])
    if "mix_hm" in io:
        mvh = io["mix_hm"].rearrange("h (t p) c -> t p h c", p=128)
        mv = [mvh[t] for t in range(NTQ)]
        MXv = [m.rearrange("p (h c) -> p h c", h=4) for m in MX]
    else:
        mvf = io["mixq"].rearrange("(t p) d -> t p d", p=128)
        mv = [mvf[t] for t in range(NTQ)]
        MXv = MX
    xv = io["xres"].rearrange("(t p) d -> t p d", p=128)
    ov = io["xnew"].rearrange("(t p) d -> t p d", p=128)
    if with_norm:
        nv = io["xn_out"].rearrange("(t p) d -> t p d", p=128)
    for t in range(NTQ):
        pb = t % 2
        TPb = cx.bankb(pb)
        ACC = cx.bank(2 + 2 * pb, 2)
        add("sync", lambda e, t=t, pb=pb: e.dma_start(out=MXv[pb], in_=mv[t]), writes=[("MX", pb)], slot=("mx", pb))
        add("sync", lambda e, t=t, pb=pb: e.dma_start(out=XR[pb], in_=xv[t]), writes=[("XR", pb)], slot=("xr", pb))
        for kc in range(8):
            add("pe", lambda e, pb=pb, kc=kc, TPb=TPb: e.transpose(out=TPb[:, kc * 128:(kc + 1) * 128], in_=MX[pb][:, kc * 128:(kc + 1) * 128], identity=IDN),
                reads=[("MX", pb), "IDN"], writes=[("TPc", pb)], banks=[pb])
        add("act", lambda e, pb=pb, TPb=TPb: e.activation(out=MT[pb].rearrange("p a b -> p (a b)"), in_=TPb, func=AF.Copy),
            reads=[("TPc", pb)], writes=[("MT", pb)], banks=[pb])
        for hf in range(2):
            for kc in range(8):
                add("pe", lambda e, pb=pb, kc=kc, hf=hf, ACC=ACC: e.matmul(out=ACC[:, hf * 512:(hf + 1) * 512], lhsT=MT[pb][:, kc, :], rhs=WO[:, kc, hf * 512:(hf + 1) * 512],
                                                                          start=(kc == 0), stop=(kc == 7)),
                    reads=[("MT", pb), "WO"], writes=[("ACC", pb)], banks=[2 + 2 * pb, 3 + 2 * pb])
        add("dve", lambda e, pb=pb, ACC=ACC: e.tensor_tensor(out=XO[pb], in0=ACC, in1=XR[pb], op=ALU.add),
            reads=[("ACC", pb), ("XR", pb)], writes=[("XO", pb)], banks=[2 + 2 * pb, 3 + 2 * pb])
        add("sync", lambda e, t=t, pb=pb: e.dma_start(out=ov[t], in_=XO[pb]), reads=[("XO", pb)], slot=("xo", pb))
        if with_norm:
            emit_norm_tile(cx, XO[pb], XB[pb], JK[pb], ST[pb], ("XO", pb), MHc)
            add("sync", lambda e, t=t, pb=pb: e.dma_start(out=nv[t], in_=XB[pb]), reads=[("xb", id(XB[pb]))], slot=("xnc", pb))
    A.pop()
    P.barrier()


def build_C(NTQ, with_norm):
    nc = bass.Bass("TRN2", target_bir_lowering=False)
    io = {}
    io["mixq"] = nc.dram_tensor("mixq", [NTQ * 128, 1024], BF16, kind="ExternalInput").ap()
    io["xres"] = nc.dram_tensor("xres", [NTQ * 128, 1024], F32, kind="ExternalInput").ap()
    io["wout"] = nc.dram_tensor("wout", [1024, 1024], F32, kind="ExternalInput").ap()
    io["ident"] = nc.dram_tensor("ident", [128, 128], BF16, kind="ExternalInput").ap()
    io["xnew"] = nc.dram_tensor("xnew", [NTQ * 128, 1024], F32, kind="ExternalOutput").ap()
    if with_norm:
        io["xn_out"] = nc.dram_tensor("xn_out", [NTQ * 128, 1024], BF16, kind="ExternalOutput").ap()
    with contextlib.ExitStack() as st:
        cx = Ctx(nc, st)
        emit_C(cx, io, NTQ, with_norm)
        cx.P.emit(nc)
    return nc


def wout_rows():
    rows = []
    for h in range(4):
        rows += list(range(h * 128, (h + 1) * 128)) + list(range(512 + h * 128, 512 + (h + 1) * 128))
    return np.array(rows)


_CACHE = {}


def _get(name, fn):
    if name not in _CACHE:
        _CACHE[name] = fn()
    return _CACHE[name]


def kernel_unfused(**inputs):
    inputs = {k: np.asarray(v) for k, v in inputs.items()}
    x = inputs["x"]
    Bsz, S, Dm = x.shape
    NT = S // 128
    NTQ = NT // 4
    TQ = S // 4
    cores = list(range(8))
    ident = np.eye(128, dtype=np.float32).astype(ml_dtypes.bfloat16)
    ncA = _get("A", lambda: build_A(NTQ))
    res = run_bass_kernel_spmd(ncA, [dict(x=np.ascontiguousarray(x[c // 4, (c % 4) * TQ:(c % 4 + 1) * TQ])) for c in cores], core_ids=cores)
    xn_q = [r["xn_out"] for r in res.results]
    xcur = [np.ascontiguousarray(x[c // 4, (c % 4) * TQ:(c % 4 + 1) * TQ]) for c in cores]
    for l in range(2):
        xn_b = [np.concatenate([xn_q[b * 4 + j] for j in range(4)], axis=0) for b in range(Bsz)]
        ncB = _get("B", lambda: build_B(NT))
        res = run_bass_kernel_spmd(ncB, [B_inputs(inputs, l, c % 4, xn_b[c // 4], S) for c in cores], core_ids=cores)
        mix = [r["mix"] for r in res.results]
        wout = np.ascontiguousarray(inputs["w_out"][l][wout_rows()]).astype(np.float32)
        last = (l == 1)
        ncC = _get("C%d" % l, lambda: build_C(NTQ, not last))
        ins = []
        for c in cores:
            b, j = c // 4, c % 4
            mixq = np.concatenate([mix[b * 4 + h][j * TQ:(j + 1) * TQ] for h in range(4)], axis=1)
            ins.append(dict(mixq=np.ascontiguousarray(mixq), xres=xcur[c], wout=wout, ident=ident))
        res = run_bass_kernel_spmd(ncC, ins, core_ids=cores)
        xcur = [r["xnew"] for r in res.results]
        if not last:
            xn_q = [r["xn_out"] for r in res.results]
    out = np.stack([np.concatenate([xcur[b * 4 + j] for j in range(4)], axis=0) for b in range(Bsz)], axis=0)
    return out.astype(np.float32)


def build_fused(NT):
    S = NT * 128
    nc = bass.Bass("TRN2", target_bir_lowering=False)
    dt = nc.dram_tensor
    x = dt("x", [S, 1024], F32, kind="ExternalInput").ap()
    w_all = dt("w_all", [2, 4, 1024, 896], F32, kind="ExternalInput").ap()
    normw = dt("normw", [2, 128, 8], F32, kind="ExternalInput").ap()
    vec64 = dt("vec64", [2, 1, 384], F32, kind="ExternalInput").ap()
    subln = dt("subln", [2, 1, 128], F32, kind="ExternalInput").ap()
    scal = dt("scal", [2, 4, 1, 4], F32, kind="ExternalInput").ap()
    cpackf = dt("cpackf", [4, 128, CF_N], F32, kind="ExternalInput").ap()
    cpackb = dt("cpackb", [4, 128, 256], BF16, kind="ExternalInput").ap()
    aug = dt("aug", [4, 4, 4, S], BF16, kind="ExternalInput").ap()
    wout = dt("wout", [2, 1024, 1024], F32, kind="ExternalInput").ap()
    identd = dt("ident", [128, 128], BF16, kind="ExternalInput").ap()
    out = dt("out", [S, 1024], F32, kind="ExternalOutput").ap()
    XNs = dt("xn_scratch", [S, 1024], BF16, kind="Internal").ap()
    MIXs = dt("mix_scratch", [4, S, 256], BF16, kind="Internal").ap()
    X1 = dt("x1_scratch", [S, 1024], F32, kind="Internal").ap()
    with contextlib.ExitStack() as st:
        cx = Ctx(nc, st)
        emit_A(cx, dict(x=x, xn_out=XNs), NT)
        for l in range(2):
            for h in range(4):
                emit_B(cx, dict(xn=XNs, w=w_all[l, h], normw=normw[l], vec64=vec64[l], subln=subln[l], scal=scal[l, h],
                                cpackf=cpackf[h], cpackb=cpackb[h], aug=aug[h], mix=MIXs[h]), NT, slope=alibi_slope(h))
            ioC = dict(mix_hm=MIXs, xres=(x if l == 0 else X1), wout=wout[l], ident=identd, xnew=(X1 if l == 0 else out))
            if l == 0:
                ioC["xn_out"] = XNs
            emit_C(cx, ioC, NT, with_norm=(l == 0))
        cx.P.emit(nc)
    print("[build_fused] ops", len(cx.P.ops), "sem counts", {str(k): v for k, v in cx.P.sem_counts.items() if k[0] == "eng"}, flush=True)
    return nc


def fused_inputs(inputs, b, S):
    f32 = np.float32
    tabs = [const_tables(h, S) for h in range(4)]
    lam_init = [0.8 - 0.6 * math.exp(-0.3 * l) for l in range(2)]
    vec = np.stack([np.stack([inputs["q_norm_w"][l], inputs["k_norm_w"][l], inputs["lambda_q1"][l], inputs["lambda_k1"][l],
                              inputs["lambda_q2"][l], inputs["lambda_k2"][l]]).reshape(1, 384) for l in range(2)]).astype(f32)
    scal = np.array([[[[inputs["ret_decay_fwd"][l, h], inputs["ret_decay_bwd"][l, h], lam_init[l], 1.0 - lam_init[l]]] for h in range(4)] for l in range(2)], f32)
    return dict(
        x=np.ascontiguousarray(inputs["x"][b]).astype(f32),
        w_all=np.stack([np.stack([inputs["w_in"][l][:, w_in_cols(h)] for h in range(4)]) for l in range(2)]).astype(f32),
        normw=np.stack([inputs["norm_w"][l].reshape(8, 128).T for l in range(2)]).astype(f32),
        vec64=vec,
        subln=np.stack([inputs["subln_w"][l].reshape(1, 128) for l in range(2)]).astype(f32),
        scal=scal,
        cpackf=np.stack([t[0] for t in tabs]), cpackb=np.stack([t[1] for t in tabs]), aug=np.stack([t[2] for t in tabs]),
        wout=np.stack([inputs["w_out"][l][wout_rows()] for l in range(2)]).astype(f32),
        ident=np.eye(128, dtype=f32).astype(ml_dtypes.bfloat16),
    )


def kernel_fused(**inputs):
    inputs = {k: np.asarray(v) for k, v in inputs.items()}
    Bsz, S, _ = inputs["x"].shape
    nc = _get("F", lambda: build_fused(S // 128))
    res = run_bass_kernel_spmd(nc, [fused_inputs(inputs, b, S) for b in range(Bsz)], core_ids=list(range(Bsz)))
    return np.stack([res.results[b]["out"] for b in range(Bsz)], axis=0).astype(np.float32)


def kernel(**inputs):
    return kernel_unfused(**inputs)
```
